# Optimizing a Trainium2 kernel written in Bass

```python
import math
import jax
import jax.numpy as jnp
from jax import lax
import numpy as np

D_MODEL = 2048
BATCH = 16
SEQ = 256
DEPTH = 2
DEC_BATCH = 8
DEC_SEQ = 1024
PAST_LEN = 512

GRID_W = 64
HEAD_DIM = 64
SHORT_W = 3
ATT_WIDTH = 3 * D_MODEL // 8
N_ATT_HEADS = ATT_WIDTH // HEAD_DIM
N_KV_HEADS = N_ATT_HEADS // 3
ATT_GROUP = N_ATT_HEADS // N_KV_HEADS
ATT_KV = N_KV_HEADS * HEAD_DIM
WINDOW = 128
BLOCK = 128
ROPE_THETA = 10000.0
NEG_INF = -1e30
HY_CH = D_MODEL // 4
HY_ORDER = 2
HY_N_FILT = HY_ORDER - 1
HY_BANDS = 16
HY_EMB = 1 + 2 * HY_BANDS
HY_FFN = 64
HY_MOD_SHIFT = 0.05
HY_DECAY_MIN = -math.log(1e-2) / 1.5
HY_DECAY_MAX = -math.log(1e-2) / 0.3
RWKV_WIDTH = D_MODEL - ATT_WIDTH - HY_CH
N_RWKV_HEADS = RWKV_WIDTH // HEAD_DIM
DECAY_LORA = 96
AAA_LORA = 96
GATE_LORA = 256
RWKV_GN_EPS = 64e-5
MIX_WIDTH = ATT_WIDTH + HY_CH + RWKV_WIDTH
D_FF = 4 * D_MODEL
N_MOD = 6
RMS_EPS = 1e-6
IN_WIDTHS = (ATT_WIDTH, ATT_KV, ATT_KV, (HY_ORDER + 1) * HY_CH, 3 * RWKV_WIDTH,
             2 * DECAY_LORA, 2 * AAA_LORA, GATE_LORA)
IN_COLS = sum(IN_WIDTHS)
IN_SPLITS = tuple(sum(IN_WIDTHS[:i + 1]) for i in range(len(IN_WIDTHS) - 1))

kernel_name = 'hybrid_dit_attn_hyena_rwkv7_step'

f32 = jnp.float32


def _rmsnorm(x, g):
    xf = x.astype(f32)
    y = xf * lax.rsqrt(jnp.mean(xf * xf, axis=-1, keepdims=True) + RMS_EPS)
    return (y * g.astype(f32)).astype(x.dtype)


def _modulation(cond, ada_w, ada_b):
    m = jax.nn.silu(cond) @ ada_w + ada_b
    m = m.reshape(cond.shape[0], 1, N_MOD, D_MODEL)
    return [m[:, :, i] for i in range(N_MOD)]


def _short_conv(x, w):
    xp = jnp.pad(x, ((0, 0), (1, 1), (0, 0)))
    return xp[:, :-2] * w[0] + xp[:, 1:-1] * w[1] + xp[:, 2:] * w[2]


def _rope_2d(x):
    L = x.shape[1]
    n_rows = L // GRID_W
    rows, cols = jnp.meshgrid(jnp.arange(n_rows), jnp.arange(GRID_W), indexing='ij')
    rows = rows.reshape(-1).astype(f32)
    cols = cols.reshape(-1).astype(f32)
    half = x.shape[-1] // 2
    freqs = ROPE_THETA ** (-jnp.arange(0, half, 2, dtype=f32) / half)

    def rot(xh, pos):
        ang = pos[:, None] * freqs[None, :]
        cos = jnp.cos(ang)[None, :, None, :]
        sin = jnp.sin(ang)[None, :, None, :]
        x1, x2 = jnp.split(xh, 2, axis=-1)
        return jnp.concatenate([x1 * cos - x2 * sin, x2 * cos + x1 * sin], axis=-1)

    xf = x.astype(f32)
    return jnp.concatenate([rot(xf[..., :half], rows), rot(xf[..., half:], cols)], axis=-1).astype(x.dtype)


def _attend(q, k, v, mask, sink):
    s = jnp.einsum('bqngd,bknd->bngqk', q.astype(f32), k.astype(f32)) * (HEAD_DIM ** -0.5)
    if mask is not None:
        s = jnp.where(mask, s, NEG_INF)
    sk = jnp.broadcast_to(sink.astype(f32)[None, :, :, None, None], s.shape[:-1] + (1,))
    p = jax.nn.softmax(jnp.concatenate([s, sk], axis=-1), axis=-1)[..., :-1]
    return jnp.einsum('bngqk,bknd->bqngd', p, v.astype(f32))


def _context_attention(q, k, v, sink):
    B, L = q.shape[:2]
    nb = L // BLOCK
    qb = q.reshape(B, nb, BLOCK, N_KV_HEADS, ATT_GROUP, HEAD_DIM).swapaxes(0, 1)
    sink_g = sink.reshape(N_KV_HEADS, ATT_GROUP)
    out = lax.map(lambda qi: _attend(qi, k, v, None, sink_g), qb)
    return out.swapaxes(0, 1).reshape(B, L, ATT_WIDTH)


def _latent_attention(q, k, v, k_ctx, v_ctx, sink):
    B, L = q.shape[:2]
    nb = L // BLOCK
    Lc = k_ctx.shape[1]
    q5 = q.reshape(B, L, N_KV_HEADS, ATT_GROUP, HEAD_DIM)
    pad = ((0, 0), (BLOCK, BLOCK), (0, 0), (0, 0))
    k_pad = jnp.pad(k, pad)
    v_pad = jnp.pad(v, pad)
    sink_g = sink.reshape(N_KV_HEADS, ATT_GROUP)
    ctx_mask = jnp.ones((BLOCK, Lc), dtype=bool)

    def block(i):
        start = i * BLOCK
        qi = lax.dynamic_slice_in_dim(q5, start, BLOCK, axis=1)
        ki = lax.dynamic_slice_in_dim(k_pad, start, 3 * BLOCK, axis=1)
        vi = lax.dynamic_slice_in_dim(v_pad, start, 3 * BLOCK, axis=1)
        qpos = start + jnp.arange(BLOCK)
        kpos = start - BLOCK + jnp.arange(3 * BLOCK)
        local = ((jnp.abs(qpos[:, None] - kpos[None, :]) <= WINDOW)
                 & (kpos >= 0)[None, :] & (kpos < L)[None, :])
        mask = jnp.concatenate([local, ctx_mask], axis=1)
        kk = jnp.concatenate([ki, k_ctx], axis=1)
        vv = jnp.concatenate([vi, v_ctx], axis=1)
        return _attend(qi, kk, vv, mask, sink_g)

    out = lax.map(block, jnp.arange(nb))
    return out.swapaxes(0, 1).reshape(B, L, ATT_WIDTH)


def _hyena_filter_spectrum(L, f1, b1, f2, b2, f3, decay):
    t = jnp.arange(L, dtype=f32)
    t01 = (t / max(L - 1, 1))[:, None]
    bands = jnp.linspace(1e-4, HY_BANDS - 1, HY_BANDS, dtype=f32)
    ang = (2.0 * math.pi / L) * t[:, None] * bands[None, :]
    feat = jnp.concatenate([t01, jnp.cos(ang), -jnp.sin(ang)], axis=-1)
    h = jnp.sin(feat @ f1.astype(f32) + b1.astype(f32))
    h = jnp.sin(h @ f2.astype(f32) + b2.astype(f32))
    h = h @ f3.astype(f32)
    h = h * (jnp.exp(-t01 * jnp.abs(decay.astype(f32))) + HY_MOD_SHIFT)
    h = h.reshape(L, 2, HY_N_FILT, HY_CH)
    fwd = h[:, 0]
    bwd = h[:, 1]
    two_sided = jnp.concatenate([fwd, jnp.zeros((1, HY_N_FILT, HY_CH), f32), bwd[:0:-1]], axis=0)
    return jnp.fft.rfft(two_sided, axis=0)


def _hyena(u, p):
    L = u.shape[1]
    u = _short_conv(u, p['hy_short_w']).astype(f32)
    parts = jnp.split(u, HY_ORDER + 1, axis=-1)
    gates, z = parts[:-1], parts[-1]
    spec = _hyena_filter_spectrum(L, p['hy_f1'], p['hy_b1'], p['hy_f2'], p['hy_b2'], p['hy_f3'], p['hy_decay'])
    for o in range(HY_N_FILT):
        z = z * gates[HY_ORDER - 1 - o]
        zf = jnp.fft.rfft(z, n=2 * L, axis=1)
        y = jnp.fft.irfft(zf * spec[None, :, o], n=2 * L, axis=1)[:, :L]
        z = y + p['hy_skip'][o].astype(f32) * z
    return z * gates[0]


def _rwkv(u_rkv, w_dn, a_dn, g_dn, p, S0):
    B, L, _ = u_rkv.shape
    rkv = _short_conv(u_rkv, p['rw_short_w']).astype(f32)
    r, k, v = jnp.split(rkv, 3, axis=-1)
    w_dn = w_dn.astype(f32).reshape(B, L, 2, DECAY_LORA)
    a_dn = a_dn.astype(f32).reshape(B, L, 2, AAA_LORA)
    w_lo = jnp.einsum('bldr,drc->dblc', jnp.tanh(w_dn), p['rw_w_up'].astype(f32))
    w_log = -jax.nn.softplus(-(p['rw_w0'].astype(f32)[:, None, None, :] + w_lo)) - 0.5
    decay = jnp.exp(-jnp.exp(w_log))
    a = jax.nn.sigmoid(p['rw_a0'].astype(f32)[:, None, None, :]
                       + jnp.einsum('bldr,drc->dblc', a_dn, p['rw_a_up'].astype(f32)))
    g = jax.nn.sigmoid(g_dn.astype(f32)) @ p['rw_g_up'].astype(f32)

    def hsplit(t):
        return t.reshape(t.shape[:-1] + (N_RWKV_HEADS, HEAD_DIM))

    kk = hsplit(k * p['rw_k_k'].astype(f32))
    kk = kk / jnp.maximum(jnp.sqrt(jnp.sum(kk * kk, axis=-1, keepdims=True)), 1e-12)
    k_dir = hsplit(k[None] * (1.0 + (a - 1.0) * p['rw_k_a'].astype(f32)))
    r_h = hsplit(r)
    v_h = hsplit(v)

    def both(t):
        return jnp.broadcast_to(t[None], (2,) + t.shape)

    def orient(t):
        return jnp.moveaxis(jnp.stack([t[0], jnp.flip(t[1], axis=1)], axis=0), 2, 0)

    xs = (orient(both(r_h)), orient(hsplit(decay)), orient(k_dir),
          orient(both(v_h)), orient(both(kk)), orient(hsplit(a)))

    def step(S, inp):
        r_t, w_t, k_t, v_t, kk_t, a_t = inp
        sk = jnp.einsum('dbhvk,dbhk->dbhv', S, kk_t)
        S = (S * w_t[..., None, :] - sk[..., None] * (kk_t * a_t)[..., None, :]
             + v_t[..., None] * k_t[..., None, :])
        return S, jnp.einsum('dbhvk,dbhk->dbhv', S, r_t)

    S_fin, ys = lax.scan(step, S0.astype(f32), xs)
    ys = jnp.moveaxis(ys, 0, 2)
    y = ys[0] + jnp.flip(ys[1], axis=1)
    mu = jnp.mean(y, axis=-1, keepdims=True)
    var = jnp.mean(jnp.square(y - mu), axis=-1, keepdims=True)
    yn = ((y - mu) * lax.rsqrt(var + RWKV_GN_EPS)).reshape(B, L, RWKV_WIDTH)
    yn = yn * p['rw_gn_w'].astype(f32) + p['rw_gn_b'].astype(f32)
    bonus = jnp.sum(r_h * hsplit(k) * p['rw_r_k'].astype(f32), axis=-1, keepdims=True) * v_h
    out = (yn + bonus.reshape(B, L, RWKV_WIDTH)) * g
    return out, S_fin


def _mixer(h, p, ctx):
    B, L, _ = h.shape
    u = h @ p['w_in']
    q, k, v, u_hy, u_rkv, w_dn, a_dn, g_dn = jnp.split(u, IN_SPLITS, axis=-1)
    q = q.reshape(B, L, N_ATT_HEADS, HEAD_DIM)
    k = k.reshape(B, L, N_KV_HEADS, HEAD_DIM)
    v = v.reshape(B, L, N_KV_HEADS, HEAD_DIM)
    if ctx is None:
        att = _context_attention(q, k, v, p['attn_sink'])
        S0 = jnp.zeros((2, B, N_RWKV_HEADS, HEAD_DIM, HEAD_DIM), f32)
    else:
        k_ctx, v_ctx, s_ctx = ctx
        att = _latent_attention(_rope_2d(q), _rope_2d(k), v, k_ctx, v_ctx, p['attn_sink'])
        S0 = jnp.swapaxes(s_ctx, 0, 1)
    hy = _hyena(u_hy, p)
    rw, S_fin = _rwkv(u_rkv, w_dn, a_dn, g_dn, p, S0)
    o = jnp.concatenate([att.astype(h.dtype), hy.astype(h.dtype), rw.astype(h.dtype)], axis=-1) @ p['w_out']
    return o, (k, v, jnp.swapaxes(S_fin, 0, 1))


def _layer(x, cond, p, ctx):
    sh1, sc1, g1, sh2, sc2, g2 = _modulation(cond, p['ada_w'], p['ada_b'])
    nw = p['norm_w']
    h = _rmsnorm(x, nw[0]) * (1.0 + sc1) + sh1
    o, st = _mixer(h, p, ctx)
    x = x + g1 * _rmsnorm(o, nw[1])
    h = _rmsnorm(x, nw[2]) * (1.0 + sc2) + sh2
    f = jnp.square(jax.nn.relu(h @ p['mlp_w1'])) @ p['mlp_w2']
    x = x + g2 * _rmsnorm(f, nw[3])
    return x, st


def setup_inputs(seed: int = 0) -> dict:
    key = jax.random.key(seed)
    ks = iter(jax.random.split(key, 40))

    def nrm(shape, scale):
        return jax.random.normal(next(ks), shape, f32) * scale

    def unif(shape, lo, hi):
        return jax.random.uniform(next(ks), shape, f32, lo, hi)

    D = D_MODEL
    return {
        'x_prompt': nrm((BATCH, SEQ, D), 1.0),
        'x_sample': nrm((DEC_BATCH, DEC_SEQ, D), 1.0),
        'cache_k': nrm((DEC_BATCH, DEPTH, PAST_LEN, N_KV_HEADS, HEAD_DIM), 1.0),
        'cache_v': nrm((DEC_BATCH, DEPTH, PAST_LEN, N_KV_HEADS, HEAD_DIM), 1.0),
        'state_rwkv': nrm((DEC_BATCH, DEPTH, 2, N_RWKV_HEADS, HEAD_DIM, HEAD_DIM), 1.0),
        'c': nrm((DEC_BATCH, D), 1.0),
        'c_ctx': nrm((D,), 1.0),
        'ada_w': nrm((DEPTH, D, N_MOD * D), 0.5 * D ** -0.5),
        'ada_b': nrm((DEPTH, N_MOD * D), 0.02),
        'norm_w': 1.0 + nrm((DEPTH, 4, D), 0.02),
        'w_in': nrm((DEPTH, D, IN_COLS), D ** -0.5),
        'w_out': nrm((DEPTH, MIX_WIDTH, D), MIX_WIDTH ** -0.5),
        'attn_sink': nrm((DEPTH, N_ATT_HEADS), 0.5),
        'hy_short_w': nrm((DEPTH, SHORT_W, (HY_ORDER + 1) * HY_CH), SHORT_W ** -0.5),
        'hy_f1': nrm((DEPTH, HY_EMB, HY_FFN), HY_EMB ** -0.5),
        'hy_b1': nrm((DEPTH, HY_FFN), 0.02),
        'hy_f2': nrm((DEPTH, HY_FFN, HY_FFN), HY_FFN ** -0.5),
        'hy_b2': nrm((DEPTH, HY_FFN), 0.02),
        'hy_f3': nrm((DEPTH, HY_FFN, 2 * HY_N_FILT * HY_CH), 0.2 * HY_FFN ** -0.5),
        'hy_decay': unif((DEPTH, 2 * HY_N_FILT * HY_CH), HY_DECAY_MIN, HY_DECAY_MAX),
        'hy_skip': nrm((DEPTH, HY_N_FILT, HY_CH), 0.5),
        'rw_short_w': nrm((DEPTH, SHORT_W, 3 * RWKV_WIDTH), SHORT_W ** -0.5),
        'rw_w0': unif((DEPTH, 2, RWKV_WIDTH), -2.0, 1.0),
        'rw_w_up': nrm((DEPTH, 2, DECAY_LORA, RWKV_WIDTH), 0.5 * DECAY_LORA ** -0.5),
        'rw_a0': nrm((DEPTH, 2, RWKV_WIDTH), 0.1),
        'rw_a_up': nrm((DEPTH, 2, AAA_LORA, RWKV_WIDTH), 0.5 * AAA_LORA ** -0.5),
        'rw_g_up': nrm((DEPTH, GATE_LORA, RWKV_WIDTH), GATE_LORA ** -0.5),
        'rw_k_k': 0.85 + nrm((DEPTH, RWKV_WIDTH), 0.02),
        'rw_k_a': 1.0 + nrm((DEPTH, RWKV_WIDTH), 0.02),
        'rw_r_k': nrm((DEPTH, N_RWKV_HEADS, HEAD_DIM), 0.1),
        'rw_gn_w': 1.0 + nrm((DEPTH, RWKV_WIDTH), 0.02),
        'rw_gn_b': nrm((DEPTH, RWKV_WIDTH), 0.02),
        'mlp_w1': nrm((DEPTH, D, D_FF), D ** -0.5),
        'mlp_w2': nrm((DEPTH, D_FF, D), D_FF ** -0.5),
    }


def reference(x_prompt, x_sample, cache_k, cache_v, state_rwkv, c, c_ctx,
              ada_w, ada_b, norm_w, w_in, w_out, attn_sink,
              hy_short_w, hy_f1, hy_b1, hy_f2, hy_b2, hy_f3, hy_decay, hy_skip,
              rw_short_w, rw_w0, rw_w_up, rw_a0, rw_a_up, rw_g_up, rw_k_k, rw_k_a, rw_r_k,
              rw_gn_w, rw_gn_b, mlp_w1, mlp_w2):
    cond_ctx = c_ctx[None, :]
    y_p = x_prompt
    y_s = x_sample
    new_k, new_v, new_s = [], [], []
    for l in range(DEPTH):
        p = {
            'ada_w': ada_w[l], 'ada_b': ada_b[l], 'norm_w': norm_w[l],
            'w_in': w_in[l], 'w_out': w_out[l], 'attn_sink': attn_sink[l],
            'hy_short_w': hy_short_w[l], 'hy_f1': hy_f1[l], 'hy_b1': hy_b1[l],
            'hy_f2': hy_f2[l], 'hy_b2': hy_b2[l], 'hy_f3': hy_f3[l],
            'hy_decay': hy_decay[l], 'hy_skip': hy_skip[l],
            'rw_short_w': rw_short_w[l], 'rw_w0': rw_w0[l], 'rw_w_up': rw_w_up[l],
            'rw_a0': rw_a0[l], 'rw_a_up': rw_a_up[l], 'rw_g_up': rw_g_up[l],
            'rw_k_k': rw_k_k[l], 'rw_k_a': rw_k_a[l], 'rw_r_k': rw_r_k[l],
            'rw_gn_w': rw_gn_w[l], 'rw_gn_b': rw_gn_b[l],
            'mlp_w1': mlp_w1[l], 'mlp_w2': mlp_w2[l],
        }
        y_p, (k_l, v_l, s_l) = _layer(y_p, cond_ctx, p, None)
        new_k.append(k_l)
        new_v.append(v_l)
        new_s.append(s_l)
        y_s, _ = _layer(y_s, c, p, (cache_k[:, l], cache_v[:, l], state_rwkv[:, l]))
    new_cache_k = jnp.stack(new_k, axis=1)
    new_cache_v = jnp.stack(new_v, axis=1)
    new_state_rwkv = jnp.stack(new_s, axis=1)
    return (y_p, y_s, new_cache_k, new_cache_v, new_state_rwkv)
```

```python
import math
from contextlib import ExitStack, contextmanager
import numpy as np
import ml_dtypes
import concourse.bass as bass
import concourse.mybir as mybir
from concourse.bass_utils import run_bass_kernel_spmd

F32 = mybir.dt.float32
BF16 = mybir.dt.bfloat16
AF = mybir.ActivationFunctionType
ALU = mybir.AluOpType

D = 2048
KC = 16
DEPTH = 2
TB = 512
CH = 128
IN_COLS = 5760
C_Q, C_K, C_V, C_HY, C_RKV, C_WDN, C_ADN, C_GDN = 0, 768, 1024, 1280, 2816, 5120, 5312, 5504
MAGIC = 12582912.0
SAME_SYNC = True


_KB = [None]


class Buf:
    __slots__ = ("name", "w", "r", "dsem", "dcnt")

    def __init__(self, name):
        self.name = name
        self.w = None
        self.r = {}
        self.dsem = None
        self.dcnt = 0
        kb = _KB[0]
        if kb is not None:
            if kb.freed:
                self.r = {"f%d" % k: t for k, t in kb.freed.items()}
            if kb.phase_all:
                kb.phase_all[-1].append(self)


class T:
    def __init__(self, t, name):
        self.t = t
        self.b = Buf(name)

    def __getitem__(self, key):
        return self.t[key]


class KB:
    def __init__(self):
        self.nc = bass.Bass("TRN2", target_bir_lowering=False)
        nc = self.nc
        self.E = {"pe": nc.tensor, "act": nc.scalar, "dve": nc.vector, "pool": nc.gpsimd, "sp": nc.sync}
        self.sem = {e: nc.alloc_semaphore("prog_" + e) for e in self.E}
        self.cnt = {e: 0 for e in self.E}
        self.waited = {e: {} for e in self.E}
        self.semkey = {}
        self.dma_latest = {}
        self.n_ins = 0
        self.dram_in = {}
        self.dram_out = {}
        self.psum = []
        self.ps_i = 0
        self.uid = 0
        self.free_dsems = []
        self.live = []
        self.freed = {}
        self.phase_all = []
        self.retired = []
        _KB[0] = self
        self.keep = []
        self.dpool = {"sp": [], "pool": []}
        self.phase_bufs = []

    def _key(self, sem):
        k = self.semkey.get(id(sem))
        if k is None:
            k = len(self.semkey) + 1
            self.semkey[id(sem)] = k
            self.keep.append(sem)
        return k

    def _wait(self, eng, deps):
        for (sem, val, owner) in deps:
            if owner == eng:
                if eng == "pe" or not SAME_SYNC:
                    continue
                if val > self.cnt[eng]:
                    continue
            elif owner in self.cnt and val > self.cnt[owner]:
                raise RuntimeError(f"wait on pending count: {eng} waits {owner} {val} > {self.cnt[owner]}")
            k = self._key(sem)
            if self.waited[eng].get(k, 0) >= val:
                continue
            self.E[eng].wait_ge(sem, val)
            self.waited[eng][k] = val

    def _deps(self, reads, writes):
        deps = []
        for b in reads:
            if b.w is not None:
                deps.append(b.w)
        for b in writes:
            if b.w is not None:
                deps.append(b.w)
            deps.extend(b.r.values())
        return deps

    def op(self, eng, fn, reads=(), writes=(), inc=True):
        self._wait(eng, self._deps(reads, writes))
        ins = fn(self.E[eng])
        self.n_ins += 1
        if inc:
            ins.then_inc(self.sem[eng], 1)
            self.cnt[eng] += 1
            tag = (self.sem[eng], self.cnt[eng], eng)
        else:
            tag = (self.sem[eng], self.cnt[eng] + 1, eng)
        for b in reads:
            b.r[eng] = tag
        for b in writes:
            b.w = tag
            b.r = {}
        return ins

    def dma(self, q, out, in_, reads=(), writes=()):
        self._wait(q, self._deps(reads, writes))
        owner = (list(writes) + list(reads))[0]
        if owner.dsem is None:
            if self.dpool[q]:
                owner.dsem = self.dpool[q].pop()
            else:
                owner.dsem = [self.nc.alloc_semaphore(f"d{self.uid}"), 0, q]
                self.uid += 1
            if self.phase_bufs:
                self.phase_bufs[-1].append(owner)
        assert owner.dsem[2] == q, "a Buf's DMA semaphore must stay on one queue type"
        ins = self.E[q].dma_start(out=out, in_=in_)
        ins.then_inc(owner.dsem[0], 16)
        owner.dsem[1] += 16
        self.n_ins += 1
        tag = (owner.dsem[0], owner.dsem[1], "dma")
        self.dma_latest[self._key(owner.dsem[0])] = tag
        for b in reads:
            b.r["dma%d" % id(owner)] = tag
        for b in writes:
            b.w = tag
            b.r = {}

    def barrier(self):
        tags = [(self.sem[f], self.cnt[f], f) for f in self.E if self.cnt[f] > 0]
        tags += list(self.dma_latest.values())
        for e in self.E:
            self._wait(e, [t for t in tags if t[2] != e])
        self.freed = {}
        for ds in self.retired:
            self.dpool[ds[2]].append(ds)
        self.retired = []

    def din(self, name, shape, dtype=F32):
        self.dram_in[name] = self.nc.dram_tensor(name, list(shape), dtype, kind="ExternalInput").ap()
        return self.dram_in[name]

    def dout(self, name, shape):
        self.dram_out[name] = self.nc.dram_tensor(name, list(shape), F32, kind="ExternalOutput").ap()
        return self.dram_out[name]

    def sb(self, name, shape, dtype=F32):
        self.uid += 1
        self.live.append((name, int(np.prod(shape[1:])) * (2 if dtype == BF16 else 4)))
        return T(self.nc.alloc_sbuf_tensor(f"{name}_{self.uid}", list(shape), dtype), name)

    @contextmanager
    def phase(self, barrier=False):
        st = ExitStack()
        self.phase_all.append([])
        kb = self

        class Ph:
            def sb(self_, name, shape, dtype=F32):
                kb.uid += 1
                try:
                    t = st.enter_context(kb.nc.sbuf_tensor(f"{name}_{kb.uid}", list(shape), dtype))
                except Exception:
                    print("LIVE:", [(n, b) for n, b in kb.live])
                    raise
                nb = int(np.prod(shape[1:])) * (2 if dtype == BF16 else 4)
                kb.live.append((name, nb))
                st.callback(lambda: kb.live.remove((name, nb)))
                return T(t, name)
        self.phase_bufs.append([])
        try:
            yield Ph()
        finally:
            allb = self.phase_all.pop()
            if barrier:
                self.barrier()
                for b in self.phase_bufs.pop():
                    self.dpool[b.dsem[2]].append(b.dsem)
                    b.dsem = None
            else:
                for b in allb:
                    for t in ([b.w] if b.w is not None else []) + list(b.r.values()):
                        if t[2] in self.cnt and t[1] > self.cnt[t[2]]:
                            t = (t[0], self.cnt[t[2]] + 1, t[2])
                        k = self._key(t[0])
                        if k not in self.freed or self.freed[k][1] < t[1]:
                            self.freed[k] = t
                for b in self.phase_bufs.pop():
                    self.retired.append(b.dsem)
                    b.dsem = None
            st.close()

    def init_psum(self):
        for i in range(8):
            self.psum.append(T(self.nc.alloc_psum_tensor(f"ps{i}", [128, 512], F32), f"ps{i}"))

    def ps(self):
        p = self.psum[self.ps_i % 8]
        self.ps_i += 1
        return p

    def mm(self, ps, out, lhsT, rhs, start, stop, reads, inc=None, tp=None):
        if inc is None:
            inc = stop
        kw = {}
        if tp is not None:
            kw["tile_position"] = tp
        return self.op("pe", lambda e: e.matmul(out, lhsT=lhsT, rhs=rhs, start=start, stop=stop, **kw),
                       reads=reads, writes=[ps.b], inc=inc)

    def tr(self, ps, out, in_, ident, reads):
        return self.op("pe", lambda e: e.transpose(out=out, in_=in_, identity=ident), reads=reads, writes=[ps.b])

    def act(self, out, in_, func, reads, writes, bias=None, scale=None, eng="act"):
        kw = {}
        if bias is not None:
            kw["bias"] = bias
        if scale is not None:
            kw["scale"] = scale
        return self.op("act", lambda e: e.activation(out=out, in_=in_, func=func, **kw), reads=reads, writes=writes)

    def tt(self, out, in0, in1, op, reads, writes, eng="dve"):
        return self.op(eng, lambda e: e.tensor_tensor(out=out, in0=in0, in1=in1, op=op), reads=reads, writes=writes)

    def ts(self, out, in0, s1, op0, reads, writes, s2=None, op1=None, eng="dve"):
        if op1 is None:
            return self.op(eng, lambda e: e.tensor_scalar(out=out, in0=in0, scalar1=s1, scalar2=None, op0=op0),
                           reads=reads, writes=writes)
        return self.op(eng, lambda e: e.tensor_scalar(out=out, in0=in0, scalar1=s1, scalar2=s2, op0=op0, op1=op1),
                       reads=reads, writes=writes)

    def stt(self, out, in0, scalar, in1, op0, op1, reads, writes):
        return self.op("dve", lambda e: e.scalar_tensor_tensor(out=out, in0=in0, scalar=scalar, in1=in1, op0=op0, op1=op1),
                       reads=reads, writes=writes)

    def copy(self, out, in_, reads, writes, eng="dve"):
        if eng == "act":
            return self.act(out, in_, AF.Identity, reads, writes)
        return self.op(eng, lambda e: e.tensor_copy(out=out, in_=in_), reads=reads, writes=writes)

    def memset(self, out, val, writes, eng="dve"):
        return self.op(eng, lambda e: e.memset(out, val), writes=writes)


class WRing:
    def __init__(self, kb, ph, name, nslots, shape):
        self.kb = kb
        self.slots = [ph.sb(f"{name}{i}", shape, BF16) for i in range(nslots)]
        self.i = 0

    def load(self, dram_view, view=None):
        s = self.slots[self.i % len(self.slots)]
        self.i += 1
        dst = s.t[:] if view is None else view(s)
        self.kb.dma("pool", dst, dram_view, writes=[s.b])
        return s


def stream(kb, ring, views, dstview=None):
    n = len(views)
    ns = len(ring.slots)
    loaded = []
    nxt = 0
    for i in range(n):
        while nxt < n and nxt < i + ns:
            loaded.append(ring.load(views[nxt], dstview))
            nxt += 1
        yield i, loaded[i]


def build(cfg):
    kb = KB()
    nc = kb.nc
    kb.init_psum()
    do_att, do_hy, do_rw = cfg.get("att", 1), cfg.get("hy", 1), cfg.get("rw", 1)

    xT = {"S": kb.din("xT_s", [D, 1024]), "P": kb.din("xT_p", [D, 512])}
    yT = {"S": kb.dout("yT_s", [D, 1024]), "P": kb.dout("yT_p", [D, 512])}
    condT = kb.din("condT", [128, KC, 2])
    ada_w = kb.din("ada_w", [DEPTH, D, 6 * D])
    ada_b = kb.din("ada_b_fm", [DEPTH, 128, 96])
    norm_w = kb.din("norm_w_fm", [DEPTH, 128, 4, KC])
    w_in = kb.din("w_in", [DEPTH, D, IN_COLS])
    w_qkp = kb.din("w_qkp", [DEPTH, D, 1088])
    w_out = kb.din("w_out", [DEPTH, D, D])
    mlp_w1 = kb.din("mlp_w1", [DEPTH, D, 4 * D])
    mlp_w2 = kb.din("mlp_w2", [DEPTH, 4 * D, D])
    consts = {}

    def cin(name, shape, dtype=F32):
        consts[name] = kb.din(name, shape, dtype)
        return consts[name]
    ident_d = cin("c_ident", [128, 128])
    bones_d = cin("c_bones", [128, 128])
    kctxT = kb.din("kctxT", [DEPTH, 4, 64, 512])
    vctx = kb.din("vctx", [DEPTH, 512, 256])
    sink_d = kb.din("sink_bc", [DEPTH, 128, 12])
    rope_c = cin("c_rope_cos", [64, 1024])
    rope_s = cin("c_rope_sin", [64, 1024])
    wmask_d = cin("c_wmask", [6, 128, 512], BF16)
    newk = kb.dout("newkT", [DEPTH, 4, 64, 512])
    newv = kb.dout("newv", [DEPTH, 512, 256])
    hy_short = kb.din("hy_short_fm", [DEPTH, 128, 12, 3])
    hy_f1 = kb.din("hy_f1", [DEPTH, 33, 64])
    hy_b1 = kb.din("hy_b1", [DEPTH, 64, 1])
    hy_f2 = kb.din("hy_f2", [DEPTH, 64, 64])
    hy_b2 = kb.din("hy_b2", [DEPTH, 64, 1])
    hy_f3 = kb.din("hy_f3", [DEPTH, 64, 1024])
    hy_decay = kb.din("hy_decay_bc", [DEPTH, 128, 1024])
    hy_skip = kb.din("hy_skip_fm", [DEPTH, 128, 4])
    hyc = {}
    for g, L in (("S", 1024), ("P", 256)):
        hyc[g] = dict(
            featT=cin(f"c_featT_{g}", [33, L]), t01=cin(f"c_t01_{g}", [128, L // 128]),
            C=cin(f"c_dftC_{g}", [L, L], BF16), S=cin(f"c_dftS_{g}", [L, L], BF16),
            Ch=cin(f"c_dftCh_{g}", [L, L], BF16), Sh=cin(f"c_dftSh_{g}", [L, L], BF16),
            Cb=cin(f"c_dftCb_{g}", [L, L], BF16), Sb=cin(f"c_dftSb_{g}", [L, L], BF16), Si=cin(f"c_dftSi_{g}", [L, L], BF16))
    rw_short = kb.din("rw_short_fm", [DEPTH, 128, 18, 3])
    rw_w0 = kb.din("rw_w0_fm", [DEPTH, 128, 2, 6])
    rw_a0 = kb.din("rw_a0_fm", [DEPTH, 128, 2, 6])
    rw_wup = kb.din("rw_w_up", [DEPTH, 2, 96, 768])
    rw_aup = kb.din("rw_a_up", [DEPTH, 2, 96, 768])
    rw_gup = kb.din("rw_g_up", [DEPTH, 256, 768])
    rw_vec = kb.din("rw_vec_fm", [DEPTH, 128, 5, 6])
    st0T = kb.din("st0T", [DEPTH, 2, 12, 64, 64])
    newst = kb.dout("newstT", [DEPTH, 2, 2, 12, 64, 64])
    scanmask_d = cin("c_scanmask", [128, 1024])
    rwmask_d = cin("c_rwmask", [2, 128, 512], BF16)
    rwmaskA_d = cin("c_rwmaskA", [2, 128, 128], BF16)

    ident = kb.sb("ident", [128, 128])
    bones = kb.sb("bones", [128, 128])
    ones_bf = kb.sb("ones_bf", [128, 128], BF16)
    mod = kb.sb("mod", [128, DEPTH, 2, 6, KC])
    nw = kb.sb("nw", [128, DEPTH, 4, KC])
    dmod = kb.sb("dmod", [128, DEPTH, 2, 6, KC])
    epsb = kb.sb("epsb", [128, 1])
    negpi = kb.sb("negpi", [128, 1])
    kb.dma("sp", ident[:], ident_d, writes=[ident.b])
    kb.dma("sp", bones[:], bones_d, writes=[bones.b])
    kb.dma("sp", nw[:], norm_w.rearrange("l p a c -> p l a c"), writes=[nw.b])
    esink_raw = kb.sb("esink_raw", [128, DEPTH, 12])
    esink_all = kb.sb("esink_all", [128, DEPTH, 12])
    kb.dma("sp", esink_raw[:], sink_d.rearrange("l p h -> p l h"), writes=[esink_raw.b])
    kb.act(esink_all[:], esink_raw[:], AF.Exp, [esink_raw.b], [esink_all.b])
    kb.memset(ones_bf[:], 1.0, [ones_bf.b])
    kb.memset(epsb[:], 1e-6, [epsb.b])
    kb.memset(negpi[:], -math.pi, [negpi.b])

    with kb.phase() as ph:
        cnd = ph.sb("cnd", [128, KC, 2])
        cndb = ph.sb("cndb", [128, KC, 2], BF16)
        adab = ph.sb("adab", [128, DEPTH, 96])
        kb.dma("sp", cnd[:], condT, writes=[cnd.b])
        kb.dma("sp", adab[:], ada_b.rearrange("l p j -> p l j"), writes=[adab.b])
        kb.act(cndb[:], cnd[:], AF.Silu, [cnd.b], [cndb.b])
        ring = WRing(kb, ph, "adaw", 3, [128, KC, 512])
        for l in range(DEPTH):
            wv = ada_w[l].rearrange("(kc p) n -> p kc n", p=128)
            views = [wv[:, :, i * 512:(i + 1) * 512] for i in range(24)]
            pst = kb.ps()
            for i, slot in stream(kb, ring, views):
                for j in range(4):
                    jj = i * 4 + j
                    for kc in range(KC):
                        kb.mm(pst, pst[:, 2 * jj:2 * jj + 2], slot[:, kc, j * 128:(j + 1) * 128], cndb[:, kc, :],
                              kc == 0, kc == KC - 1, [slot.b, cndb.b])
            for g in range(2):
                kb.tt(mod[:, l, g, :, :].rearrange("p a c -> p (a c)"),
                      pst[:, 0:192].rearrange("p (j g) -> p j g", g=2)[:, :, g], adab[:, l, :], ALU.add,
                      [pst.b, adab.b], [mod.b])
        for l in range(DEPTH):
            for g in range(2):
                m = lambda i: mod[:, l, g, i, :]
                dm = lambda i: dmod[:, l, g, i, :]
                kb.stt(dm(0), m(1), 1.0, nw[:, l, 0, :], ALU.add, ALU.mult, [mod.b, nw.b], [dmod.b])
                kb.copy(dm(1), m(0), [mod.b], [dmod.b])
                kb.tt(dm(2), m(2), nw[:, l, 1, :], ALU.mult, [mod.b, nw.b], [dmod.b])
                kb.stt(dm(3), m(4), 1.0, nw[:, l, 2, :], ALU.add, ALU.mult, [mod.b, nw.b], [dmod.b])
                kb.copy(dm(4), m(3), [mod.b], [dmod.b])
                kb.tt(dm(5), m(5), nw[:, l, 3, :], ALU.mult, [mod.b, nw.b], [dmod.b])

    def rstd_from(ph, src_chunks, T_, tag, out=None):
        rstd = out if out is not None else ph.sb("rstd" + tag, [128, T_])
        sq = ph.sb("sq" + tag, [128, 2, TB], BF16)
        sqb = [Buf("sq0"), Buf("sq1")]
        for tb in range(T_ // TB):
            pst = kb.ps()
            for c in range(KC):
                ap, bufs = src_chunks(c, tb)
                kb.act(sq[:, c % 2, :], ap, AF.Square, bufs, [sqb[c % 2]])
                kb.mm(pst, pst[:, :], ones_bf[:], sq[:, c % 2, :], c == 0, c == KC - 1, [sqb[c % 2], ones_bf.b], inc=True)
            sl = slice(tb * TB, (tb + 1) * TB)
            kb.act(rstd[:, sl], pst[:, :], AF.Ln, [pst.b, epsb.b], [rstd.b], bias=epsb[:], scale=1.0 / D)
            kb.act(rstd[:, sl], rstd[:, sl], AF.Exp, [rstd.b], [rstd.b], scale=-0.5)
        return rstd

    rstd_cache = {}

    def make_h(ph, x, T_, l, g, which, tag):
        h = ph.sb("h" + tag, [128, KC, T_], BF16)
        with kb.phase() as p2:
            if "r" in rstd_cache:
                rstd = rstd_cache["r"]
            else:
                rstd = rstd_from(p2, lambda c, tb: (x[:, c, tb * TB:(tb + 1) * TB], [x.b]), T_, tag)
            tmp = p2.sb("htmp", [128, 2, T_])
            tb_ = [Buf("t0"), Buf("t1")]
            for c in range(KC):
                kb.tt(tmp[:, c % 2, :], x[:, c, :], rstd[:], ALU.mult, [x.b, rstd.b], [tb_[c % 2]])
                kb.act(h[:, c, :], tmp[:, c % 2, :], AF.Identity, [tb_[c % 2], dmod.b], [h.b],
                       bias=dmod[:, l, g, which + 1, c:c + 1], scale=dmod[:, l, g, which, c:c + 1])
        return h


    def proj(ph, wd, ncols_list, xk, nkc, T_, evac, name, pk=128, nslots=3, tilew=None, loadw=None, group=1):
        wds = wd if isinstance(wd, (list, tuple)) else [wd]
        ns_ = len(wds)
        tilew = tilew or max(w for _, w in ncols_list)
        ring = WRing(kb, ph, name, nslots, [pk, nkc, ns_, group * tilew])
        wvs = [w_.rearrange("(kc p) n -> p kc n", p=pk) for w_ in wds]
        n = len(ncols_list)
        ngr = (n + group - 1) // group
        loaded = []
        nxt = 0
        for ci in range(n):
            gidx = ci // group
            while nxt < ngr and nxt < gidx + nslots:
                s_ = ring.slots[nxt % nslots]
                first = nxt * group
                c0_, _w = ncols_list[first]
                totw = sum(w__ for _, w__ in ncols_list[first:first + group])
                for j in range(ns_):
                    c0j = c0_[j] if isinstance(c0_, (list, tuple)) else c0_
                    kb.dma("pool", s_[:, :, j, 0:totw], wvs[j][:, :, c0j:c0j + totw], writes=[s_.b])
                loaded.append(s_)
                nxt += 1
            slot = loaded[gidx]
            w = ncols_list[ci][1]
            off = (ci % group) * w
            tbw = min(TB, T_)
            for tb in range(max(1, T_ // TB)):
                sl = slice(tb * tbw, (tb + 1) * tbw)
                psts = []
                for j in range(ns_):
                    pst = kb.ps()
                    for kc in range(nkc):
                        xap, xb = xk(kc, sl)
                        kb.mm(pst, pst[0:w, 0:tbw], slot[:, kc, j, off:off + w], xap, kc == 0, kc == nkc - 1, [slot.b] + xb)
                    psts.append(pst)
                evac(ci, tb, psts[0] if ns_ == 1 else psts, w, sl)

    def run_group(g):
        with kb.phase(barrier=True) as phg:
            run_group_(g, phg)

    def run_group_(g, phg):
        T_ = 1024 if g == "S" else 512
        L = 1024 if g == "S" else 256
        nseq = T_ // L
        gi = 0 if g == "S" else 1
        NTB = T_ // TB
        x = phg.sb("x" + g, [128, KC, T_])
        kb.dma("sp", x[:], xT[g].rearrange("(c p) t -> p c t", p=128), writes=[x.b])
        for l in range(DEPTH):
            with kb.phase(barrier=True) as phm:
                rwout = phm.sb("rwout", [128, 6, T_], BF16)
                rstd1 = phm.sb("rstd1", [128, T_])
                with kb.phase() as pr_:
                    rstd_from(pr_, lambda c, tb: (x[:, c, tb * TB:(tb + 1) * TB], [x.b]), T_, "c", out=rstd1)
                rstd_cache["r"] = rstd1
                if do_rw:
                    rwkv_mixer(phm, g, l, x, rwout, T_, L, nseq, gi)
                else:
                    kb.memset(rwout[:], 0.0, [rwout.b])
                mix = phm.sb("mix", [128, 10, T_], BF16)
                kb.memset(mix[:], 0.0, [mix.b])
                if do_hy:
                    hyena_mixer(phm, g, l, x, mix, T_, L, nseq, gi)
                if do_att:
                    att_mixer(phm, g, l, x, mix, T_, L, nseq, gi)
                rstd_cache.pop("r", None)
                with kb.phase() as ph:
                    resid_branch(ph, w_out[l], KC, lambda kc, sl: ((mix[:, kc, sl], [mix.b]) if kc < 10 else (rwout[:, kc - 10, sl], [rwout.b])),
                                 x, T_, l, gi, 2, "wo")
            for tb in range(NTB):
                sl = slice(tb * TB, (tb + 1) * TB)
                with kb.phase(barrier=True) as ph:
                    xs = T(x.t, "xs")
                    xs.b = x.b
                    h2 = ph.sb("h2", [128, KC, TB], BF16)
                    with kb.phase() as p2:
                        rstd = rstd_from(p2, lambda c, _tb: (x[:, c, sl], [x.b]), TB, "m")
                        tmp = p2.sb("htmp", [128, 2, TB])
                        tb_ = [Buf("t0"), Buf("t1")]
                        for c in range(KC):
                            kb.tt(tmp[:, c % 2, :], x[:, c, sl], rstd[:], ALU.mult, [x.b, rstd.b], [tb_[c % 2]])
                            kb.act(h2[:, c, :], tmp[:, c % 2, :], AF.Identity, [tb_[c % 2], dmod.b], [h2.b],
                                   bias=dmod[:, l, gi, 4, c:c + 1], scale=dmod[:, l, gi, 3, c:c + 1])
                    fbuf = ph.sb("fbuf", [128, KC, TB])
                    for half in range(2):
                        with kb.phase() as p3:
                            hid = p3.sb("hid", [128, 32, TB], BF16)
                            rtmp = p3.sb("rtmp", [128, 2, TB])
                            rb = [Buf("r0"), Buf("r1")]

                            def ev1(ci, _tb, pst, w, _sl):
                                kb.act(rtmp[:, ci % 2, :], pst[:, :], AF.Relu, [pst.b], [rb[ci % 2]])
                                kb.tt(hid[:, ci, :], rtmp[:, ci % 2, :], rtmp[:, ci % 2, :], ALU.mult, [rb[ci % 2]], [hid.b])
                            proj(p3, mlp_w1[l], [(half * 4096 + i * 128, 128) for i in range(32)],
                                 lambda kc, s_: (h2[:, kc, s_], [h2.b]), KC, TB, ev1, "w1", group=2)

                            def ev2(ci, _tb, pst, w, _sl):
                                if half == 0:
                                    kb.copy(fbuf[:, ci, :], pst[:, :], [pst.b], [fbuf.b], eng="act")
                                else:
                                    kb.tt(fbuf[:, ci, :], fbuf[:, ci, :], pst[:, :], ALU.add, [pst.b, fbuf.b], [fbuf.b])
                            proj(p3, mlp_w2[l][half * 4096:(half + 1) * 4096, :], [(i * 128, 128) for i in range(KC)],
                                 lambda kc, s_: (hid[:, kc, s_], [hid.b]), 32, TB, ev2, "w2", nslots=2)
                    norm_add(ph, fbuf, x, sl, TB, l, gi, 5, "f")
        kb.dma("sp", yT[g].rearrange("(c p) t -> p c t", p=128), x[:], reads=[x.b])

    def norm_add(ph, obuf, x, sl, tw, l, gi, which, tag):
        with kb.phase() as p2:
            rstd = rstd_from(p2, lambda c, _tb: (obuf[:, c, :], [obuf.b]), tw, tag)
            tmp = p2.sb("ntmp", [128, 2, tw])
            tb_ = [Buf("t0"), Buf("t1")]
            for c in range(KC):
                kb.tt(tmp[:, c % 2, :], obuf[:, c, :], rstd[:], ALU.mult, [obuf.b, rstd.b], [tb_[c % 2]])
                kb.stt(x[:, c, sl], tmp[:, c % 2, :], dmod[:, l, gi, which, c:c + 1], x[:, c, sl], ALU.mult, ALU.add,
                       [tb_[c % 2], dmod.b, x.b], [x.b])

    def resid_branch(ph, wd, nkc, xk, x, T_, l, gi, which, name):
        for tb in range(T_ // TB):
            sl = slice(tb * TB, (tb + 1) * TB)
            with kb.phase() as p1:
                obuf = p1.sb("obuf", [128, KC, TB])

                def ev(ci, _tb, pst, w, _sl):
                    kb.copy(obuf[:, ci, :], pst[:, :], [pst.b], [obuf.b], eng="act")
                proj(p1, wd, [(i * 128, 128) for i in range(KC)],
                     lambda kc, s_: xk(kc, slice(sl.start + s_.start, sl.start + s_.stop)), nkc, TB, ev, name, group=2)
                norm_add(p1, obuf, x, sl, TB, l, gi, which, "o")

    def rwkv_mixer(phm, g, l, x, rwout, T_, L, nseq, gi):
        NCH = T_ // 128
        ncs = L // 128
        NB = max(1, T_ // TB)
        BW = min(TB, T_)
        CPB = BW // 128
        C0 = math.exp(-0.5)
        c4 = lambda ap: ap.rearrange("p (c t) -> p c t", t=128)
        with kb.phase() as ph:
            wdn = ph.sb("wdn", [128, 2, T_], BF16)
            adn = ph.sb("adn", [128, 2, T_], BF16)
            gdn = ph.sb("gdn", [128, 2, T_], BF16)
            rsw = ph.sb("rsw", [128, 18, 3])
            w0 = ph.sb("w0", [128, 2, 6])
            a0 = ph.sb("a0", [128, 2, 6])
            vec = ph.sb("vec", [128, 5, 6])
            rwm = ph.sb("rwm", [128, 2, 512], BF16)
            rwmA = ph.sb("rwmA", [128, 2, 128], BF16)
            scm = ph.sb("scm", [128, BW])
            eps24 = ph.sb("eps24", [128, 1])
            epsgn = ph.sb("epsgn", [128, 1])
            identb = ph.sb("identb", [128, 128], BF16)
            kb.memset(wdn[:], 0.0, [wdn.b])
            kb.memset(adn[:], 0.0, [adn.b])
            kb.memset(eps24[:], 1e-24, [eps24.b])
            kb.memset(epsgn[:], 64e-5, [epsgn.b])
            kb.copy(identb[:], ident[:], [ident.b], [identb.b])
            kb.dma("sp", rsw[:], rw_short[l], writes=[rsw.b])
            kb.dma("sp", w0[:], rw_w0[l], writes=[w0.b])
            kb.dma("sp", a0[:], rw_a0[l], writes=[a0.b])
            kb.dma("sp", vec[:], rw_vec[l], writes=[vec.b])
            kb.dma("sp", rwm[:], rwmask_d.rearrange("d p q -> p d q"), writes=[rwm.b])
            kb.dma("sp", rwmA[:], rwmaskA_d.rearrange("d p q -> p d q"), writes=[rwmA.b])
            kb.dma("sp", scm[:], scanmask_d[:, 0:BW], writes=[scm.b])
            h_keep = make_h(ph, x, T_, l, gi, 0, "rk") if g == "P" else None
            with kb.phase() as p1:
                h = h_keep if h_keep is not None else make_h(p1, x, T_, l, gi, 0, "r")
                tiles = [(C_WDN, 96), (C_WDN + 96, 96), (C_ADN, 96), (C_ADN + 96, 96), (C_GDN, 128), (C_GDN + 128, 128)]

                def evl(ci, tb, pst, w, sl):
                    n_ = sl.stop - sl.start
                    if ci < 2:
                        kb.act(wdn[0:96, ci, sl], pst[0:96, 0:n_], AF.Tanh, [pst.b], [wdn.b])
                    elif ci < 4:
                        kb.act(adn[0:96, ci - 2, sl], pst[0:96, 0:n_], AF.Identity, [pst.b], [adn.b])
                    else:
                        kb.act(gdn[:, ci - 4, sl], pst[:, 0:n_], AF.Sigmoid, [pst.b], [gdn.b])
                proj(p1, w_in[l], tiles, lambda kc, sl: (h[:, kc, sl], [h.b]), KC, T_, evl, "wlo", nslots=2)
            wv_in = w_in[l].rearrange("(kc p) n -> p kc n", p=128)
            for j in range(6):
                with kb.phase(barrier=True) as pp:
                    KRz = [[pp.sb(f"KRz{d}{hh}", [128, NCH, 2, 128], BF16) for hh in range(2)] for d in range(2)]
                    BK = [pp.sb(f"BK{d}", [128, NCH, 2, 128], BF16) for d in range(2)]
                    BKT = [pp.sb(f"BKT{d}", [128, NCH, 2, 128], BF16) for d in range(2)]
                    etot = pp.sb("etot", [128, 2, NCH])
                    VT = pp.sb("VT", [128, NCH, 128], BF16)
                    yacc = pp.sb("yacc", [128, T_])
                    bonus = pp.sb("bonus", [128, T_], BF16)
                    ups = pp.sb("ups", [128, 3, 2, 128], BF16)
                    cv = [pp.sb(f"rcv{a}", [128, T_]) for a in range(3)]
                    kb.memset(ups[:], 0.0, [ups.b])
                    for d in range(2):
                        for hh in range(2):
                            kb.memset(KRz[d][hh][:], 0.0, [KRz[d][hh].b])
                        kb.dma("pool", ups[0:96, 0, d, :], rw_wup[l, d][:, j * 128:(j + 1) * 128], writes=[ups.b])
                        kb.dma("pool", ups[0:96, 1, d, :], rw_aup[l, d][:, j * 128:(j + 1) * 128], writes=[ups.b])
                        kb.dma("pool", ups[:, 2, d, :], rw_gup[l][d * 128:(d + 1) * 128, j * 128:(j + 1) * 128], writes=[ups.b])
                    kb.memset(yacc[:], 0.0, [yacc.b])
                    with kb.phase() as pj:
                        h = h_keep if h_keep is not None else make_h(pj, x, T_, l, gi, 0, "r")
                        wt = pj.sb("wrkv", [128, KC, 3, 128], BF16)
                        for a in range(3):
                            c0 = C_RKV + a * 768 + j * 128
                            kb.dma("pool", wt[:, :, a, :], wv_in[:, :, c0:c0 + 128], writes=[wt.b])
                        for a in range(3):
                            wi = a * 6 + j
                            pss = []
                            for tb in range(NB):
                                pst = kb.ps()
                                for kc in range(KC):
                                    kb.mm(pst, pst[:, 0:BW], wt[:, kc, a, :], h[:, kc, tb * BW:(tb + 1) * BW], kc == 0, kc == KC - 1, [wt.b, h.b])
                                pss.append(pst)
                            for tb in range(NB):
                                kb.ts(cv[a][:, tb * BW:(tb + 1) * BW], pss[tb][:, 0:BW], rsw[:, wi, 1:2], ALU.mult, [pss[tb].b, rsw.b], [cv[a].b])
                            for s_ in range(nseq):
                                for tb in range(NB):
                                    lo, hi = max(s_ * L, tb * BW), min((s_ + 1) * L, (tb + 1) * BW)
                                    if lo >= hi:
                                        continue
                                    o = tb * BW
                                    kb.stt(cv[a][:, lo + 1:hi], pss[tb][:, lo - o:hi - 1 - o], rsw[:, wi, 0:1], cv[a][:, lo + 1:hi], ALU.mult, ALU.add,
                                           [pss[tb].b, rsw.b, cv[a].b], [cv[a].b])
                                    kb.stt(cv[a][:, lo:hi - 1], pss[tb][:, lo + 1 - o:hi - o], rsw[:, wi, 2:3], cv[a][:, lo:hi - 1], ALU.mult, ALU.add,
                                           [pss[tb].b, rsw.b, cv[a].b], [cv[a].b])
                                    if lo > s_ * L:
                                        kb.stt(cv[a][:, lo:lo + 1], pss[tb - 1][:, BW - 1:BW], rsw[:, wi, 0:1], cv[a][:, lo:lo + 1], ALU.mult, ALU.add,
                                               [pss[tb - 1].b, rsw.b, cv[a].b], [cv[a].b])
                                    if hi < (s_ + 1) * L:
                                        kb.stt(cv[a][:, hi - 1:hi], pss[tb + 1][:, 0:1], rsw[:, wi, 2:3], cv[a][:, hi - 1:hi], ALU.mult, ALU.add,
                                               [pss[tb + 1].b, rsw.b, cv[a].b], [cv[a].b])
                    rc, kc_, vc = cv
                    with kb.phase() as pq:
                        tp_ = [pq.sb(f"tp{i}", [128, BW]) for i in range(8)]
                        kap, t1, cum, ta, b32, kd32, excl, E = tp_
                        ntot = pq.sb("ntot", [128, CPB])
                        for bi in range(NB):
                            bs = slice(bi * BW, (bi + 1) * BW)
                            cs = slice(bi * CPB, (bi + 1) * CPB)
                            kb.ts(kap[:], kc_[:, bs], vec[:, 0, j:j + 1], ALU.mult, [kc_.b, vec.b], [kap.b])
                            kb.tt(t1[:], kap[:], kap[:], ALU.mult, [kap.b], [t1.b])
                            pst = kb.ps()
                            kb.mm(pst, pst[:, 0:BW], bones[:], t1[:], True, True, [bones.b, t1.b])
                            kb.act(t1[:], pst[:, 0:BW], AF.Ln, [pst.b, eps24.b], [t1.b], bias=eps24[:], scale=1.0)
                            kb.act(t1[:], t1[:], AF.Exp, [t1.b], [t1.b], scale=-0.5)
                            kb.tt(kap[:], kap[:], t1[:], ALU.mult, [kap.b, t1.b], [kap.b])
                            kb.tt(t1[:], rc[:, bs], kc_[:, bs], ALU.mult, [rc.b, kc_.b], [t1.b])
                            kb.ts(t1[:], t1[:], vec[:, 2, j:j + 1], ALU.mult, [t1.b, vec.b], [t1.b])
                            pst = kb.ps()
                            kb.mm(pst, pst[:, 0:BW], bones[:], t1[:], True, True, [bones.b, t1.b])
                            kb.tt(bonus[:, bs], pst[:, 0:BW], vc[:, bs], ALU.mult, [pst.b, vc.b], [bonus.b])
                            for c in range(CPB):
                                cg = bi * CPB + c
                                pst = kb.ps()
                                kb.tr(pst, pst[:, 0:128], vc[:, cg * 128:(cg + 1) * 128], ident[:], [vc.b, ident.b])
                                kb.copy(VT[:, cg, :], pst[:, 0:128], [pst.b], [VT.b], eng="act")
                            for d in range(2):
                                pst = kb.ps()
                                kb.mm(pst, pst[:, 0:BW], ups[:, 0, d, :], wdn[:, d, bs], True, True, [ups.b, wdn.b])
                                kb.act(t1[:], pst[:, 0:BW], AF.Sigmoid, [pst.b, w0.b], [t1.b], bias=w0[:, d, j:j + 1], scale=1.0)
                                kb.ts(t1[:], t1[:], -C0, ALU.mult, [t1.b], [t1.b])
                                kb.op("dve", lambda e: e.tensor_tensor_scan(out=cum[:], data0=scm[:], data1=t1[:], initial=0.0,
                                                                             op0=ALU.mult, op1=ALU.add), [scm.b, t1.b], [cum.b])
                                kb.tt(excl[:], cum[:], t1[:], ALU.subtract, [cum.b, t1.b], [excl.b])
                                pst = kb.ps()
                                kb.mm(pst, pst[:, 0:BW], ups[:, 1, d, :], adn[:, d, bs], True, True, [ups.b, adn.b])
                                kb.act(ta[:], pst[:, 0:BW], AF.Sigmoid, [pst.b, a0.b], [ta.b], bias=a0[:, d, j:j + 1], scale=1.0)
                                kb.tt(b32[:], kap[:], ta[:], ALU.mult, [kap.b, ta.b], [b32.b])
                                kb.ts(ta[:], ta[:], -1.0, ALU.add, [ta.b, vec.b], [ta.b], s2=vec[:, 1, j:j + 1], op1=ALU.mult)
                                kb.stt(kd32[:], ta[:], 1.0, kc_[:, bs], ALU.add, ALU.mult, [ta.b, kc_.b], [kd32.b])
                                totv = c4(cum[:])[:, :, 127]
                                kb.act(etot[:, d, cs], totv, AF.Exp, [cum.b], [etot.b])
                                kb.ts(ntot[:], totv, -1.0, ALU.mult, [cum.b], [ntot.b])

                                def expo(src, sc, use_tot, d=d):
                                    if d == 1:
                                        kb.act(E[:], src[:], AF.Exp, [src.b], [E.b], scale=sc)
                                    else:
                                        for c in range(CPB):
                                            bias = ntot[:, c:c + 1] if use_tot < 0 else cum[:, c * 128 + 127:c * 128 + 128]
                                            kb.act(E[:, c * 128:(c + 1) * 128], src[:, c * 128:(c + 1) * 128], AF.Exp, [src.b, ntot.b, cum.b], [E.b],
                                                   bias=bias, scale=sc)
                                if d == 0:
                                    expo(excl, 1.0, -1)
                                else:
                                    expo(cum, -1.0, 0)
                                for hh in range(2):
                                    pb = 64 * hh
                                    kb.tt(KRz[d][hh][pb:pb + 64, cs, 0, :], c4(kap[pb:pb + 64, :]), c4(E[pb:pb + 64, :]), ALU.mult,
                                          [kap.b, E.b], [KRz[d][hh].b])
                                if d == 0:
                                    expo(cum, 1.0, -1)
                                else:
                                    expo(excl, -1.0, 0)
                                for hh in range(2):
                                    pb = 64 * hh
                                    kb.tt(KRz[d][hh][pb:pb + 64, cs, 1, :], c4(rc[pb:pb + 64, bs]), c4(E[pb:pb + 64, :]), ALU.mult,
                                          [rc.b, E.b], [KRz[d][hh].b])
                                if d == 0:
                                    expo(cum, -1.0, +1)
                                else:
                                    expo(excl, 1.0, 0)
                                kb.tt(b32[:], b32[:], E[:], ALU.mult, [b32.b, E.b], [b32.b])
                                kb.tt(kd32[:], kd32[:], E[:], ALU.mult, [kd32.b, E.b], [kd32.b])
                                kb.copy(BK[d][:, cs, 0, :], c4(b32[:]), [b32.b], [BK[d].b])
                                kb.copy(BK[d][:, cs, 1, :], c4(kd32[:]), [kd32.b], [BK[d].b])
                                for c in range(CPB):
                                    cg = bi * CPB + c
                                    pst = kb.ps()
                                    kb.tr(pst, pst[:, 0:128], b32[:, c * 128:(c + 1) * 128], ident[:], [b32.b, ident.b])
                                    kb.tr(pst, pst[:, 128:256], kd32[:, c * 128:(c + 1) * 128], ident[:], [kd32.b, ident.b])
                                    kb.copy(BKT[d][:, cg, :, :], pst[:, 0:256].rearrange("p (a k) -> p a k", a=2), [pst.b], [BKT[d].b], eng="act")
                    with kb.phase() as pc:
                        SS = pc.sb("SS", [128, nseq, 2, 64])
                        SH = pc.sb("SH", [128, nseq, 2, 64])
                        SHb = pc.sb("SHb", [128, nseq, 2, 64], BF16)
                        RHs = [pc.sb(f"RH{i}", [128, 4, 64], BF16) for i in range(2)]
                        USs = [pc.sb(f"US{i}", [128, 4, 64], BF16) for i in range(2)]
                        if g == "S":
                            for d in range(2):
                                for hh in range(2):
                                    kb.dma("sp", SS[64 * hh:64 * hh + 64, 0, d, :], st0T[l, d, 2 * j + hh], writes=[SS.b])
                        else:
                            kb.memset(SS[:], 0.0, [SS.b])
                        u3 = lambda ap, n_, w_: ap.rearrange("p (u t) -> p u t", t=w_)
                        HS = min(ncs, 4)
                        NBT = (HS * nseq * 4) // 4
                        AB3 = [pc.sb(f"AB3_{b}", [128, 4, 384], BF16) for b in range(NBT)]
                        MTs = [pc.sb(f"MTs{b}", [128, 4, 128], BF16) for b in range(NBT)]
                        XX = [[pc.sb(f"XX{b}{i}", [128, 4, 128], BF16) for i in range(2)] for b in range(NBT)]
                        XXT = [[pc.sb(f"XXT{b}{i}", [128, 4, 128], BF16) for i in range(2)] for b in range(NBT)]

                        def units_of(i):
                            us = []
                            for s_ in range(nseq):
                                for d in range(2):
                                    c = s_ * ncs + (i if d == 0 else ncs - 1 - i)
                                    for hh in range(2):
                                        us.append((hh, d, c, s_))
                            return us
                        for half in range(ncs // HS):
                            steps = list(range(half * HS, (half + 1) * HS))
                            batches = []
                            for i in steps:
                                us = units_of(i)
                                for b0 in range(0, len(us), 4):
                                    batches.append((i, us[b0:b0 + 4]))
                            for b, (i, ub) in enumerate(batches):
                                pA = kb.ps()
                                for u, (hh, d, c, s_) in enumerate(ub):
                                    pst = kb.ps()
                                    kr = KRz[d][hh][:, c, :, :].rearrange("p a t -> p (a t)")
                                    kb.mm(pst, pst[:, 0:256], BK[d][:, c, 0, :], kr, True, True, [BK[d].b, KRz[d][hh].b])
                                    kb.mm(pst, pst[:, 256:512], BK[d][:, c, 1, :], kr, True, True, [BK[d].b, KRz[d][hh].b])
                                    kb.tt(XXT[b][0][:, u, :], pst[:, 0:128], rwm[:, d, 0:128], ALU.mult, [pst.b, rwm.b], [XXT[b][0].b])
                                    kb.tt(AB3[b][:, u, :], pst[:, 128:512], rwm[:, d, 128:512], ALU.mult, [pst.b, rwm.b], [AB3[b].b])
                                    kb.mm(pA, pA[:, u * 128:(u + 1) * 128], KRz[d][hh][:, c, 0, :], BK[d][:, c, 0, :], True, True,
                                          [BK[d].b, KRz[d][hh].b])
                                    kb.tt(XX[b][0][:, u, :], pA[:, u * 128:(u + 1) * 128], rwmA[:, d, :], ALU.mult, [pA.b, rwmA.b], [XX[b][0].b])
                                for u in range(4):
                                    kb.tt(MTs[b][:, u, :], XXT[b][0][:, u, :], identb[:], ALU.add, [XXT[b][0].b, identb.b], [MTs[b].b])
                            cur = 0
                            for r_ in range(6):
                                p1s, p2s = [], []
                                for b in range(len(batches)):
                                    X, XT = XX[b][cur], XXT[b][cur]
                                    p1_ = kb.ps()
                                    for u in range(4):
                                        kb.mm(p1_, p1_[:, u * 128:(u + 1) * 128], XT[:, u, :], X[:, u, :], True, True, [XT.b, X.b])
                                    p1s.append(p1_)
                                    kb.copy(XX[b][1 - cur][:], u3(p1_[:, :], 4, 128), [p1_.b], [XX[b][1 - cur].b], eng=("act" if b % 2 else "dve"))
                                    if r_ < 5:
                                        p2_ = kb.ps()
                                        for u in range(4):
                                            kb.mm(p2_, p2_[:, u * 128:(u + 1) * 128], X[:, u, :], XT[:, u, :], True, True, [XT.b, X.b])
                                        kb.copy(XXT[b][1 - cur][:], u3(p2_[:, :], 4, 128), [p2_.b], [XXT[b][1 - cur].b], eng="act")
                                for b in range(len(batches)):
                                    X2 = XX[b][1 - cur]
                                    p3_ = kb.ps()
                                    for u in range(4):
                                        kb.mm(p3_, p3_[:, u * 128:(u + 1) * 128], X2[:, u, :], MTs[b][:, u, :], True, True, [X2.b, MTs[b].b])
                                    kb.tt(MTs[b][:], MTs[b][:], u3(p3_[:, :], 4, 128), ALU.add, [MTs[b].b, p3_.b], [MTs[b].b])
                                cur = 1 - cur
                            for i in steps:
                                for s_ in range(nseq):
                                    for d in range(2):
                                        c = s_ * ncs + (i if d == 0 else ncs - 1 - i)
                                        kb.ts(SH[:, s_, d, :], SS[:, s_, d, :], etot[:, d, c:c + 1], ALU.mult, [SS.b, etot.b], [SH.b])
                                kb.copy(SHb[:], SH[:], [SH.b], [SHb.b], eng="act")
                                bl = [(b, ub) for b, (ii, ub) in enumerate(batches) if ii == i]
                                pRs, pUs = {}, {}
                                for b, ub in bl:
                                    pR = kb.ps()
                                    for u, (hh, d, c, s_) in enumerate(ub):
                                        pb = 64 * hh
                                        kb.mm(pR, pR[:, u * 64:(u + 1) * 64], KRz[d][hh][:, c, 0, :], SHb[:, s_, d, :], True, False,
                                              [KRz[d][hh].b, SHb.b], inc=False)
                                        kb.mm(pR, pR[:, u * 64:(u + 1) * 64], AB3[b][:, u, 128:256], VT[:, c, pb:pb + 64], False, True,
                                              [AB3[b].b, VT.b], inc=True)
                                    pRs[b] = pR
                                RHb = {}
                                for k_, (b, ub) in enumerate(bl):
                                    RHt = RHs[k_]
                                    kb.act(RHt[:], u3(pRs[b][:, 0:256], 4, 64), AF.Identity, [pRs[b].b], [RHt.b], scale=-1.0)
                                    RHb[b] = RHt
                                for b, ub in bl:
                                    pU = kb.ps()
                                    for u in range(4):
                                        kb.mm(pU, pU[:, u * 64:(u + 1) * 64], MTs[b][:, u, :], RHb[b][:, u, :], True, True, [MTs[b].b, RHb[b].b])
                                    pUs[b] = pU
                                USb = {}
                                for k_, (b, ub) in enumerate(bl):
                                    USt = USs[k_]
                                    kb.copy(USt[:], u3(pUs[b][:, 0:256], 4, 64), [pUs[b].b], [USt.b], eng="act")
                                    USb[b] = USt
                                for b, ub in bl:
                                    pY = kb.ps()
                                    pS = kb.ps()
                                    for u, (hh, d, c, s_) in enumerate(ub):
                                        pb = 64 * hh
                                        q = u // 2
                                        yo = pY[pb:pb + 64, q * 128:(q + 1) * 128]
                                        kb.mm(pY, yo, SHb[:, s_, d, :], KRz[d][hh][:, c, 1, :], True, False, [SHb.b, KRz[d][hh].b], inc=False, tp=(0, pb))
                                        kb.mm(pY, yo, USb[b][:, u, :], AB3[b][:, u, 0:128], False, False, [USb[b].b, AB3[b].b], inc=False, tp=(0, pb))
                                        kb.mm(pY, yo, VT[:, c, pb:pb + 64], AB3[b][:, u, 256:384], False, True, [VT.b, AB3[b].b], inc=True, tp=(0, pb))
                                        so = pS[pb:pb + 64, q * 64:(q + 1) * 64]
                                        kb.mm(pS, so, BKT[d][:, c, 0, pb:pb + 64], USb[b][:, u, :], True, False, [BKT[d].b, USb[b].b], inc=False, tp=(0, pb))
                                        kb.mm(pS, so, BKT[d][:, c, 1, pb:pb + 64], VT[:, c, pb:pb + 64], False, True, [BKT[d].b, VT.b], inc=True, tp=(0, pb))
                                    for q in range(2):
                                        hh, d, c, s_ = ub[2 * q]
                                        kb.tt(yacc[:, c * 128:(c + 1) * 128], yacc[:, c * 128:(c + 1) * 128], pY[:, q * 128:(q + 1) * 128], ALU.add,
                                              [yacc.b, pY.b], [yacc.b])
                                        kb.tt(SS[:, s_, d, :], SH[:, s_, d, :], pS[:, q * 64:(q + 1) * 64], ALU.add, [SH.b, pS.b], [SS.b])
                        if g == "P":
                            for s_ in range(nseq):
                                for d in range(2):
                                    for hh in range(2):
                                        kb.dma("sp", newst[l, s_, d, 2 * j + hh], SS[64 * hh:64 * hh + 64, s_, d, :], reads=[SS.b])
                    with kb.phase() as pe:
                        dev = pe.sb("dev", [128, BW])
                        sq2 = pe.sb("sq2", [128, BW])
                        for bi in range(NB):
                            bs = slice(bi * BW, (bi + 1) * BW)
                            pst = kb.ps()
                            kb.mm(pst, pst[:, 0:BW], bones[:], yacc[:, bs], True, True, [bones.b, yacc.b])
                            kb.stt(dev[:], pst[:, 0:BW], -1.0 / 64, yacc[:, bs], ALU.mult, ALU.add, [pst.b, yacc.b], [dev.b])
                            kb.tt(sq2[:], dev[:], dev[:], ALU.mult, [dev.b], [sq2.b])
                            pst = kb.ps()
                            kb.mm(pst, pst[:, 0:BW], bones[:], sq2[:], True, True, [bones.b, sq2.b])
                            kb.act(sq2[:], pst[:, 0:BW], AF.Ln, [pst.b, epsgn.b], [sq2.b], bias=epsgn[:], scale=1.0 / 64)
                            kb.act(sq2[:], sq2[:], AF.Exp, [sq2.b], [sq2.b], scale=-0.5)
                            kb.tt(dev[:], dev[:], sq2[:], ALU.mult, [dev.b, sq2.b], [dev.b])
                            kb.ts(dev[:], dev[:], vec[:, 3, j:j + 1], ALU.mult, [dev.b, vec.b], [dev.b], s2=vec[:, 4, j:j + 1], op1=ALU.add)
                            kb.tt(dev[:], dev[:], bonus[:, bs], ALU.add, [dev.b, bonus.b], [dev.b])
                            pst = kb.ps()
                            for k2 in range(2):
                                kb.mm(pst, pst[:, 0:BW], ups[:, 2, k2, :], gdn[:, k2, bs], k2 == 0, k2 == 1, [ups.b, gdn.b])
                            kb.tt(rwout[:, j, bs], dev[:], pst[:, 0:BW], ALU.mult, [dev.b, pst.b], [rwout.b])

    def hyena_mixer(phm, g, l, x, mix, T_, L, nseq, gi):
        NT = L // 128
        hc = hyc[g]
        TW = min(L, 512)
        with kb.phase() as ph:
            x0c = ph.sb("x0c", [128, 4, T_], BF16)
            zf = ph.sb("zf", [128, 4, T_], BF16)
            ztm = ph.sb("ztm", [128, T_ // 128, 512], BF16)
            shw = ph.sb("shw", [128, 12, 3])
            skp = ph.sb("skp", [128, 4])
            kb.dma("sp", shw[:], hy_short[l], writes=[shw.b])
            kb.dma("sp", skp[:], hy_skip[l], writes=[skp.b])
            with kb.phase() as p1:
                h = make_h(p1, x, T_, l, gi, 0, "h")
                raw = [p1.sb(f"raw{i}", [128, T_]) for i in range(3)]
                cv = [p1.sb(f"cv{i}", [128, T_]) for i in range(3)]
                zt32 = p1.sb("zt32", [128, T_])
                tiles = []
                for jc in range(4):
                    for a in range(3):
                        tiles.append((C_HY + a * 512 + jc * 128, 128))
                NTBp = max(1, T_ // TB)

                def ev(ci, tb, pst, w, sl):
                    a, jc = ci % 3, ci // 3
                    kb.copy(raw[a][:, sl], pst[:, 0:sl.stop - sl.start], [pst.b], [raw[a].b])
                    if tb != NTBp - 1:
                        return
                    wi = a * 4 + jc
                    for s_ in range(nseq):
                        a0, b0 = s_ * L, (s_ + 1) * L
                        kb.ts(cv[a][:, a0:b0], raw[a][:, a0:b0], shw[:, wi, 1:2], ALU.mult, [raw[a].b, shw.b], [cv[a].b])
                        kb.stt(cv[a][:, a0 + 1:b0], raw[a][:, a0:b0 - 1], shw[:, wi, 0:1], cv[a][:, a0 + 1:b0], ALU.mult, ALU.add,
                               [raw[a].b, shw.b, cv[a].b], [cv[a].b])
                        kb.stt(cv[a][:, a0:b0 - 1], raw[a][:, a0 + 1:b0], shw[:, wi, 2:3], cv[a][:, a0:b0 - 1], ALU.mult, ALU.add,
                               [raw[a].b, shw.b, cv[a].b], [cv[a].b])
                    if a != 2:
                        return
                    kb.copy(x0c[:, jc, :], cv[0][:], [cv[0].b], [x0c.b])
                    kb.tt(zt32[:], cv[1][:], cv[2][:], ALU.mult, [cv[1].b, cv[2].b], [zt32.b])
                    kb.copy(zf[:, jc, :], zt32[:], [zt32.b], [zf.b])
                    for tt_ in range(T_ // 128):
                        pt_ = kb.ps()
                        kb.tr(pt_, pt_[:, 0:128], zt32[:, tt_ * 128:(tt_ + 1) * 128], ident[:], [zt32.b, ident.b])
                        kb.copy(ztm[:, tt_, jc * 128:(jc + 1) * 128], pt_[:, 0:128], [pt_.b], [ztm.b])
                proj(p1, w_in[l], tiles, lambda kc, sl: (h[:, kc, sl], [h.b]), KC, T_, ev, "why")
            HP = ph.sb("HP", [128, NT, 512], BF16)
            HQ = ph.sb("HQ", [128, NT, 512], BF16)
            YP = ph.sb("YP", [128, nseq * NT, 512], BF16)
            YQ = ph.sb("YQ", [128, nseq * NT, 512], BF16)
            with kb.phase() as pa0:
              filt = pa0.sb("filt", [128, NT, 1024], BF16)
              with kb.phase() as pa:
                ft = pa.sb("ft", [128, L])
                f1 = pa.sb("f1", [128, 64])
                f2 = pa.sb("f2", [128, 64])
                f3 = pa.sb("f3", [128, 1024])
                b12 = pa.sb("b12", [64, 2])
                h1 = pa.sb("h1", [128, L])
                h2 = pa.sb("h2f", [128, L])
                vb = pa.sb("vb", [64, TW])
                nn = pa.sb("nn", [64, TW])
                dec = pa.sb("dec", [128, 1024])
                ee = pa.sb("ee", [128, 1024])
                nt01 = pa.sb("nt01", [128, NT])
                for t_ in (ft, f1, f2, f3, h1, h2):
                    kb.memset(t_[:], 0.0, [t_.b])
                kb.dma("sp", ft[0:33, :], hc["featT"], writes=[ft.b])
                kb.dma("sp", f1[0:33, :], hy_f1[l], writes=[f1.b])
                kb.dma("sp", f2[0:64, :], hy_f2[l], writes=[f2.b])
                kb.dma("sp", f3[0:64, :], hy_f3[l], writes=[f3.b])
                kb.dma("sp", b12[:, 0:1], hy_b1[l], writes=[b12.b])
                kb.dma("sp", b12[:, 1:2], hy_b2[l], writes=[b12.b])
                kb.dma("sp", dec[:], hy_decay[l], writes=[dec.b])
                kb.dma("sp", nt01[:], hc["t01"], writes=[nt01.b])
                kb.stt(ee[:], dec[:], -1.0, dec[:], ALU.mult, ALU.max, [dec.b], [ee.b])
                kb.copy(dec[:], ee[:], [ee.b], [dec.b])

                def sin_layer(wt, src, dst, bcol):
                    for b_ in range(L // TW):
                        sl = slice(b_ * TW, (b_ + 1) * TW)
                        pst = kb.ps()
                        kb.mm(pst, pst[0:64, 0:TW], wt[:, 0:64], src[:, sl], True, True, [wt.b, src.b])
                        kb.ts(vb[:], pst[0:64, 0:TW], b12[:, bcol:bcol + 1], ALU.add, [pst.b, b12.b], [vb.b])
                        kb.ts(nn[:], vb[:], 1.0 / (2 * math.pi), ALU.mult, [vb.b], [nn.b], s2=MAGIC, op1=ALU.add)
                        kb.ts(nn[:], nn[:], -MAGIC, ALU.add, [nn.b], [nn.b])
                        kb.stt(vb[:], nn[:], -2 * math.pi, vb[:], ALU.mult, ALU.add, [nn.b, vb.b], [vb.b])
                        kb.act(dst[0:64, sl], vb[:], AF.Sin, [vb.b], [dst.b])
                sin_layer(f1, ft, h1, 0)
                sin_layer(f2, h1, h2, 1)
                for j in range(NT):
                    kb.act(ee[:], dec[:], AF.Exp, [dec.b, nt01.b], [ee.b], scale=nt01[:, j:j + 1])
                    for hf in range(2):
                        pst = kb.ps()
                        kb.mm(pst, pst[:, :], h2[:, j * 128:(j + 1) * 128], f3[:, hf * 512:(hf + 1) * 512], True, True, [h2.b, f3.b])
                        kb.stt(filt[:, j, hf * 512:(hf + 1) * 512], ee[:, hf * 512:(hf + 1) * 512], 0.05, pst[:, :], ALU.add, ALU.mult,
                               [ee.b, pst.b], [filt.b])
              with kb.phase() as pa:
                ring = WRing(kb, pa, "hm", 2, [128, 4, NT, 128])
                mats = [hc["Ch"], hc["Cb"], hc["Sh"], hc["Sb"]]
                mv = [m_.rearrange("(j p) f -> p j f", p=128) for m_ in mats]
                for i in range(NT):
                    slot = ring.slots[i % 2]
                    for a in range(4):
                        kb.dma("sp", slot[:, a, :, :], mv[a][:, :, i * 128:(i + 1) * 128], writes=[slot.b])
                    for q_, dstH in ((0, HP), (1, HQ)):
                        pst = kb.ps()
                        for j in range(NT):
                            kb.mm(pst, pst[:, :], slot[:, 2 * q_, j, :], filt[:, j, 0:512], j == 0, False, [slot.b, filt.b])
                            kb.mm(pst, pst[:, :], slot[:, 2 * q_ + 1, j, :], filt[:, j, 512:1024], False, j == NT - 1, [slot.b, filt.b])
                        kb.copy(dstH[:, i, :], pst[:, :], [pst.b], [dstH.b])
            with kb.phase() as pf:
                ring = WRing(kb, pf, "fm", 2, [128, 2, NT, 128])
                mv = [m_.rearrange("(j p) f -> p j f", p=128) for m_ in (hc["C"], hc["S"])]
                tm = pf.sb("tm", [128, 4, 512])
                for i in range(NT):
                    slot = ring.slots[i % 2]
                    for a in range(2):
                        kb.dma("sp", slot[:, a, :, :], mv[a][:, :, i * 128:(i + 1) * 128], writes=[slot.b])
                    for s_ in range(nseq):
                        pp = kb.ps()
                        pq = kb.ps()
                        for a, pst in ((0, pp), (1, pq)):
                            for j in range(NT):
                                kb.mm(pst, pst[:, :], slot[:, a, j, :], ztm[:, s_ * NT + j, :], j == 0, j == NT - 1, [slot.b, ztm.b])
                        ii = s_ * NT + i
                        rd = [pp.b, pq.b, HP.b, HQ.b]
                        kb.tt(tm[:, 0, :], pp[:, :], HP[:, i, :], ALU.mult, rd, [tm.b])
                        kb.tt(tm[:, 1, :], pq[:, :], HQ[:, i, :], ALU.mult, rd, [tm.b])
                        kb.tt(tm[:, 2, :], pp[:, :], HQ[:, i, :], ALU.mult, rd, [tm.b])
                        kb.tt(tm[:, 3, :], pq[:, :], HP[:, i, :], ALU.mult, rd, [tm.b])
                        kb.tt(YP[:, ii, :], tm[:, 0, :], tm[:, 1, :], ALU.subtract, [tm.b], [YP.b])
                        kb.tt(YQ[:, ii, :], tm[:, 2, :], tm[:, 3, :], ALU.add, [tm.b], [YQ.b])
                        if i == 0:
                            kb.copy(YP[0:1, ii, :], tm[0:1, 0, :], [tm.b], [YP.b])
                            kb.copy(YQ[0:1, ii, :], tm[0:1, 1, :], [tm.b], [YQ.b])
            with kb.phase() as pi_:
                Cm = pi_.sb("Cm", [128, NT, L], BF16)
                Sm = pi_.sb("Sm", [128, NT, L], BF16)
                kb.dma("sp", Cm[:], hc["C"].rearrange("(i p) t -> p i t", p=128), writes=[Cm.b])
                kb.dma("sp", Sm[:], hc["Si"].rearrange("(i p) t -> p i t", p=128), writes=[Sm.b])
                t32 = pi_.sb("t32", [128, 2, TW])
                tbf = [Buf("a"), Buf("b")]
                n_ = 0
                for s_ in range(nseq):
                    for jc in range(4):
                        for tbk in range(L // TW):
                            pst = kb.ps()
                            for i in range(NT):
                                ii = s_ * NT + i
                                kb.mm(pst, pst[:, 0:TW], YP[:, ii, jc * 128:(jc + 1) * 128], Cm[:, i, tbk * TW:(tbk + 1) * TW], i == 0, False,
                                      [YP.b, Cm.b])
                                kb.mm(pst, pst[:, 0:TW], YQ[:, ii, jc * 128:(jc + 1) * 128], Sm[:, i, tbk * TW:(tbk + 1) * TW], False, i == NT - 1,
                                      [YQ.b, Sm.b])
                            tsl = slice(s_ * L + tbk * TW, s_ * L + (tbk + 1) * TW)
                            kb.stt(t32[:, n_ % 2, :], zf[:, jc, tsl], skp[:, jc:jc + 1], pst[:, 0:TW], ALU.mult, ALU.add,
                                   [zf.b, skp.b, pst.b], [tbf[n_ % 2]])
                            kb.tt(mix[:, 6 + jc, tsl], t32[:, n_ % 2, :], x0c[:, jc, tsl], ALU.mult, [tbf[n_ % 2], x0c.b], [mix.b])
                            n_ += 1

    def att_mixer(phm, g, l, x, mix, T_, L, nseq, gi):
        with kb.phase() as ph:
            qT = ph.sb("qT", [128, 12, T_], BF16)
            kT = ph.sb("kT", [128, 4, T_], BF16)
            kb.memset(qT[:], 0.0, [qT.b])
            kb.memset(kT[:], 0.0, [kT.b])
            vtm = ph.sb("vtm", [128, T_ // 128, 256], BF16)
            esink = T(esink_all.t, "esink")
            esink.b = esink_all.b
            esink_l = l
            if cfg.get("a_stop", 9) <= 1:
                return
            with kb.phase() as p1:
                h = make_h(p1, x, T_, l, gi, 0, "a")
                if cfg.get("a_stop", 9) <= 2:
                    return
                xk = lambda kc, sl: (h[:, kc, sl], [h.b])
                tiles = [(C_Q + 64 * i, 64) for i in range(12)] + [(C_K + 64 * i, 64) for i in range(4)]
                if g == "P":
                    kst = p1.sb("kst", [64, 4, T_])

                    def evqk(ci, tb, pst, w, sl):
                        ce = "dve"
                        if ci < 12:
                            kb.copy(qT[0:64, ci, sl], pst[0:64, :], [pst.b], [qT.b], eng=ce)
                        else:
                            kb.copy(kT[0:64, ci - 12, sl], pst[0:64, :], [pst.b], [kT.b], eng=ce)
                            kb.copy(kst[:, ci - 12, sl], pst[0:64, :], [pst.b], [kst.b])
                    proj(p1, w_in[l], tiles, xk, KC, T_, evqk, "wqk")
                    if not cfg.get("no_newk", 0):
                        kb.dma("sp", newk[l].rearrange("h d t -> d h t"), kst[:], reads=[kst.b])
                else:
                    rc = p1.sb("rc", [64, T_])
                    rs = p1.sb("rs", [64, T_])
                    kb.dma("sp", rc[:], rope_c, writes=[rc.b])
                    kb.dma("sp", rs[:], rope_s, writes=[rs.b])
                    rt = p1.sb("rt", [64, 2, TB])
                    tiles2 = [((C_Q + 64 * i, 64 * i), 64) for i in range(12)] + [((C_K + 64 * i, 768 + 64 * i), 64) for i in range(4)]

                    def evqk(ci, tb, psts, w, sl):
                        dst = qT[0:64, ci, sl] if ci < 12 else kT[0:64, ci - 12, sl]
                        db = qT.b if ci < 12 else kT.b
                        kb.tt(rt[:, 0, :], psts[0][0:64, :], rc[:, sl], ALU.mult, [psts[0].b, rc.b], [rt.b])
                        kb.tt(rt[:, 1, :], psts[1][0:64, :], rs[:, sl], ALU.mult, [psts[1].b, rs.b], [rt.b])
                        kb.tt(dst, rt[:, 0, :], rt[:, 1, :], ALU.add, [rt.b], [db])
                    proj(p1, [w_in[l], w_qkp[l]], tiles2, xk, KC, T_, evqk, "wqk", nslots=2)
                if cfg.get("a_stop", 9) <= 3:
                    return
                wv_ = p1.sb("wv", [128, KC, 256], BF16)
                kb.dma("pool", wv_[:], w_in[l].rearrange("(kc p) n -> p kc n", p=128)[:, :, C_V:C_V + 256], writes=[wv_.b])
                if g == "P":
                    vst = p1.sb("vst", [128, T_ // 128, 256])
                for tt_ in range(T_ // 128):
                    pst = kb.ps()
                    for kc in range(KC):
                        kb.mm(pst, pst[:, 0:256], h[:, kc, tt_ * 128:(tt_ + 1) * 128], wv_[:, kc, :], kc == 0, kc == KC - 1, [h.b, wv_.b])
                    kb.copy(vtm[:, tt_, :], pst[:, 0:256], [pst.b], [vtm.b], eng="act")
                    if g == "P":
                        kb.copy(vst[:, tt_, :], pst[:, 0:256], [pst.b, vtm.b], [vst.b])
                if g == "P":
                    kb.dma("sp", newv[l].rearrange("(a p) c -> p a c", p=128), vst[:], reads=[vst.b])
            if cfg.get("att_noscore", 0):
                return
            with kb.phase() as p2:
                NPT = 4
                pT = [p2.sb(f"pT{i}", [128, 512], BF16) for i in range(NPT)]
                pti = [0]
                rden = p2.sb("rden", [128, 512])
                if g == "S":
                    kcx = p2.sb("kcx", [128, 4, 512], BF16)
                    kb.memset(kcx[:], 0.0, [kcx.b])
                    vcx = p2.sb("vcx", [128, 4, 256], BF16)
                    wm = p2.sb("wm", [128, 6, 512], BF16)
                    kb.dma("pool", kcx[0:64], kctxT[l].rearrange("h d t -> d h t"), writes=[kcx.b])
                    kb.dma("pool", vcx[:], vctx[l].rearrange("(a p) c -> p a c", p=128), writes=[vcx.b])
                    kb.dma("sp", wm[:], wmask_d.rearrange("r p q -> p r q"), writes=[wm.b])
                QW = 512 if g == "S" else 256
                for qg in range(T_ // QW):
                    q0 = qg * QW
                    for hq in range(12):
                        kvh = hq // 3
                        pb = 64 * (hq % 2)
                        kts = []
                        if g == "S":
                            for j in range(4):
                                kts.append((kcx[:, kvh, j * 128:(j + 1) * 128], [kcx.b], vcx[:, j, kvh * 64:(kvh + 1) * 64], [vcx.b], None))
                            for jt in range(max(0, 4 * qg - 1), min(8, 4 * qg + 5)):
                                kts.append((kT[:, kvh, jt * 128:(jt + 1) * 128], [kT.b], vtm[:, jt, kvh * 64:(kvh + 1) * 64], [vtm.b],
                                            wm[:, jt - 4 * qg + 1, :]))
                        else:
                            for j in range(2):
                                jt = qg * 2 + j
                                kts.append((kT[:, kvh, jt * 128:(jt + 1) * 128], [kT.b], vtm[:, jt, kvh * 64:(kvh + 1) * 64], [vtm.b], None))
                        po = kb.psum[2 * (hq % 2)]
                        pd = kb.psum[2 * (hq % 2) + 1]
                        nk_ = len(kts)
                        for i, (kap, kbufs, vap, vbufs, mask) in enumerate(kts):
                            pss = kb.psum[4 + pti[0] % 4]
                            kb.mm(pss, pss[:, 0:QW], kap, qT[:, hq, q0:q0 + QW], True, True, kbufs + [qT.b])
                            pt = pT[pti[0] % NPT]
                            pti[0] += 1
                            kb.act(pt[:, 0:QW], pss[:, 0:QW], AF.Exp, [pss.b], [pt.b], scale=0.125)
                            if mask is not None:
                                kb.tt(pt[:, 0:QW], pt[:, 0:QW], mask, ALU.mult, [pt.b, wm.b], [pt.b])
                            kb.mm(po, po[pb:pb + 64, 0:QW], vap, pt[:, 0:QW], i == 0, i == nk_ - 1, vbufs + [pt.b], tp=(0, pb))
                            kb.mm(pd, pd[pb:pb + 64, 0:QW], ones_bf[:, 0:64], pt[:, 0:QW], i == 0, i == nk_ - 1, [ones_bf.b, pt.b], tp=(0, pb))
                        kb.ts(rden[pb:pb + 64, 0:QW], pd[pb:pb + 64, 0:QW], esink_all[pb:pb + 64, l, hq:hq + 1], ALU.add, [pd.b, esink.b], [rden.b])
                        kb.op("dve", lambda e: e.reciprocal(out=rden[pb:pb + 64, 0:QW], in_=rden[pb:pb + 64, 0:QW]), [rden.b], [rden.b])
                        kb.tt(mix[pb:pb + 64, hq // 2, q0:q0 + QW], po[pb:pb + 64, 0:QW], rden[pb:pb + 64, 0:QW], ALU.mult,
                              [po.b, rden.b], [mix.b])

    for g in cfg.get("groups", ("S", "P")):
        run_group(g)
    kb.barrier()
    return kb


def _fm(v, p=128):
    v = np.asarray(v)
    n = v.shape[-1] // p
    return np.ascontiguousarray(np.moveaxis(v.reshape(v.shape[:-1] + (n, p)), -1, 0))


def host_consts():
    c = {}
    c["c_ident"] = np.eye(128, dtype=np.float32)
    b = np.zeros((128, 128), np.float32)
    b[:64, :64] = 1
    b[64:, 64:] = 1
    c["c_bones"] = b
    t = np.arange(1024)
    rows, cols = (t // 64).astype(np.float64), (t % 64).astype(np.float64)
    freqs = 10000.0 ** (-np.arange(0, 32, 2, dtype=np.float64) / 32)
    rc = np.zeros((64, 1024)); rs = np.zeros((64, 1024))
    for d in range(64):
        pos = rows if d < 32 else cols
        ang = pos * freqs[d % 16]
        rc[d] = np.cos(ang)
        rs[d] = np.sin(ang) * (-1.0 if (d % 32) < 16 else 1.0)
    c["c_rope_cos"] = rc.astype(np.float32)
    c["c_rope_sin"] = rs.astype(np.float32)
    wm = np.zeros((6, 128, 512), np.float32)
    kk_, qq_ = np.meshgrid(np.arange(128), np.arange(512), indexing="ij")
    for r in range(6):
        rel = r - 1
        wm[r] = (np.abs(qq_ - rel * 128 - kk_) <= 128)
    c["c_wmask"] = wm.astype(ml_dtypes.bfloat16)
    sm = np.ones((128, 1024), np.float32)
    sm[:, ::128] = 0.0
    c["c_scanmask"] = sm
    ss_, tt_ = np.meshgrid(np.arange(128), np.arange(128), indexing="ij")
    rwm = np.zeros((2, 128, 512), np.float32)
    rwa = np.zeros((2, 128, 128), np.float32)
    for d in range(2):
        strict = (ss_ < tt_) if d == 0 else (ss_ > tt_)
        incl = (ss_ <= tt_) if d == 0 else (ss_ >= tt_)
        rwm[d] = np.concatenate([-1.0 * strict, 1.0 * incl, 1.0 * strict, 1.0 * incl], axis=1)
        rwa[d] = -1.0 * strict.T
    c["c_rwmask"] = rwm.astype(ml_dtypes.bfloat16)
    c["c_rwmaskA"] = rwa.astype(ml_dtypes.bfloat16)
    c.update(hy_consts(1024, "S"))
    c.update(hy_consts(256, "P"))
    return c


def hy_consts(L, g):
    c = {}
    bf = ml_dtypes.bfloat16
    t = np.arange(L, dtype=np.float64)
    t01 = t / max(L - 1, 1)
    bands = np.linspace(1e-4, 15, 16)
    ang = (2.0 * math.pi / L) * t[:, None] * bands[None, :]
    feat = np.concatenate([t01[:, None], np.cos(ang), -np.sin(ang)], axis=-1)
    c[f"c_featT_{g}"] = np.ascontiguousarray(feat.T).astype(np.float32)
    c[f"c_t01_{g}"] = np.ascontiguousarray((-t01).reshape(L // 128, 128).T).astype(np.float32)
    f = np.arange(L, dtype=np.float64)
    A = math.pi * np.outer(t, f) / L
    sgn = np.where(np.arange(L) % 2 == 0, 1.0, -1.0)
    C = np.cos(A)
    Sf = np.sin(A)
    Sf[:, 0] = sgn
    sc = np.full(L, 1.0 / L)
    sc[0] = 0.5 / L
    Ch = C * sc[None, :]
    Sh = Sf * sc[None, :]
    Cb = Ch.copy()
    Cb[0, :] = 0
    Sb = -np.sin(A) * sc[None, :]
    Sb[:, 0] = sgn * sc[0]
    Sb[0, :] = 0
    c[f"c_dftC_{g}"] = C.astype(bf)
    c[f"c_dftS_{g}"] = Sf.astype(bf)
    c[f"c_dftSi_{g}"] = np.ascontiguousarray(Sf.T).astype(bf)
    c[f"c_dftCh_{g}"] = Ch.astype(bf)
    c[f"c_dftSh_{g}"] = Sh.astype(bf)
    c[f"c_dftCb_{g}"] = Cb.astype(bf)
    c[f"c_dftSb_{g}"] = Sb.astype(bf)
    return c


def _perm64():
    p = np.zeros(64, np.int64)
    for d in range(64):
        p[d] = d + 16 if (d % 32) < 16 else d - 16
    return p


_CACHE = {}


def kernel(**inp):
    cfg = inp.pop("_cfg", {})
    key = repr(sorted(cfg.items()))
    if key not in _CACHE:
        _CACHE[key] = build(cfg)
    kb = _CACHE[key]
    f32 = np.float32
    A = {k: np.asarray(v) for k, v in inp.items()}
    shared = dict(host_consts())
    for k in ("ada_w", "w_in", "w_out", "mlp_w1", "mlp_w2", "hy_f1", "hy_f2", "hy_f3", "rw_w_up", "rw_a_up", "rw_g_up"):
        shared[k] = np.ascontiguousarray(A[k], dtype=f32)
    pm = _perm64()
    qk_idx = np.concatenate([h * 64 + pm for h in range(12)] + [768 + h * 64 + pm for h in range(4)])
    qk_idx = np.concatenate([qk_idx, np.arange(64)])
    shared["w_qkp"] = np.ascontiguousarray(A["w_in"][:, :, qk_idx], dtype=f32)
    shared["sink_bc"] = np.ascontiguousarray(np.broadcast_to(A["attn_sink"][:, None, :], (DEPTH, 128, 12)), dtype=f32)
    shared["rw_short_fm"] = np.ascontiguousarray(np.stack([_fm(A["rw_short_w"][l]).transpose(0, 2, 1) for l in range(DEPTH)]))
    shared["rw_w0_fm"] = np.ascontiguousarray(np.stack([_fm(A["rw_w0"][l]) for l in range(DEPTH)]))
    shared["rw_a0_fm"] = np.ascontiguousarray(np.stack([_fm(A["rw_a0"][l]) for l in range(DEPTH)]))
    shared["rw_vec_fm"] = np.ascontiguousarray(np.stack([_fm(np.stack([A["rw_k_k"][l], A["rw_k_a"][l], A["rw_r_k"][l].reshape(768),
                                                                        A["rw_gn_w"][l], A["rw_gn_b"][l]])) for l in range(DEPTH)]))
    shared["hy_short_fm"] = np.ascontiguousarray(np.stack([_fm(A["hy_short_w"][l]).transpose(0, 2, 1) for l in range(DEPTH)]))
    shared["hy_b1"] = np.ascontiguousarray(A["hy_b1"].reshape(DEPTH, 64, 1))
    shared["hy_b2"] = np.ascontiguousarray(A["hy_b2"].reshape(DEPTH, 64, 1))
    shared["hy_decay_bc"] = np.ascontiguousarray(np.broadcast_to(A["hy_decay"][:, None, :], (DEPTH, 128, 1024)), dtype=f32)
    shared["hy_skip_fm"] = np.ascontiguousarray(np.stack([_fm(A["hy_skip"][l, 0]) for l in range(DEPTH)]))
    shared["ada_b_fm"] = np.ascontiguousarray(np.stack([_fm(A["ada_b"][l]) for l in range(DEPTH)]))
    shared["norm_w_fm"] = np.ascontiguousarray(np.stack([_fm(A["norm_w"][l]) for l in range(DEPTH)]))
    in_maps = []
    for c in range(8):
        m = dict(shared)
        m["xT_s"] = np.ascontiguousarray(A["x_sample"][c].T)
        m["xT_p"] = np.ascontiguousarray(A["x_prompt"][2 * c:2 * c + 2].reshape(512, D).T)
        cond = np.stack([A["c"][c], A["c_ctx"]], axis=-1)
        m["kctxT"] = np.ascontiguousarray(A["cache_k"][c].transpose(0, 2, 3, 1))
        m["vctx"] = np.ascontiguousarray(A["cache_v"][c].reshape(DEPTH, 512, 256))
        m["st0T"] = np.ascontiguousarray(A["state_rwkv"][c].transpose(0, 1, 2, 4, 3))
        m["condT"] = np.ascontiguousarray(cond.reshape(KC, 128, 2).transpose(1, 0, 2))
        in_maps.append(m)
    names = set(kb.dram_in.keys())
    in_maps = [{k: v for k, v in m.items() if k in names} for m in in_maps]
    missing = names - set(in_maps[0].keys())
    for k in missing:
        ap = kb.dram_in[k]
        dt = ml_dtypes.bfloat16 if ap.dtype == BF16 else f32
        for m in in_maps:
            m[k] = np.zeros(ap.shape, dt)
    ncores = cfg.get("ncores", 8)
    if cfg.get("trace", 0):
        res = run_bass_kernel_spmd(kb.nc, in_maps[:ncores], core_ids=list(range(ncores)), trace=True)
        print("EXEC_NS", res.exec_time_ns, "n_ins", kb.n_ins, "cnt", kb.cnt)
    else:
        res = run_bass_kernel_spmd(kb.nc, in_maps[:ncores], core_ids=list(range(ncores)))
    R = list(res.results)
    while len(R) < 8:
        R.append(R[0])
    y_s = np.stack([R[c]["yT_s"].T for c in range(8)])
    y_p = np.concatenate([R[c]["yT_p"].T.reshape(2, 256, D) for c in range(8)])
    nk = np.zeros((16, DEPTH, 256, 4, 64), f32)
    nv = np.zeros((16, DEPTH, 256, 4, 64), f32)
    ns = np.zeros((16, DEPTH, 2, 12, 64, 64), f32)
    for c in range(8):
        kT = R[c]["newkT"]
        v_ = R[c]["newv"]
        sT = R[c]["newstT"]
        for s in range(2):
            nk[2 * c + s] = kT[:, :, :, s * 256:(s + 1) * 256].transpose(0, 3, 1, 2)
            nv[2 * c + s] = v_[:, s * 256:(s + 1) * 256, :].reshape(DEPTH, 256, 4, 64)
            ns[2 * c + s] = sT[:, s].transpose(0, 1, 2, 4, 3)
    return (y_p.astype(f32), y_s.astype(f32), nk, nv, ns)
```

```python
import math
from contextlib import ExitStack, contextmanager
import numpy as np
import ml_dtypes
import concourse.bass as bass
import concourse.mybir as mybir
from concourse.bass_utils import run_bass_kernel_spmd

F32 = mybir.dt.float32
BF16 = mybir.dt.bfloat16
AF = mybir.ActivationFunctionType
ALU = mybir.AluOpType

D = 2048
KC = 16
DEPTH = 2
TB = 512
CH = 128
IN_COLS = 5760
C_Q, C_K, C_V, C_HY, C_RKV, C_WDN, C_ADN, C_GDN = 0, 768, 1024, 1280, 2816, 5120, 5312, 5504
MAGIC = 12582912.0
SAME_SYNC = True


_KB = [None]


class Buf:
    __slots__ = ("name", "w", "r", "dsem", "dcnt")

    def __init__(self, name):
        self.name = name
        self.w = None
        self.r = {}
        self.dsem = None
        self.dcnt = 0
        kb = _KB[0]
        if kb is not None:
            if kb.freed:
                self.r = {"f%d" % k: t for k, t in kb.freed.items()}
            if kb.phase_all:
                kb.phase_all[-1].append(self)


class T:
    def __init__(self, t, name):
        self.t = t
        self.b = Buf(name)

    def __getitem__(self, key):
        return self.t[key]


class KB:
    def __init__(self):
        self.nc = bass.Bass("TRN2", target_bir_lowering=False)
        nc = self.nc
        self.E = {"pe": nc.tensor, "act": nc.scalar, "dve": nc.vector, "pool": nc.gpsimd, "sp": nc.sync}
        self.sem = {e: nc.alloc_semaphore("prog_" + e) for e in self.E}
        self.cnt = {e: 0 for e in self.E}
        self.waited = {e: {} for e in self.E}
        self.semkey = {}
        self.dma_latest = {}
        self.n_ins = 0
        self.dram_in = {}
        self.dram_out = {}
        self.psum = []
        self.ps_i = 0
        self.uid = 0
        self.free_dsems = []
        self.live = []
        self.freed = {}
        self.phase_all = []
        self.retired = []
        _KB[0] = self
        self.keep = []
        self.dpool = {"sp": [], "pool": []}
        self.phase_bufs = []

    def _key(self, sem):
        k = self.semkey.get(id(sem))
        if k is None:
            k = len(self.semkey) + 1
            self.semkey[id(sem)] = k
            self.keep.append(sem)
        return k

    def _wait(self, eng, deps):
        for (sem, val, owner) in deps:
            if owner == eng:
                if eng == "pe" or not SAME_SYNC:
                    continue
                if val > self.cnt[eng]:
                    continue
            elif owner in self.cnt and val > self.cnt[owner]:
                raise RuntimeError(f"wait on pending count: {eng} waits {owner} {val} > {self.cnt[owner]}")
            k = self._key(sem)
            if self.waited[eng].get(k, 0) >= val:
                continue
            self.E[eng].wait_ge(sem, val)
            self.waited[eng][k] = val

    def _deps(self, reads, writes):
        deps = []
        for b in reads:
            if b.w is not None:
                deps.append(b.w)
        for b in writes:
            if b.w is not None:
                deps.append(b.w)
            deps.extend(b.r.values())
        return deps

    def op(self, eng, fn, reads=(), writes=(), inc=True):
        self._wait(eng, self._deps(reads, writes))
        ins = fn(self.E[eng])
        self.n_ins += 1
        if inc:
            ins.then_inc(self.sem[eng], 1)
            self.cnt[eng] += 1
            tag = (self.sem[eng], self.cnt[eng], eng)
        else:
            tag = (self.sem[eng], self.cnt[eng] + 1, eng)
        for b in reads:
            b.r[eng] = tag
        for b in writes:
            b.w = tag
            b.r = {}
        return ins

    def dma(self, q, out, in_, reads=(), writes=()):
        self._wait(q, self._deps(reads, writes))
        owner = (list(writes) + list(reads))[0]
        if owner.dsem is None:
            if self.dpool[q]:
                owner.dsem = self.dpool[q].pop()
            else:
                owner.dsem = [self.nc.alloc_semaphore(f"d{self.uid}"), 0, q]
                self.uid += 1
            if self.phase_bufs:
                self.phase_bufs[-1].append(owner)
        assert owner.dsem[2] == q, "a Buf's DMA semaphore must stay on one queue type"
        ins = self.E[q].dma_start(out=out, in_=in_)
        ins.then_inc(owner.dsem[0], 16)
        owner.dsem[1] += 16
        self.n_ins += 1
        tag = (owner.dsem[0], owner.dsem[1], "dma")
        self.dma_latest[self._key(owner.dsem[0])] = tag
        for b in reads:
            b.r["dma%d" % id(owner)] = tag
        for b in writes:
            b.w = tag
            b.r = {}

    def barrier(self):
        tags = [(self.sem[f], self.cnt[f], f) for f in self.E if self.cnt[f] > 0]
        tags += list(self.dma_latest.values())
        for e in self.E:
            self._wait(e, [t for t in tags if t[2] != e])
        self.freed = {}
        for ds in self.retired:
            self.dpool[ds[2]].append(ds)
        self.retired = []

    def din(self, name, shape, dtype=F32):
        self.dram_in[name] = self.nc.dram_tensor(name, list(shape), dtype, kind="ExternalInput").ap()
        return self.dram_in[name]

    def dout(self, name, shape):
        self.dram_out[name] = self.nc.dram_tensor(name, list(shape), F32, kind="ExternalOutput").ap()
        return self.dram_out[name]

    def sb(self, name, shape, dtype=F32):
        self.uid += 1
        self.live.append((name, int(np.prod(shape[1:])) * (2 if dtype == BF16 else 4)))
        return T(self.nc.alloc_sbuf_tensor(f"{name}_{self.uid}", list(shape), dtype), name)

    @contextmanager
    def phase(self, barrier=False):
        st = ExitStack()
        self.phase_all.append([])
        kb = self

        class Ph:
            def sb(self_, name, shape, dtype=F32):
                kb.uid += 1
                try:
                    t = st.enter_context(kb.nc.sbuf_tensor(f"{name}_{kb.uid}", list(shape), dtype))
                except Exception:
                    print("LIVE:", [(n, b) for n, b in kb.live])
                    raise
                nb = int(np.prod(shape[1:])) * (2 if dtype == BF16 else 4)
                kb.live.append((name, nb))
                st.callback(lambda: kb.live.remove((name, nb)))
                return T(t, name)
        self.phase_bufs.append([])
        try:
            yield Ph()
        finally:
            allb = self.phase_all.pop()
            if barrier:
                self.barrier()
                for b in self.phase_bufs.pop():
                    self.dpool[b.dsem[2]].append(b.dsem)
                    b.dsem = None
            else:
                for b in allb:
                    for t in ([b.w] if b.w is not None else []) + list(b.r.values()):
                        if t[2] in self.cnt and t[1] > self.cnt[t[2]]:
                            t = (t[0], self.cnt[t[2]] + 1, t[2])
                        k = self._key(t[0])
                        if k not in self.freed or self.freed[k][1] < t[1]:
                            self.freed[k] = t
                for b in self.phase_bufs.pop():
                    self.retired.append(b.dsem)
                    b.dsem = None
            st.close()

    def init_psum(self):
        for i in range(8):
            self.psum.append(T(self.nc.alloc_psum_tensor(f"ps{i}", [128, 512], F32), f"ps{i}"))

    def ps(self):
        p = self.psum[self.ps_i % 8]
        self.ps_i += 1
        return p

    def mm(self, ps, out, lhsT, rhs, start, stop, reads, inc=None, tp=None):
        if inc is None:
            inc = stop
        kw = {}
        if tp is not None:
            kw["tile_position"] = tp
        return self.op("pe", lambda e: e.matmul(out, lhsT=lhsT, rhs=rhs, start=start, stop=stop, **kw),
                       reads=reads, writes=[ps.b], inc=inc)

    def tr(self, ps, out, in_, ident, reads):
        return self.op("pe", lambda e: e.transpose(out=out, in_=in_, identity=ident), reads=reads, writes=[ps.b])

    def act(self, out, in_, func, reads, writes, bias=None, scale=None, eng="act"):
        kw = {}
        if bias is not None:
            kw["bias"] = bias
        if scale is not None:
            kw["scale"] = scale
        return self.op("act", lambda e: e.activation(out=out, in_=in_, func=func, **kw), reads=reads, writes=writes)

    def tt(self, out, in0, in1, op, reads, writes, eng="dve"):
        return self.op(eng, lambda e: e.tensor_tensor(out=out, in0=in0, in1=in1, op=op), reads=reads, writes=writes)

    def ts(self, out, in0, s1, op0, reads, writes, s2=None, op1=None, eng="dve"):
        if op1 is None:
            return self.op(eng, lambda e: e.tensor_scalar(out=out, in0=in0, scalar1=s1, scalar2=None, op0=op0),
                           reads=reads, writes=writes)
        return self.op(eng, lambda e: e.tensor_scalar(out=out, in0=in0, scalar1=s1, scalar2=s2, op0=op0, op1=op1),
                       reads=reads, writes=writes)

    def stt(self, out, in0, scalar, in1, op0, op1, reads, writes):
        return self.op("dve", lambda e: e.scalar_tensor_tensor(out=out, in0=in0, scalar=scalar, in1=in1, op0=op0, op1=op1),
                       reads=reads, writes=writes)

    def copy(self, out, in_, reads, writes, eng="dve"):
        if eng == "act":
            return self.act(out, in_, AF.Identity, reads, writes)
        return self.op(eng, lambda e: e.tensor_copy(out=out, in_=in_), reads=reads, writes=writes)

    def memset(self, out, val, writes, eng="dve"):
        return self.op(eng, lambda e: e.memset(out, val), writes=writes)


class WRing:
    def __init__(self, kb, ph, name, nslots, shape):
        self.kb = kb
        self.slots = [ph.sb(f"{name}{i}", shape, BF16) for i in range(nslots)]
        self.i = 0

    def load(self, dram_view, view=None):
        s = self.slots[self.i % len(self.slots)]
        self.i += 1
        dst = s.t[:] if view is None else view(s)
        self.kb.dma("pool", dst, dram_view, writes=[s.b])
        return s


def stream(kb, ring, views, dstview=None):
    n = len(views)
    ns = len(ring.slots)
    loaded = []
    nxt = 0
    for i in range(n):
        while nxt < n and nxt < i + ns:
            loaded.append(ring.load(views[nxt], dstview))
            nxt += 1
        yield i, loaded[i]


def build(cfg):
    kb = KB()
    nc = kb.nc
    kb.init_psum()
    do_att, do_hy, do_rw = cfg.get("att", 1), cfg.get("hy", 1), cfg.get("rw", 1)

    xT = {"S": kb.din("xT_s", [D, 1024]), "P": kb.din("xT_p", [D, 512])}
    yT = {"S": kb.dout("yT_s", [D, 1024]), "P": kb.dout("yT_p", [D, 512])}
    condT = kb.din("condT", [128, KC, 2])
    ada_w = kb.din("ada_w", [DEPTH, D, 6 * D])
    ada_b = kb.din("ada_b_fm", [DEPTH, 128, 96])
    norm_w = kb.din("norm_w_fm", [DEPTH, 128, 4, KC])
    w_in = kb.din("w_in", [DEPTH, D, IN_COLS])
    w_qkp = kb.din("w_qkp", [DEPTH, D, 1088])
    w_out = kb.din("w_out", [DEPTH, D, D])
    mlp_w1 = kb.din("mlp_w1", [DEPTH, D, 4 * D])
    mlp_w2 = kb.din("mlp_w2", [DEPTH, 4 * D, D])
    consts = {}

    def cin(name, shape, dtype=F32):
        consts[name] = kb.din(name, shape, dtype)
        return consts[name]
    ident_d = cin("c_ident", [128, 128])
    bones_d = cin("c_bones", [128, 128])
    kctxT = kb.din("kctxT", [DEPTH, 4, 64, 512])
    vctx = kb.din("vctx", [DEPTH, 512, 256])
    sink_d = kb.din("sink_bc", [DEPTH, 128, 12])
    rope_c = cin("c_rope_cos", [64, 1024])
    rope_s = cin("c_rope_sin", [64, 1024])
    wmask_d = cin("c_wmask", [6, 128, 512], BF16)
    newk = kb.dout("newkT", [DEPTH, 4, 64, 512])
    newv = kb.dout("newv", [DEPTH, 512, 256])
    hy_short = kb.din("hy_short_fm", [DEPTH, 128, 12, 3])
    hy_f1 = kb.din("hy_f1", [DEPTH, 33, 64])
    hy_b1 = kb.din("hy_b1", [DEPTH, 64, 1])
    hy_f2 = kb.din("hy_f2", [DEPTH, 64, 64])
    hy_b2 = kb.din("hy_b2", [DEPTH, 64, 1])
    hy_f3 = kb.din("hy_f3", [DEPTH, 64, 1024])
    hy_decay = kb.din("hy_decay_bc", [DEPTH, 128, 1024])
    hy_skip = kb.din("hy_skip_fm", [DEPTH, 128, 4])
    hyc = {}
    for g, L in (("S", 1024), ("P", 256)):
        hyc[g] = dict(
            featT=cin(f"c_featT_{g}", [33, L]), t01=cin(f"c_t01_{g}", [128, L // 128]),
            C=cin(f"c_dftC_{g}", [L, L], BF16), S=cin(f"c_dftS_{g}", [L, L], BF16),
            Ch=cin(f"c_dftCh_{g}", [L, L], BF16), Sh=cin(f"c_dftSh_{g}", [L, L], BF16),
            Cb=cin(f"c_dftCb_{g}", [L, L], BF16), Sb=cin(f"c_dftSb_{g}", [L, L], BF16), Si=cin(f"c_dftSi_{g}", [L, L], BF16))
    rw_short = kb.din("rw_short_fm", [DEPTH, 128, 18, 3])
    rw_w0 = kb.din("rw_w0_fm", [DEPTH, 128, 2, 6])
    rw_a0 = kb.din("rw_a0_fm", [DEPTH, 128, 2, 6])
    rw_wup = kb.din("rw_w_up", [DEPTH, 2, 96, 768])
    rw_aup = kb.din("rw_a_up", [DEPTH, 2, 96, 768])
    rw_gup = kb.din("rw_g_up", [DEPTH, 256, 768])
    rw_vec = kb.din("rw_vec_fm", [DEPTH, 128, 5, 6])
    st0T = kb.din("st0T", [DEPTH, 2, 12, 64, 64])
    newst = kb.dout("newstT", [DEPTH, 2, 2, 12, 64, 64])
    scanmask_d = cin("c_scanmask", [128, 1024])
    rwmask_d = cin("c_rwmask", [2, 128, 512], BF16)
    rwmaskA_d = cin("c_rwmaskA", [2, 128, 128], BF16)

    ident = kb.sb("ident", [128, 128])
    bones = kb.sb("bones", [128, 128])
    ones_bf = kb.sb("ones_bf", [128, 128], BF16)
    mod = kb.sb("mod", [128, DEPTH, 2, 6, KC])
    nw = kb.sb("nw", [128, DEPTH, 4, KC])
    dmod = kb.sb("dmod", [128, DEPTH, 2, 6, KC])
    epsb = kb.sb("epsb", [128, 1])
    negpi = kb.sb("negpi", [128, 1])
    kb.dma("sp", ident[:], ident_d, writes=[ident.b])
    kb.dma("sp", bones[:], bones_d, writes=[bones.b])
    kb.dma("sp", nw[:], norm_w.rearrange("l p a c -> p l a c"), writes=[nw.b])
    esink_raw = kb.sb("esink_raw", [128, DEPTH, 12])
    esink_all = kb.sb("esink_all", [128, DEPTH, 12])
    kb.dma("sp", esink_raw[:], sink_d.rearrange("l p h -> p l h"), writes=[esink_raw.b])
    kb.act(esink_all[:], esink_raw[:], AF.Exp, [esink_raw.b], [esink_all.b])
    kb.memset(ones_bf[:], 1.0, [ones_bf.b])
    kb.memset(epsb[:], 1e-6, [epsb.b])
    kb.memset(negpi[:], -math.pi, [negpi.b])

    with kb.phase() as ph:
        cnd = ph.sb("cnd", [128, KC, 2])
        cndb = ph.sb("cndb", [128, KC, 2], BF16)
        adab = ph.sb("adab", [128, DEPTH, 96])
        kb.dma("sp", cnd[:], condT, writes=[cnd.b])
        kb.dma("sp", adab[:], ada_b.rearrange("l p j -> p l j"), writes=[adab.b])
        kb.act(cndb[:], cnd[:], AF.Silu, [cnd.b], [cndb.b])
        ring = WRing(kb, ph, "adaw", 3, [128, KC, 512])
        for l in range(DEPTH):
            wv = ada_w[l].rearrange("(kc p) n -> p kc n", p=128)
            views = [wv[:, :, i * 512:(i + 1) * 512] for i in range(24)]
            pst = kb.ps()
            for i, slot in stream(kb, ring, views):
                for j in range(4):
                    jj = i * 4 + j
                    for kc in range(KC):
                        kb.mm(pst, pst[:, 2 * jj:2 * jj + 2], slot[:, kc, j * 128:(j + 1) * 128], cndb[:, kc, :],
                              kc == 0, kc == KC - 1, [slot.b, cndb.b])
            for g in range(2):
                kb.tt(mod[:, l, g, :, :].rearrange("p a c -> p (a c)"),
                      pst[:, 0:192].rearrange("p (j g) -> p j g", g=2)[:, :, g], adab[:, l, :], ALU.add,
                      [pst.b, adab.b], [mod.b])
        for l in range(DEPTH):
            for g in range(2):
                m = lambda i: mod[:, l, g, i, :]
                dm = lambda i: dmod[:, l, g, i, :]
                kb.stt(dm(0), m(1), 1.0, nw[:, l, 0, :], ALU.add, ALU.mult, [mod.b, nw.b], [dmod.b])
                kb.copy(dm(1), m(0), [mod.b], [dmod.b])
                kb.tt(dm(2), m(2), nw[:, l, 1, :], ALU.mult, [mod.b, nw.b], [dmod.b])
                kb.stt(dm(3), m(4), 1.0, nw[:, l, 2, :], ALU.add, ALU.mult, [mod.b, nw.b], [dmod.b])
                kb.copy(dm(4), m(3), [mod.b], [dmod.b])
                kb.tt(dm(5), m(5), nw[:, l, 3, :], ALU.mult, [mod.b, nw.b], [dmod.b])

    def rstd_from(ph, src_chunks, T_, tag, out=None):
        rstd = out if out is not None else ph.sb("rstd" + tag, [128, T_])
        sq = ph.sb("sq" + tag, [128, 2, TB], BF16)
        sqb = [Buf("sq0"), Buf("sq1")]
        for tb in range(T_ // TB):
            pst = kb.ps()
            for c in range(KC):
                ap, bufs = src_chunks(c, tb)
                kb.act(sq[:, c % 2, :], ap, AF.Square, bufs, [sqb[c % 2]])
                kb.mm(pst, pst[:, :], ones_bf[:], sq[:, c % 2, :], c == 0, c == KC - 1, [sqb[c % 2], ones_bf.b], inc=True)
            sl = slice(tb * TB, (tb + 1) * TB)
            kb.act(rstd[:, sl], pst[:, :], AF.Ln, [pst.b, epsb.b], [rstd.b], bias=epsb[:], scale=1.0 / D)
            kb.act(rstd[:, sl], rstd[:, sl], AF.Exp, [rstd.b], [rstd.b], scale=-0.5)
        return rstd

    rstd_cache = {}

    def make_h(ph, x, T_, l, g, which, tag):
        h = ph.sb("h" + tag, [128, KC, T_], BF16)
        with kb.phase() as p2:
            if "r" in rstd_cache:
                rstd = rstd_cache["r"]
            else:
                rstd = rstd_from(p2, lambda c, tb: (x[:, c, tb * TB:(tb + 1) * TB], [x.b]), T_, tag)
            tmp = p2.sb("htmp", [128, 2, T_])
            tb_ = [Buf("t0"), Buf("t1")]
            for c in range(KC):
                kb.tt(tmp[:, c % 2, :], x[:, c, :], rstd[:], ALU.mult, [x.b, rstd.b], [tb_[c % 2]])
                kb.act(h[:, c, :], tmp[:, c % 2, :], AF.Identity, [tb_[c % 2], dmod.b], [h.b],
                       bias=dmod[:, l, g, which + 1, c:c + 1], scale=dmod[:, l, g, which, c:c + 1])
        return h


    def proj(ph, wd, ncols_list, xk, nkc, T_, evac, name, pk=128, nslots=3, tilew=None, loadw=None, group=1):
        wds = wd if isinstance(wd, (list, tuple)) else [wd]
        ns_ = len(wds)
        tilew = tilew or max(w for _, w in ncols_list)
        ring = WRing(kb, ph, name, nslots, [pk, nkc, ns_, group * tilew])
        wvs = [w_.rearrange("(kc p) n -> p kc n", p=pk) for w_ in wds]
        n = len(ncols_list)
        ngr = (n + group - 1) // group
        loaded = []
        nxt = 0
        for ci in range(n):
            gidx = ci // group
            while nxt < ngr and nxt < gidx + nslots:
                s_ = ring.slots[nxt % nslots]
                first = nxt * group
                c0_, _w = ncols_list[first]
                totw = sum(w__ for _, w__ in ncols_list[first:first + group])
                for j in range(ns_):
                    c0j = c0_[j] if isinstance(c0_, (list, tuple)) else c0_
                    kb.dma("pool", s_[:, :, j, 0:totw], wvs[j][:, :, c0j:c0j + totw], writes=[s_.b])
                loaded.append(s_)
                nxt += 1
            slot = loaded[gidx]
            w = ncols_list[ci][1]
            off = (ci % group) * w
            tbw = min(TB, T_)
            for tb in range(max(1, T_ // TB)):
                sl = slice(tb * tbw, (tb + 1) * tbw)
                psts = []
                for j in range(ns_):
                    pst = kb.ps()
                    for kc in range(nkc):
                        xap, xb = xk(kc, sl)
                        kb.mm(pst, pst[0:w, 0:tbw], slot[:, kc, j, off:off + w], xap, kc == 0, kc == nkc - 1, [slot.b] + xb)
                    psts.append(pst)
                evac(ci, tb, psts[0] if ns_ == 1 else psts, w, sl)

    def run_group(g):
        with kb.phase(barrier=True) as phg:
            run_group_(g, phg)

    def run_group_(g, phg):
        T_ = 1024 if g == "S" else 512
        L = 1024 if g == "S" else 256
        nseq = T_ // L
        gi = 0 if g == "S" else 1
        NTB = T_ // TB
        x = phg.sb("x" + g, [128, KC, T_])
        kb.dma("sp", x[:], xT[g].rearrange("(c p) t -> p c t", p=128), writes=[x.b])
        for l in range(DEPTH):
            with kb.phase(barrier=True) as phm:
                rwout = phm.sb("rwout", [128, 6, T_], BF16)
                rstd1 = phm.sb("rstd1", [128, T_])
                with kb.phase() as pr_:
                    rstd_from(pr_, lambda c, tb: (x[:, c, tb * TB:(tb + 1) * TB], [x.b]), T_, "c", out=rstd1)
                rstd_cache["r"] = rstd1
                if do_rw:
                    rwkv_mixer(phm, g, l, x, rwout, T_, L, nseq, gi)
                else:
                    kb.memset(rwout[:], 0.0, [rwout.b])
                mix = phm.sb("mix", [128, 10, T_], BF16)
                kb.memset(mix[:], 0.0, [mix.b])
                if do_hy:
                    hyena_mixer(phm, g, l, x, mix, T_, L, nseq, gi)
                if do_att:
                    att_mixer(phm, g, l, x, mix, T_, L, nseq, gi)
                rstd_cache.pop("r", None)
                with kb.phase() as ph:
                    resid_branch(ph, w_out[l], KC, lambda kc, sl: ((mix[:, kc, sl], [mix.b]) if kc < 10 else (rwout[:, kc - 10, sl], [rwout.b])),
                                 x, T_, l, gi, 2, "wo")
            for tb in range(NTB):
                sl = slice(tb * TB, (tb + 1) * TB)
                with kb.phase(barrier=True) as ph:
                    xs = T(x.t, "xs")
                    xs.b = x.b
                    h2 = ph.sb("h2", [128, KC, TB], BF16)
                    with kb.phase() as p2:
                        rstd = rstd_from(p2, lambda c, _tb: (x[:, c, sl], [x.b]), TB, "m")
                        tmp = p2.sb("htmp", [128, 2, TB])
                        tb_ = [Buf("t0"), Buf("t1")]
                        for c in range(KC):
                            kb.tt(tmp[:, c % 2, :], x[:, c, sl], rstd[:], ALU.mult, [x.b, rstd.b], [tb_[c % 2]])
                            kb.act(h2[:, c, :], tmp[:, c % 2, :], AF.Identity, [tb_[c % 2], dmod.b], [h2.b],
                                   bias=dmod[:, l, gi, 4, c:c + 1], scale=dmod[:, l, gi, 3, c:c + 1])
                    fbuf = ph.sb("fbuf", [128, KC, TB])
                    for half in range(2):
                        with kb.phase() as p3:
                            hid = p3.sb("hid", [128, 32, TB], BF16)
                            rtmp = p3.sb("rtmp", [128, 2, TB])
                            rb = [Buf("r0"), Buf("r1")]

                            def ev1(ci, _tb, pst, w, _sl):
                                kb.act(rtmp[:, ci % 2, :], pst[:, :], AF.Relu, [pst.b], [rb[ci % 2]])
                                kb.tt(hid[:, ci, :], rtmp[:, ci % 2, :], rtmp[:, ci % 2, :], ALU.mult, [rb[ci % 2]], [hid.b])
                            proj(p3, mlp_w1[l], [(half * 4096 + i * 128, 128) for i in range(32)],
                                 lambda kc, s_: (h2[:, kc, s_], [h2.b]), KC, TB, ev1, "w1", group=2)

                            def ev2(ci, _tb, pst, w, _sl):
                                if half == 0:
                                    kb.copy(fbuf[:, ci, :], pst[:, :], [pst.b], [fbuf.b], eng="act")
                                else:
                                    kb.tt(fbuf[:, ci, :], fbuf[:, ci, :], pst[:, :], ALU.add, [pst.b, fbuf.b], [fbuf.b])
                            ring2 = WRing(kb, p3, "w2", 3, [128, 16, 256])
                            w2v = mlp_w2[l][half * 4096:(half + 1) * 4096, :].rearrange("(kc p) n -> p kc n", p=128)
                            jobs = [(tp_, ks) for tp_ in range(8) for ks in range(2)]
                            ld = []
                            nx = 0
                            for ji, (tp_, ks) in enumerate(jobs):
                                while nx < len(jobs) and nx < ji + 3:
                                    tq, kq = jobs[nx]
                                    s2 = ring2.slots[nx % 3]
                                    kb.dma("pool", s2[:], w2v[:, kq * 16:(kq + 1) * 16, tq * 256:(tq + 1) * 256], writes=[s2.b])
                                    ld.append(s2)
                                    nx += 1
                                slot = ld[ji]
                                if ks == 0:
                                    pp2 = [kb.ps(), kb.ps()]
                                for t2 in range(2):
                                    for kc in range(16):
                                        kb.mm(pp2[t2], pp2[t2][:, :], slot[:, kc, t2 * 128:(t2 + 1) * 128], hid[:, ks * 16 + kc, :],
                                              ks == 0 and kc == 0, ks == 1 and kc == 15, [slot.b, hid.b], inc=(kc == 15))
                                if ks == 1:
                                    for t2 in range(2):
                                        ev2(tp_ * 2 + t2, 0, pp2[t2], 128, None)
                    norm_add(ph, fbuf, x, sl, TB, l, gi, 5, "f")
        kb.dma("sp", yT[g].rearrange("(c p) t -> p c t", p=128), x[:], reads=[x.b])

    def norm_add(ph, obuf, x, sl, tw, l, gi, which, tag):
        with kb.phase() as p2:
            rstd = rstd_from(p2, lambda c, _tb: (obuf[:, c, :], [obuf.b]), tw, tag)
            tmp = p2.sb("ntmp", [128, 2, tw])
            tb_ = [Buf("t0"), Buf("t1")]
            for c in range(KC):
                kb.tt(tmp[:, c % 2, :], obuf[:, c, :], rstd[:], ALU.mult, [obuf.b, rstd.b], [tb_[c % 2]])
                kb.stt(x[:, c, sl], tmp[:, c % 2, :], dmod[:, l, gi, which, c:c + 1], x[:, c, sl], ALU.mult, ALU.add,
                       [tb_[c % 2], dmod.b, x.b], [x.b])

    def resid_branch(ph, wd, nkc, xk, x, T_, l, gi, which, name):
        for tb in range(T_ // TB):
            sl = slice(tb * TB, (tb + 1) * TB)
            with kb.phase() as p1:
                obuf = p1.sb("obuf", [128, KC, TB])

                def ev(ci, _tb, pst, w, _sl):
                    kb.copy(obuf[:, ci, :], pst[:, :], [pst.b], [obuf.b], eng="act")
                proj(p1, wd, [(i * 128, 128) for i in range(KC)],
                     lambda kc, s_: xk(kc, slice(sl.start + s_.start, sl.start + s_.stop)), nkc, TB, ev, name, group=2)
                norm_add(p1, obuf, x, sl, TB, l, gi, which, "o")

    def rwkv_mixer(phm, g, l, x, rwout, T_, L, nseq, gi):
        NCH = T_ // 128
        ncs = L // 128
        NB = max(1, T_ // TB)
        BW = min(TB, T_)
        CPB = BW // 128
        C0 = math.exp(-0.5)
        c4 = lambda ap: ap.rearrange("p (c t) -> p c t", t=128)
        with kb.phase() as ph:
            wdn = ph.sb("wdn", [128, 2, T_], BF16)
            adn = ph.sb("adn", [128, 2, T_], BF16)
            gdn = ph.sb("gdn", [128, 2, T_], BF16)
            rsw = ph.sb("rsw", [128, 18, 3])
            w0 = ph.sb("w0", [128, 2, 6])
            a0 = ph.sb("a0", [128, 2, 6])
            vec = ph.sb("vec", [128, 5, 6])
            rwm = ph.sb("rwm", [128, 2, 512], BF16)
            rwmA = ph.sb("rwmA", [128, 2, 128], BF16)
            scm = ph.sb("scm", [128, BW])
            eps24 = ph.sb("eps24", [128, 1])
            epsgn = ph.sb("epsgn", [128, 1])
            identb = ph.sb("identb", [128, 128], BF16)
            kb.memset(wdn[:], 0.0, [wdn.b])
            kb.memset(adn[:], 0.0, [adn.b])
            kb.memset(eps24[:], 1e-24, [eps24.b])
            kb.memset(epsgn[:], 64e-5, [epsgn.b])
            kb.copy(identb[:], ident[:], [ident.b], [identb.b])
            kb.dma("sp", rsw[:], rw_short[l], writes=[rsw.b])
            kb.dma("sp", w0[:], rw_w0[l], writes=[w0.b])
            kb.dma("sp", a0[:], rw_a0[l], writes=[a0.b])
            kb.dma("sp", vec[:], rw_vec[l], writes=[vec.b])
            kb.dma("sp", rwm[:], rwmask_d.rearrange("d p q -> p d q"), writes=[rwm.b])
            kb.dma("sp", rwmA[:], rwmaskA_d.rearrange("d p q -> p d q"), writes=[rwmA.b])
            kb.dma("sp", scm[:], scanmask_d[:, 0:BW], writes=[scm.b])
            h_keep = make_h(ph, x, T_, l, gi, 0, "rk") if g == "P" else None
            with kb.phase() as p1:
                h = h_keep if h_keep is not None else make_h(p1, x, T_, l, gi, 0, "r")
                tiles = [(C_WDN, 96), (C_WDN + 96, 96), (C_ADN, 96), (C_ADN + 96, 96), (C_GDN, 128), (C_GDN + 128, 128)]

                def evl(ci, tb, pst, w, sl):
                    n_ = sl.stop - sl.start
                    if ci < 2:
                        kb.act(wdn[0:96, ci, sl], pst[0:96, 0:n_], AF.Tanh, [pst.b], [wdn.b])
                    elif ci < 4:
                        kb.act(adn[0:96, ci - 2, sl], pst[0:96, 0:n_], AF.Identity, [pst.b], [adn.b])
                    else:
                        kb.act(gdn[:, ci - 4, sl], pst[:, 0:n_], AF.Sigmoid, [pst.b], [gdn.b])
                proj(p1, w_in[l], tiles, lambda kc, sl: (h[:, kc, sl], [h.b]), KC, T_, evl, "wlo", nslots=2)
            wv_in = w_in[l].rearrange("(kc p) n -> p kc n", p=128)
            for j in range(6):
                with kb.phase(barrier=True) as pp:
                    KRz = [[pp.sb(f"KRz{d}{hh}", [128, NCH, 2, 128], BF16) for hh in range(2)] for d in range(2)]
                    BK = [pp.sb(f"BK{d}", [128, NCH, 2, 128], BF16) for d in range(2)]
                    BKT = [pp.sb(f"BKT{d}", [128, NCH, 2, 128], BF16) for d in range(2)]
                    etot = pp.sb("etot", [128, 2, NCH])
                    VT = pp.sb("VT", [128, NCH, 128], BF16)
                    yacc = pp.sb("yacc", [128, T_])
                    bonus = pp.sb("bonus", [128, T_], BF16)
                    ups = pp.sb("ups", [128, 3, 2, 128], BF16)
                    cv = [pp.sb(f"rcv{a}", [128, T_]) for a in range(3)]
                    kb.memset(ups[:], 0.0, [ups.b])
                    for d in range(2):
                        for hh in range(2):
                            kb.memset(KRz[d][hh][:], 0.0, [KRz[d][hh].b])
                        kb.dma("pool", ups[0:96, 0, d, :], rw_wup[l, d][:, j * 128:(j + 1) * 128], writes=[ups.b])
                        kb.dma("pool", ups[0:96, 1, d, :], rw_aup[l, d][:, j * 128:(j + 1) * 128], writes=[ups.b])
                        kb.dma("pool", ups[:, 2, d, :], rw_gup[l][d * 128:(d + 1) * 128, j * 128:(j + 1) * 128], writes=[ups.b])
                    kb.memset(yacc[:], 0.0, [yacc.b])
                    with kb.phase() as pj:
                        h = h_keep if h_keep is not None else make_h(pj, x, T_, l, gi, 0, "r")
                        wt = pj.sb("wrkv", [128, KC, 3, 128], BF16)
                        for a in range(3):
                            c0 = C_RKV + a * 768 + j * 128
                            kb.dma("pool", wt[:, :, a, :], wv_in[:, :, c0:c0 + 128], writes=[wt.b])
                        for a in range(3):
                            wi = a * 6 + j
                            pss = []
                            for tb in range(NB):
                                pst = kb.ps()
                                for kc in range(KC):
                                    kb.mm(pst, pst[:, 0:BW], wt[:, kc, a, :], h[:, kc, tb * BW:(tb + 1) * BW], kc == 0, kc == KC - 1, [wt.b, h.b])
                                pss.append(pst)
                            for tb in range(NB):
                                kb.ts(cv[a][:, tb * BW:(tb + 1) * BW], pss[tb][:, 0:BW], rsw[:, wi, 1:2], ALU.mult, [pss[tb].b, rsw.b], [cv[a].b])
                            for s_ in range(nseq):
                                for tb in range(NB):
                                    lo, hi = max(s_ * L, tb * BW), min((s_ + 1) * L, (tb + 1) * BW)
                                    if lo >= hi:
                                        continue
                                    o = tb * BW
                                    kb.stt(cv[a][:, lo + 1:hi], pss[tb][:, lo - o:hi - 1 - o], rsw[:, wi, 0:1], cv[a][:, lo + 1:hi], ALU.mult, ALU.add,
                                           [pss[tb].b, rsw.b, cv[a].b], [cv[a].b])
                                    kb.stt(cv[a][:, lo:hi - 1], pss[tb][:, lo + 1 - o:hi - o], rsw[:, wi, 2:3], cv[a][:, lo:hi - 1], ALU.mult, ALU.add,
                                           [pss[tb].b, rsw.b, cv[a].b], [cv[a].b])
                                    if lo > s_ * L:
                                        kb.stt(cv[a][:, lo:lo + 1], pss[tb - 1][:, BW - 1:BW], rsw[:, wi, 0:1], cv[a][:, lo:lo + 1], ALU.mult, ALU.add,
                                               [pss[tb - 1].b, rsw.b, cv[a].b], [cv[a].b])
                                    if hi < (s_ + 1) * L:
                                        kb.stt(cv[a][:, hi - 1:hi], pss[tb + 1][:, 0:1], rsw[:, wi, 2:3], cv[a][:, hi - 1:hi], ALU.mult, ALU.add,
                                               [pss[tb + 1].b, rsw.b, cv[a].b], [cv[a].b])
                    rc, kc_, vc = cv
                    with kb.phase() as pq:
                        tp_ = [pq.sb(f"tp{i}", [128, BW]) for i in range(8)]
                        kap, t1, cum, ta, b32, kd32, excl, E = tp_
                        ntot = pq.sb("ntot", [128, CPB])
                        for bi in range(NB):
                            bs = slice(bi * BW, (bi + 1) * BW)
                            cs = slice(bi * CPB, (bi + 1) * CPB)
                            kb.ts(kap[:], kc_[:, bs], vec[:, 0, j:j + 1], ALU.mult, [kc_.b, vec.b], [kap.b])
                            kb.tt(t1[:], kap[:], kap[:], ALU.mult, [kap.b], [t1.b])
                            pst = kb.ps()
                            kb.mm(pst, pst[:, 0:BW], bones[:], t1[:], True, True, [bones.b, t1.b])
                            kb.act(t1[:], pst[:, 0:BW], AF.Ln, [pst.b, eps24.b], [t1.b], bias=eps24[:], scale=1.0)
                            kb.act(t1[:], t1[:], AF.Exp, [t1.b], [t1.b], scale=-0.5)
                            kb.tt(kap[:], kap[:], t1[:], ALU.mult, [kap.b, t1.b], [kap.b])
                            kb.tt(t1[:], rc[:, bs], kc_[:, bs], ALU.mult, [rc.b, kc_.b], [t1.b])
                            kb.ts(t1[:], t1[:], vec[:, 2, j:j + 1], ALU.mult, [t1.b, vec.b], [t1.b])
                            pst = kb.ps()
                            kb.mm(pst, pst[:, 0:BW], bones[:], t1[:], True, True, [bones.b, t1.b])
                            kb.tt(bonus[:, bs], pst[:, 0:BW], vc[:, bs], ALU.mult, [pst.b, vc.b], [bonus.b])
                            for c in range(CPB):
                                cg = bi * CPB + c
                                pst = kb.ps()
                                kb.tr(pst, pst[:, 0:128], vc[:, cg * 128:(cg + 1) * 128], ident[:], [vc.b, ident.b])
                                kb.copy(VT[:, cg, :], pst[:, 0:128], [pst.b], [VT.b], eng="act")
                            for d in range(2):
                                pst = kb.ps()
                                kb.mm(pst, pst[:, 0:BW], ups[:, 0, d, :], wdn[:, d, bs], True, True, [ups.b, wdn.b])
                                kb.act(t1[:], pst[:, 0:BW], AF.Sigmoid, [pst.b, w0.b], [t1.b], bias=w0[:, d, j:j + 1], scale=1.0)
                                kb.ts(t1[:], t1[:], -C0, ALU.mult, [t1.b], [t1.b])
                                kb.op("dve", lambda e: e.tensor_tensor_scan(out=cum[:], data0=scm[:], data1=t1[:], initial=0.0,
                                                                             op0=ALU.mult, op1=ALU.add), [scm.b, t1.b], [cum.b])
                                kb.tt(excl[:], cum[:], t1[:], ALU.subtract, [cum.b, t1.b], [excl.b])
                                pst = kb.ps()
                                kb.mm(pst, pst[:, 0:BW], ups[:, 1, d, :], adn[:, d, bs], True, True, [ups.b, adn.b])
                                kb.act(ta[:], pst[:, 0:BW], AF.Sigmoid, [pst.b, a0.b], [ta.b], bias=a0[:, d, j:j + 1], scale=1.0)
                                kb.tt(b32[:], kap[:], ta[:], ALU.mult, [kap.b, ta.b], [b32.b])
                                kb.ts(ta[:], ta[:], -1.0, ALU.add, [ta.b, vec.b], [ta.b], s2=vec[:, 1, j:j + 1], op1=ALU.mult)
                                kb.stt(kd32[:], ta[:], 1.0, kc_[:, bs], ALU.add, ALU.mult, [ta.b, kc_.b], [kd32.b])
                                totv = c4(cum[:])[:, :, 127]
                                kb.act(etot[:, d, cs], totv, AF.Exp, [cum.b], [etot.b])
                                kb.ts(ntot[:], totv, -1.0, ALU.mult, [cum.b], [ntot.b])

                                def expo(src, sc, use_tot, d=d):
                                    if d == 1:
                                        kb.act(E[:], src[:], AF.Exp, [src.b], [E.b], scale=sc)
                                    else:
                                        for c in range(CPB):
                                            bias = ntot[:, c:c + 1] if use_tot < 0 else cum[:, c * 128 + 127:c * 128 + 128]
                                            kb.act(E[:, c * 128:(c + 1) * 128], src[:, c * 128:(c + 1) * 128], AF.Exp, [src.b, ntot.b, cum.b], [E.b],
                                                   bias=bias, scale=sc)
                                if d == 0:
                                    expo(excl, 1.0, -1)
                                else:
                                    expo(cum, -1.0, 0)
                                for hh in range(2):
                                    pb = 64 * hh
                                    kb.tt(KRz[d][hh][pb:pb + 64, cs, 0, :], c4(kap[pb:pb + 64, :]), c4(E[pb:pb + 64, :]), ALU.mult,
                                          [kap.b, E.b], [KRz[d][hh].b])
                                if d == 0:
                                    expo(cum, 1.0, -1)
                                else:
                                    expo(excl, -1.0, 0)
                                for hh in range(2):
                                    pb = 64 * hh
                                    kb.tt(KRz[d][hh][pb:pb + 64, cs, 1, :], c4(rc[pb:pb + 64, bs]), c4(E[pb:pb + 64, :]), ALU.mult,
                                          [rc.b, E.b], [KRz[d][hh].b])
                                if d == 0:
                                    expo(cum, -1.0, +1)
                                else:
                                    expo(excl, 1.0, 0)
                                kb.tt(b32[:], b32[:], E[:], ALU.mult, [b32.b, E.b], [b32.b])
                                kb.tt(kd32[:], kd32[:], E[:], ALU.mult, [kd32.b, E.b], [kd32.b])
                                kb.copy(BK[d][:, cs, 0, :], c4(b32[:]), [b32.b], [BK[d].b])
                                kb.copy(BK[d][:, cs, 1, :], c4(kd32[:]), [kd32.b], [BK[d].b])
                                for c in range(CPB):
                                    cg = bi * CPB + c
                                    pst = kb.ps()
                                    kb.tr(pst, pst[:, 0:128], b32[:, c * 128:(c + 1) * 128], ident[:], [b32.b, ident.b])
                                    kb.tr(pst, pst[:, 128:256], kd32[:, c * 128:(c + 1) * 128], ident[:], [kd32.b, ident.b])
                                    kb.copy(BKT[d][:, cg, :, :], pst[:, 0:256].rearrange("p (a k) -> p a k", a=2), [pst.b], [BKT[d].b], eng="act")
                    with kb.phase() as pc:
                        SS = pc.sb("SS", [128, nseq, 2, 64])
                        SH = pc.sb("SH", [128, nseq, 2, 64])
                        SHb = pc.sb("SHb", [128, nseq, 2, 64], BF16)
                        RHs = [pc.sb(f"RH{i}", [128, 4, 64], BF16) for i in range(2)]
                        USs = [pc.sb(f"US{i}", [128, 4, 64], BF16) for i in range(2)]
                        if g == "S":
                            for d in range(2):
                                for hh in range(2):
                                    kb.dma("sp", SS[64 * hh:64 * hh + 64, 0, d, :], st0T[l, d, 2 * j + hh], writes=[SS.b])
                        else:
                            kb.memset(SS[:], 0.0, [SS.b])
                        u3 = lambda ap, n_, w_: ap.rearrange("p (u t) -> p u t", t=w_)
                        HS = min(ncs, 4)
                        NBT = (HS * nseq * 4) // 4
                        AB3 = [pc.sb(f"AB3_{b}", [128, 4, 384], BF16) for b in range(NBT)]
                        MTs = [pc.sb(f"MTs{b}", [128, 4, 128], BF16) for b in range(NBT)]
                        XX = [[pc.sb(f"XX{b}{i}", [128, 4, 128], BF16) for i in range(2)] for b in range(NBT)]
                        XXT = [[pc.sb(f"XXT{b}{i}", [128, 4, 128], BF16) for i in range(2)] for b in range(NBT)]

                        def units_of(i):
                            us = []
                            for s_ in range(nseq):
                                for d in range(2):
                                    c = s_ * ncs + (i if d == 0 else ncs - 1 - i)
                                    for hh in range(2):
                                        us.append((hh, d, c, s_))
                            return us
                        for half in range(ncs // HS):
                            steps = list(range(half * HS, (half + 1) * HS))
                            batches = []
                            for i in steps:
                                us = units_of(i)
                                for b0 in range(0, len(us), 4):
                                    batches.append((i, us[b0:b0 + 4]))
                            for b, (i, ub) in enumerate(batches):
                                pA = kb.ps()
                                for u, (hh, d, c, s_) in enumerate(ub):
                                    pst = kb.ps()
                                    kr = KRz[d][hh][:, c, :, :].rearrange("p a t -> p (a t)")
                                    kb.mm(pst, pst[:, 0:256], BK[d][:, c, 0, :], kr, True, True, [BK[d].b, KRz[d][hh].b])
                                    kb.mm(pst, pst[:, 256:512], BK[d][:, c, 1, :], kr, True, True, [BK[d].b, KRz[d][hh].b])
                                    kb.tt(XXT[b][0][:, u, :], pst[:, 0:128], rwm[:, d, 0:128], ALU.mult, [pst.b, rwm.b], [XXT[b][0].b])
                                    kb.tt(AB3[b][:, u, :], pst[:, 128:512], rwm[:, d, 128:512], ALU.mult, [pst.b, rwm.b], [AB3[b].b])
                                    kb.mm(pA, pA[:, u * 128:(u + 1) * 128], KRz[d][hh][:, c, 0, :], BK[d][:, c, 0, :], True, True,
                                          [BK[d].b, KRz[d][hh].b])
                                    kb.tt(XX[b][0][:, u, :], pA[:, u * 128:(u + 1) * 128], rwmA[:, d, :], ALU.mult, [pA.b, rwmA.b], [XX[b][0].b])
                                for u in range(4):
                                    kb.tt(MTs[b][:, u, :], XXT[b][0][:, u, :], identb[:], ALU.add, [XXT[b][0].b, identb.b], [MTs[b].b])
                            cur = 0
                            for r_ in range(6):
                                p1s, p2s = [], []
                                for b in range(len(batches)):
                                    X, XT = XX[b][cur], XXT[b][cur]
                                    p1_ = kb.ps()
                                    for u in range(4):
                                        kb.mm(p1_, p1_[:, u * 128:(u + 1) * 128], XT[:, u, :], X[:, u, :], True, True, [XT.b, X.b])
                                    p1s.append(p1_)
                                    kb.copy(XX[b][1 - cur][:], u3(p1_[:, :], 4, 128), [p1_.b], [XX[b][1 - cur].b], eng=("act" if b % 2 else "dve"))
                                    if r_ < 5:
                                        p2_ = kb.ps()
                                        for u in range(4):
                                            kb.mm(p2_, p2_[:, u * 128:(u + 1) * 128], X[:, u, :], XT[:, u, :], True, True, [XT.b, X.b])
                                        kb.copy(XXT[b][1 - cur][:], u3(p2_[:, :], 4, 128), [p2_.b], [XXT[b][1 - cur].b], eng="act")
                                for b in range(len(batches)):
                                    X2 = XX[b][1 - cur]
                                    p3_ = kb.ps()
                                    for u in range(4):
                                        kb.mm(p3_, p3_[:, u * 128:(u + 1) * 128], X2[:, u, :], MTs[b][:, u, :], True, True, [X2.b, MTs[b].b])
                                    kb.tt(MTs[b][:], MTs[b][:], u3(p3_[:, :], 4, 128), ALU.add, [MTs[b].b, p3_.b], [MTs[b].b])
                                cur = 1 - cur
                            for i in steps:
                                for s_ in range(nseq):
                                    for d in range(2):
                                        c = s_ * ncs + (i if d == 0 else ncs - 1 - i)
                                        kb.ts(SH[:, s_, d, :], SS[:, s_, d, :], etot[:, d, c:c + 1], ALU.mult, [SS.b, etot.b], [SH.b])
                                kb.copy(SHb[:], SH[:], [SH.b], [SHb.b], eng="act")
                                bl = [(b, ub) for b, (ii, ub) in enumerate(batches) if ii == i]
                                pRs, pUs = {}, {}
                                for b, ub in bl:
                                    pR = kb.ps()
                                    for u, (hh, d, c, s_) in enumerate(ub):
                                        pb = 64 * hh
                                        kb.mm(pR, pR[:, u * 64:(u + 1) * 64], KRz[d][hh][:, c, 0, :], SHb[:, s_, d, :], True, False,
                                              [KRz[d][hh].b, SHb.b], inc=False)
                                        kb.mm(pR, pR[:, u * 64:(u + 1) * 64], AB3[b][:, u, 128:256], VT[:, c, pb:pb + 64], False, True,
                                              [AB3[b].b, VT.b], inc=True)
                                    pRs[b] = pR
                                RHb = {}
                                for k_, (b, ub) in enumerate(bl):
                                    RHt = RHs[k_]
                                    kb.act(RHt[:], u3(pRs[b][:, 0:256], 4, 64), AF.Identity, [pRs[b].b], [RHt.b], scale=-1.0)
                                    RHb[b] = RHt
                                for b, ub in bl:
                                    pU = kb.ps()
                                    for u in range(4):
                                        kb.mm(pU, pU[:, u * 64:(u + 1) * 64], MTs[b][:, u, :], RHb[b][:, u, :], True, True, [MTs[b].b, RHb[b].b])
                                    pUs[b] = pU
                                USb = {}
                                for k_, (b, ub) in enumerate(bl):
                                    USt = USs[k_]
                                    kb.copy(USt[:], u3(pUs[b][:, 0:256], 4, 64), [pUs[b].b], [USt.b], eng="act")
                                    USb[b] = USt
                                for b, ub in bl:
                                    pY = kb.ps()
                                    pS = kb.ps()
                                    for u, (hh, d, c, s_) in enumerate(ub):
                                        pb = 64 * hh
                                        q = u // 2
                                        yo = pY[pb:pb + 64, q * 128:(q + 1) * 128]
                                        kb.mm(pY, yo, SHb[:, s_, d, :], KRz[d][hh][:, c, 1, :], True, False, [SHb.b, KRz[d][hh].b], inc=False, tp=(0, pb))
                                        kb.mm(pY, yo, USb[b][:, u, :], AB3[b][:, u, 0:128], False, False, [USb[b].b, AB3[b].b], inc=False, tp=(0, pb))
                                        kb.mm(pY, yo, VT[:, c, pb:pb + 64], AB3[b][:, u, 256:384], False, True, [VT.b, AB3[b].b], inc=True, tp=(0, pb))
                                        so = pS[pb:pb + 64, q * 64:(q + 1) * 64]
                                        kb.mm(pS, so, BKT[d][:, c, 0, pb:pb + 64], USb[b][:, u, :], True, False, [BKT[d].b, USb[b].b], inc=False, tp=(0, pb))
                                        kb.mm(pS, so, BKT[d][:, c, 1, pb:pb + 64], VT[:, c, pb:pb + 64], False, True, [BKT[d].b, VT.b], inc=True, tp=(0, pb))
                                    for q in range(2):
                                        hh, d, c, s_ = ub[2 * q]
                                        kb.tt(yacc[:, c * 128:(c + 1) * 128], yacc[:, c * 128:(c + 1) * 128], pY[:, q * 128:(q + 1) * 128], ALU.add,
                                              [yacc.b, pY.b], [yacc.b])
                                        kb.tt(SS[:, s_, d, :], SH[:, s_, d, :], pS[:, q * 64:(q + 1) * 64], ALU.add, [SH.b, pS.b], [SS.b])
                        if g == "P":
                            for s_ in range(nseq):
                                for d in range(2):
                                    for hh in range(2):
                                        kb.dma("sp", newst[l, s_, d, 2 * j + hh], SS[64 * hh:64 * hh + 64, s_, d, :], reads=[SS.b])
                    with kb.phase() as pe:
                        dev = pe.sb("dev", [128, BW])
                        sq2 = pe.sb("sq2", [128, BW])
                        for bi in range(NB):
                            bs = slice(bi * BW, (bi + 1) * BW)
                            pst = kb.ps()
                            kb.mm(pst, pst[:, 0:BW], bones[:], yacc[:, bs], True, True, [bones.b, yacc.b])
                            kb.stt(dev[:], pst[:, 0:BW], -1.0 / 64, yacc[:, bs], ALU.mult, ALU.add, [pst.b, yacc.b], [dev.b])
                            kb.tt(sq2[:], dev[:], dev[:], ALU.mult, [dev.b], [sq2.b])
                            pst = kb.ps()
                            kb.mm(pst, pst[:, 0:BW], bones[:], sq2[:], True, True, [bones.b, sq2.b])
                            kb.act(sq2[:], pst[:, 0:BW], AF.Ln, [pst.b, epsgn.b], [sq2.b], bias=epsgn[:], scale=1.0 / 64)
                            kb.act(sq2[:], sq2[:], AF.Exp, [sq2.b], [sq2.b], scale=-0.5)
                            kb.tt(dev[:], dev[:], sq2[:], ALU.mult, [dev.b, sq2.b], [dev.b])
                            kb.ts(dev[:], dev[:], vec[:, 3, j:j + 1], ALU.mult, [dev.b, vec.b], [dev.b], s2=vec[:, 4, j:j + 1], op1=ALU.add)
                            kb.tt(dev[:], dev[:], bonus[:, bs], ALU.add, [dev.b, bonus.b], [dev.b])
                            pst = kb.ps()
                            for k2 in range(2):
                                kb.mm(pst, pst[:, 0:BW], ups[:, 2, k2, :], gdn[:, k2, bs], k2 == 0, k2 == 1, [ups.b, gdn.b])
                            kb.tt(rwout[:, j, bs], dev[:], pst[:, 0:BW], ALU.mult, [dev.b, pst.b], [rwout.b])

    def hyena_mixer(phm, g, l, x, mix, T_, L, nseq, gi):
        NT = L // 128
        hc = hyc[g]
        TW = min(L, 512)
        with kb.phase() as ph:
            x0c = ph.sb("x0c", [128, 4, T_], BF16)
            zf = ph.sb("zf", [128, 4, T_], BF16)
            ztm = ph.sb("ztm", [128, T_ // 128, 512], BF16)
            shw = ph.sb("shw", [128, 12, 3])
            skp = ph.sb("skp", [128, 4])
            kb.dma("sp", shw[:], hy_short[l], writes=[shw.b])
            kb.dma("sp", skp[:], hy_skip[l], writes=[skp.b])
            with kb.phase() as p1:
                h = make_h(p1, x, T_, l, gi, 0, "h")
                raw = [p1.sb(f"raw{i}", [128, T_]) for i in range(3)]
                cv = [p1.sb(f"cv{i}", [128, T_]) for i in range(3)]
                zt32 = p1.sb("zt32", [128, T_])
                tiles = []
                for jc in range(4):
                    for a in range(3):
                        tiles.append((C_HY + a * 512 + jc * 128, 128))
                NTBp = max(1, T_ // TB)

                def ev(ci, tb, pst, w, sl):
                    a, jc = ci % 3, ci // 3
                    kb.copy(raw[a][:, sl], pst[:, 0:sl.stop - sl.start], [pst.b], [raw[a].b])
                    if tb != NTBp - 1:
                        return
                    wi = a * 4 + jc
                    for s_ in range(nseq):
                        a0, b0 = s_ * L, (s_ + 1) * L
                        kb.ts(cv[a][:, a0:b0], raw[a][:, a0:b0], shw[:, wi, 1:2], ALU.mult, [raw[a].b, shw.b], [cv[a].b])
                        kb.stt(cv[a][:, a0 + 1:b0], raw[a][:, a0:b0 - 1], shw[:, wi, 0:1], cv[a][:, a0 + 1:b0], ALU.mult, ALU.add,
                               [raw[a].b, shw.b, cv[a].b], [cv[a].b])
                        kb.stt(cv[a][:, a0:b0 - 1], raw[a][:, a0 + 1:b0], shw[:, wi, 2:3], cv[a][:, a0:b0 - 1], ALU.mult, ALU.add,
                               [raw[a].b, shw.b, cv[a].b], [cv[a].b])
                    if a != 2:
                        return
                    kb.copy(x0c[:, jc, :], cv[0][:], [cv[0].b], [x0c.b])
                    kb.tt(zt32[:], cv[1][:], cv[2][:], ALU.mult, [cv[1].b, cv[2].b], [zt32.b])
                    kb.copy(zf[:, jc, :], zt32[:], [zt32.b], [zf.b])
                    for tt_ in range(T_ // 128):
                        pt_ = kb.ps()
                        kb.tr(pt_, pt_[:, 0:128], zt32[:, tt_ * 128:(tt_ + 1) * 128], ident[:], [zt32.b, ident.b])
                        kb.copy(ztm[:, tt_, jc * 128:(jc + 1) * 128], pt_[:, 0:128], [pt_.b], [ztm.b])
                proj(p1, w_in[l], tiles, lambda kc, sl: (h[:, kc, sl], [h.b]), KC, T_, ev, "why")
            HP = ph.sb("HP", [128, NT, 512], BF16)
            HQ = ph.sb("HQ", [128, NT, 512], BF16)
            YP = ph.sb("YP", [128, nseq * NT, 512], BF16)
            YQ = ph.sb("YQ", [128, nseq * NT, 512], BF16)
            with kb.phase() as pa0:
              filt = pa0.sb("filt", [128, NT, 1024], BF16)
              with kb.phase() as pa:
                ft = pa.sb("ft", [128, L])
                f1 = pa.sb("f1", [128, 64])
                f2 = pa.sb("f2", [128, 64])
                f3 = pa.sb("f3", [128, 1024])
                b12 = pa.sb("b12", [64, 2])
                h1 = pa.sb("h1", [128, L])
                h2 = pa.sb("h2f", [128, L])
                vb = pa.sb("vb", [64, TW])
                nn = pa.sb("nn", [64, TW])
                dec = pa.sb("dec", [128, 1024])
                ee = pa.sb("ee", [128, 1024])
                nt01 = pa.sb("nt01", [128, NT])
                for t_ in (ft, f1, f2, f3, h1, h2):
                    kb.memset(t_[:], 0.0, [t_.b])
                kb.dma("sp", ft[0:33, :], hc["featT"], writes=[ft.b])
                kb.dma("sp", f1[0:33, :], hy_f1[l], writes=[f1.b])
                kb.dma("sp", f2[0:64, :], hy_f2[l], writes=[f2.b])
                kb.dma("sp", f3[0:64, :], hy_f3[l], writes=[f3.b])
                kb.dma("sp", b12[:, 0:1], hy_b1[l], writes=[b12.b])
                kb.dma("sp", b12[:, 1:2], hy_b2[l], writes=[b12.b])
                kb.dma("sp", dec[:], hy_decay[l], writes=[dec.b])
                kb.dma("sp", nt01[:], hc["t01"], writes=[nt01.b])
                kb.stt(ee[:], dec[:], -1.0, dec[:], ALU.mult, ALU.max, [dec.b], [ee.b])
                kb.copy(dec[:], ee[:], [ee.b], [dec.b])

                def sin_layer(wt, src, dst, bcol):
                    for b_ in range(L // TW):
                        sl = slice(b_ * TW, (b_ + 1) * TW)
                        pst = kb.ps()
                        kb.mm(pst, pst[0:64, 0:TW], wt[:, 0:64], src[:, sl], True, True, [wt.b, src.b])
                        kb.ts(vb[:], pst[0:64, 0:TW], b12[:, bcol:bcol + 1], ALU.add, [pst.b, b12.b], [vb.b])
                        kb.ts(nn[:], vb[:], 1.0 / (2 * math.pi), ALU.mult, [vb.b], [nn.b], s2=MAGIC, op1=ALU.add)
                        kb.ts(nn[:], nn[:], -MAGIC, ALU.add, [nn.b], [nn.b])
                        kb.stt(vb[:], nn[:], -2 * math.pi, vb[:], ALU.mult, ALU.add, [nn.b, vb.b], [vb.b])
                        kb.act(dst[0:64, sl], vb[:], AF.Sin, [vb.b], [dst.b])
                sin_layer(f1, ft, h1, 0)
                sin_layer(f2, h1, h2, 1)
                for j in range(NT):
                    kb.act(ee[:], dec[:], AF.Exp, [dec.b, nt01.b], [ee.b], scale=nt01[:, j:j + 1])
                    for hf in range(2):
                        pst = kb.ps()
                        kb.mm(pst, pst[:, :], h2[:, j * 128:(j + 1) * 128], f3[:, hf * 512:(hf + 1) * 512], True, True, [h2.b, f3.b])
                        kb.stt(filt[:, j, hf * 512:(hf + 1) * 512], ee[:, hf * 512:(hf + 1) * 512], 0.05, pst[:, :], ALU.add, ALU.mult,
                               [ee.b, pst.b], [filt.b])
              with kb.phase() as pa:
                ring = WRing(kb, pa, "hm", 2, [128, 4, NT, 128])
                mats = [hc["Ch"], hc["Cb"], hc["Sh"], hc["Sb"]]
                mv = [m_.rearrange("(j p) f -> p j f", p=128) for m_ in mats]
                for i in range(NT):
                    slot = ring.slots[i % 2]
                    for a in range(4):
                        kb.dma("sp", slot[:, a, :, :], mv[a][:, :, i * 128:(i + 1) * 128], writes=[slot.b])
                    for q_, dstH in ((0, HP), (1, HQ)):
                        pst = kb.ps()
                        for j in range(NT):
                            kb.mm(pst, pst[:, :], slot[:, 2 * q_, j, :], filt[:, j, 0:512], j == 0, False, [slot.b, filt.b])
                            kb.mm(pst, pst[:, :], slot[:, 2 * q_ + 1, j, :], filt[:, j, 512:1024], False, j == NT - 1, [slot.b, filt.b])
                        kb.copy(dstH[:, i, :], pst[:, :], [pst.b], [dstH.b])
            with kb.phase() as pf:
                ring = WRing(kb, pf, "fm", 2, [128, 2, NT, 128])
                mv = [m_.rearrange("(j p) f -> p j f", p=128) for m_ in (hc["C"], hc["S"])]
                tm = pf.sb("tm", [128, 4, 512])
                for i in range(NT):
                    slot = ring.slots[i % 2]
                    for a in range(2):
                        kb.dma("sp", slot[:, a, :, :], mv[a][:, :, i * 128:(i + 1) * 128], writes=[slot.b])
                    for s_ in range(nseq):
                        pp = kb.ps()
                        pq = kb.ps()
                        for a, pst in ((0, pp), (1, pq)):
                            for j in range(NT):
                                kb.mm(pst, pst[:, :], slot[:, a, j, :], ztm[:, s_ * NT + j, :], j == 0, j == NT - 1, [slot.b, ztm.b])
                        ii = s_ * NT + i
                        rd = [pp.b, pq.b, HP.b, HQ.b]
                        kb.tt(tm[:, 0, :], pp[:, :], HP[:, i, :], ALU.mult, rd, [tm.b])
                        kb.tt(tm[:, 1, :], pq[:, :], HQ[:, i, :], ALU.mult, rd, [tm.b])
                        kb.tt(tm[:, 2, :], pp[:, :], HQ[:, i, :], ALU.mult, rd, [tm.b])
                        kb.tt(tm[:, 3, :], pq[:, :], HP[:, i, :], ALU.mult, rd, [tm.b])
                        kb.tt(YP[:, ii, :], tm[:, 0, :], tm[:, 1, :], ALU.subtract, [tm.b], [YP.b])
                        kb.tt(YQ[:, ii, :], tm[:, 2, :], tm[:, 3, :], ALU.add, [tm.b], [YQ.b])
                        if i == 0:
                            kb.copy(YP[0:1, ii, :], tm[0:1, 0, :], [tm.b], [YP.b])
                            kb.copy(YQ[0:1, ii, :], tm[0:1, 1, :], [tm.b], [YQ.b])
            with kb.phase() as pi_:
                Cm = pi_.sb("Cm", [128, NT, L], BF16)
                Sm = pi_.sb("Sm", [128, NT, L], BF16)
                kb.dma("sp", Cm[:], hc["C"].rearrange("(i p) t -> p i t", p=128), writes=[Cm.b])
                kb.dma("sp", Sm[:], hc["Si"].rearrange("(i p) t -> p i t", p=128), writes=[Sm.b])
                t32 = pi_.sb("t32", [128, 2, TW])
                tbf = [Buf("a"), Buf("b")]
                n_ = 0
                for s_ in range(nseq):
                    for jc in range(4):
                        for tbk in range(L // TW):
                            pst = kb.ps()
                            for i in range(NT):
                                ii = s_ * NT + i
                                kb.mm(pst, pst[:, 0:TW], YP[:, ii, jc * 128:(jc + 1) * 128], Cm[:, i, tbk * TW:(tbk + 1) * TW], i == 0, False,
                                      [YP.b, Cm.b])
                                kb.mm(pst, pst[:, 0:TW], YQ[:, ii, jc * 128:(jc + 1) * 128], Sm[:, i, tbk * TW:(tbk + 1) * TW], False, i == NT - 1,
                                      [YQ.b, Sm.b])
                            tsl = slice(s_ * L + tbk * TW, s_ * L + (tbk + 1) * TW)
                            kb.stt(t32[:, n_ % 2, :], zf[:, jc, tsl], skp[:, jc:jc + 1], pst[:, 0:TW], ALU.mult, ALU.add,
                                   [zf.b, skp.b, pst.b], [tbf[n_ % 2]])
                            kb.tt(mix[:, 6 + jc, tsl], t32[:, n_ % 2, :], x0c[:, jc, tsl], ALU.mult, [tbf[n_ % 2], x0c.b], [mix.b])
                            n_ += 1

    def att_mixer(phm, g, l, x, mix, T_, L, nseq, gi):
        with kb.phase() as ph:
            qT = ph.sb("qT", [128, 12, T_], BF16)
            kT = ph.sb("kT", [128, 4, T_], BF16)
            kb.memset(qT[:], 0.0, [qT.b])
            kb.memset(kT[:], 0.0, [kT.b])
            vtm = ph.sb("vtm", [128, T_ // 128, 256], BF16)
            esink = T(esink_all.t, "esink")
            esink.b = esink_all.b
            esink_l = l
            if cfg.get("a_stop", 9) <= 1:
                return
            with kb.phase() as p1:
                h = make_h(p1, x, T_, l, gi, 0, "a")
                if cfg.get("a_stop", 9) <= 2:
                    return
                xk = lambda kc, sl: (h[:, kc, sl], [h.b])
                tiles = [(C_Q + 64 * i, 64) for i in range(12)] + [(C_K + 64 * i, 64) for i in range(4)]
                if g == "P":
                    kst = p1.sb("kst", [64, 4, T_])

                    def evqk(ci, tb, pst, w, sl):
                        ce = "dve"
                        if ci < 12:
                            kb.copy(qT[0:64, ci, sl], pst[0:64, :], [pst.b], [qT.b], eng=ce)
                        else:
                            kb.copy(kT[0:64, ci - 12, sl], pst[0:64, :], [pst.b], [kT.b], eng=ce)
                            kb.copy(kst[:, ci - 12, sl], pst[0:64, :], [pst.b], [kst.b])
                    proj(p1, w_in[l], tiles, xk, KC, T_, evqk, "wqk")
                    if not cfg.get("no_newk", 0):
                        kb.dma("sp", newk[l].rearrange("h d t -> d h t"), kst[:], reads=[kst.b])
                else:
                    rc = p1.sb("rc", [64, T_])
                    rs = p1.sb("rs", [64, T_])
                    kb.dma("sp", rc[:], rope_c, writes=[rc.b])
                    kb.dma("sp", rs[:], rope_s, writes=[rs.b])
                    rt = p1.sb("rt", [64, 2, TB])
                    tiles2 = [((C_Q + 64 * i, 64 * i), 64) for i in range(12)] + [((C_K + 64 * i, 768 + 64 * i), 64) for i in range(4)]

                    def evqk(ci, tb, psts, w, sl):
                        dst = qT[0:64, ci, sl] if ci < 12 else kT[0:64, ci - 12, sl]
                        db = qT.b if ci < 12 else kT.b
                        kb.tt(rt[:, 0, :], psts[0][0:64, :], rc[:, sl], ALU.mult, [psts[0].b, rc.b], [rt.b])
                        kb.tt(rt[:, 1, :], psts[1][0:64, :], rs[:, sl], ALU.mult, [psts[1].b, rs.b], [rt.b])
                        kb.tt(dst, rt[:, 0, :], rt[:, 1, :], ALU.add, [rt.b], [db])
                    proj(p1, [w_in[l], w_qkp[l]], tiles2, xk, KC, T_, evqk, "wqk", nslots=2)
                if cfg.get("a_stop", 9) <= 3:
                    return
                wv_ = p1.sb("wv", [128, KC, 256], BF16)
                kb.dma("pool", wv_[:], w_in[l].rearrange("(kc p) n -> p kc n", p=128)[:, :, C_V:C_V + 256], writes=[wv_.b])
                if g == "P":
                    vst = p1.sb("vst", [128, T_ // 128, 256])
                for tt_ in range(T_ // 128):
                    pst = kb.ps()
                    for kc in range(KC):
                        kb.mm(pst, pst[:, 0:256], h[:, kc, tt_ * 128:(tt_ + 1) * 128], wv_[:, kc, :], kc == 0, kc == KC - 1, [h.b, wv_.b])
                    kb.copy(vtm[:, tt_, :], pst[:, 0:256], [pst.b], [vtm.b], eng="act")
                    if g == "P":
                        kb.copy(vst[:, tt_, :], pst[:, 0:256], [pst.b, vtm.b], [vst.b])
                if g == "P":
                    kb.dma("sp", newv[l].rearrange("(a p) c -> p a c", p=128), vst[:], reads=[vst.b])
            if cfg.get("att_noscore", 0):
                return
            with kb.phase() as p2:
                NPT = 4
                pT = [p2.sb(f"pT{i}", [128, 512], BF16) for i in range(NPT)]
                pti = [0]
                rden = p2.sb("rden", [128, 512])
                if g == "S":
                    kcx = p2.sb("kcx", [128, 4, 512], BF16)
                    kb.memset(kcx[:], 0.0, [kcx.b])
                    vcx = p2.sb("vcx", [128, 4, 256], BF16)
                    wm = p2.sb("wm", [128, 6, 512], BF16)
                    kb.dma("pool", kcx[0:64], kctxT[l].rearrange("h d t -> d h t"), writes=[kcx.b])
                    kb.dma("pool", vcx[:], vctx[l].rearrange("(a p) c -> p a c", p=128), writes=[vcx.b])
                    kb.dma("sp", wm[:], wmask_d.rearrange("r p q -> p r q"), writes=[wm.b])
                QW = 512 if g == "S" else 256
                for qg in range(T_ // QW):
                    q0 = qg * QW
                    for hq in range(12):
                        kvh = hq // 3
                        pb = 64 * (hq % 2)
                        kts = []
                        if g == "S":
                            for j in range(4):
                                kts.append((kcx[:, kvh, j * 128:(j + 1) * 128], [kcx.b], vcx[:, j, kvh * 64:(kvh + 1) * 64], [vcx.b], None))
                            for jt in range(max(0, 4 * qg - 1), min(8, 4 * qg + 5)):
                                kts.append((kT[:, kvh, jt * 128:(jt + 1) * 128], [kT.b], vtm[:, jt, kvh * 64:(kvh + 1) * 64], [vtm.b],
                                            wm[:, jt - 4 * qg + 1, :]))
                        else:
                            for j in range(2):
                                jt = qg * 2 + j
                                kts.append((kT[:, kvh, jt * 128:(jt + 1) * 128], [kT.b], vtm[:, jt, kvh * 64:(kvh + 1) * 64], [vtm.b], None))
                        po = kb.psum[2 * (hq % 2)]
                        pd = kb.psum[2 * (hq % 2) + 1]
                        nk_ = len(kts)
                        for i, (kap, kbufs, vap, vbufs, mask) in enumerate(kts):
                            pss = kb.psum[4 + pti[0] % 4]
                            kb.mm(pss, pss[:, 0:QW], kap, qT[:, hq, q0:q0 + QW], True, True, kbufs + [qT.b])
                            pt = pT[pti[0] % NPT]
                            pti[0] += 1
                            kb.act(pt[:, 0:QW], pss[:, 0:QW], AF.Exp, [pss.b], [pt.b], scale=0.125)
                            if mask is not None:
                                kb.tt(pt[:, 0:QW], pt[:, 0:QW], mask, ALU.mult, [pt.b, wm.b], [pt.b])
                            kb.mm(po, po[pb:pb + 64, 0:QW], vap, pt[:, 0:QW], i == 0, i == nk_ - 1, vbufs + [pt.b], tp=(0, pb))
                            kb.mm(pd, pd[pb:pb + 64, 0:QW], ones_bf[:, 0:64], pt[:, 0:QW], i == 0, i == nk_ - 1, [ones_bf.b, pt.b], tp=(0, pb))
                        kb.ts(rden[pb:pb + 64, 0:QW], pd[pb:pb + 64, 0:QW], esink_all[pb:pb + 64, l, hq:hq + 1], ALU.add, [pd.b, esink.b], [rden.b])
                        kb.op("dve", lambda e: e.reciprocal(out=rden[pb:pb + 64, 0:QW], in_=rden[pb:pb + 64, 0:QW]), [rden.b], [rden.b])
                        kb.tt(mix[pb:pb + 64, hq // 2, q0:q0 + QW], po[pb:pb + 64, 0:QW], rden[pb:pb + 64, 0:QW], ALU.mult,
                              [po.b, rden.b], [mix.b])

    for g in cfg.get("groups", ("S", "P")):
        run_group(g)
    kb.barrier()
    return kb


def _fm(v, p=128):
    v = np.asarray(v)
    n = v.shape[-1] // p
    return np.ascontiguousarray(np.moveaxis(v.reshape(v.shape[:-1] + (n, p)), -1, 0))


def host_consts():
    c = {}
    c["c_ident"] = np.eye(128, dtype=np.float32)
    b = np.zeros((128, 128), np.float32)
    b[:64, :64] = 1
    b[64:, 64:] = 1
    c["c_bones"] = b
    t = np.arange(1024)
    rows, cols = (t // 64).astype(np.float64), (t % 64).astype(np.float64)
    freqs = 10000.0 ** (-np.arange(0, 32, 2, dtype=np.float64) / 32)
    rc = np.zeros((64, 1024)); rs = np.zeros((64, 1024))
    for d in range(64):
        pos = rows if d < 32 else cols
        ang = pos * freqs[d % 16]
        rc[d] = np.cos(ang)
        rs[d] = np.sin(ang) * (-1.0 if (d % 32) < 16 else 1.0)
    c["c_rope_cos"] = rc.astype(np.float32)
    c["c_rope_sin"] = rs.astype(np.float32)
    wm = np.zeros((6, 128, 512), np.float32)
    kk_, qq_ = np.meshgrid(np.arange(128), np.arange(512), indexing="ij")
    for r in range(6):
        rel = r - 1
        wm[r] = (np.abs(qq_ - rel * 128 - kk_) <= 128)
    c["c_wmask"] = wm.astype(ml_dtypes.bfloat16)
    sm = np.ones((128, 1024), np.float32)
    sm[:, ::128] = 0.0
    c["c_scanmask"] = sm
    ss_, tt_ = np.meshgrid(np.arange(128), np.arange(128), indexing="ij")
    rwm = np.zeros((2, 128, 512), np.float32)
    rwa = np.zeros((2, 128, 128), np.float32)
    for d in range(2):
        strict = (ss_ < tt_) if d == 0 else (ss_ > tt_)
        incl = (ss_ <= tt_) if d == 0 else (ss_ >= tt_)
        rwm[d] = np.concatenate([-1.0 * strict, 1.0 * incl, 1.0 * strict, 1.0 * incl], axis=1)
        rwa[d] = -1.0 * strict.T
    c["c_rwmask"] = rwm.astype(ml_dtypes.bfloat16)
    c["c_rwmaskA"] = rwa.astype(ml_dtypes.bfloat16)
    c.update(hy_consts(1024, "S"))
    c.update(hy_consts(256, "P"))
    return c


def hy_consts(L, g):
    c = {}
    bf = ml_dtypes.bfloat16
    t = np.arange(L, dtype=np.float64)
    t01 = t / max(L - 1, 1)
    bands = np.linspace(1e-4, 15, 16)
    ang = (2.0 * math.pi / L) * t[:, None] * bands[None, :]
    feat = np.concatenate([t01[:, None], np.cos(ang), -np.sin(ang)], axis=-1)
    c[f"c_featT_{g}"] = np.ascontiguousarray(feat.T).astype(np.float32)
    c[f"c_t01_{g}"] = np.ascontiguousarray((-t01).reshape(L // 128, 128).T).astype(np.float32)
    f = np.arange(L, dtype=np.float64)
    A = math.pi * np.outer(t, f) / L
    sgn = np.where(np.arange(L) % 2 == 0, 1.0, -1.0)
    C = np.cos(A)
    Sf = np.sin(A)
    Sf[:, 0] = sgn
    sc = np.full(L, 1.0 / L)
    sc[0] = 0.5 / L
    Ch = C * sc[None, :]
    Sh = Sf * sc[None, :]
    Cb = Ch.copy()
    Cb[0, :] = 0
    Sb = -np.sin(A) * sc[None, :]
    Sb[:, 0] = sgn * sc[0]
    Sb[0, :] = 0
    c[f"c_dftC_{g}"] = C.astype(bf)
    c[f"c_dftS_{g}"] = Sf.astype(bf)
    c[f"c_dftSi_{g}"] = np.ascontiguousarray(Sf.T).astype(bf)
    c[f"c_dftCh_{g}"] = Ch.astype(bf)
    c[f"c_dftSh_{g}"] = Sh.astype(bf)
    c[f"c_dftCb_{g}"] = Cb.astype(bf)
    c[f"c_dftSb_{g}"] = Sb.astype(bf)
    return c


def _perm64():
    p = np.zeros(64, np.int64)
    for d in range(64):
        p[d] = d + 16 if (d % 32) < 16 else d - 16
    return p


_CACHE = {}


def kernel(**inp):
    cfg = inp.pop("_cfg", {})
    key = repr(sorted(cfg.items()))
    if key not in _CACHE:
        _CACHE[key] = build(cfg)
    kb = _CACHE[key]
    f32 = np.float32
    A = {k: np.asarray(v) for k, v in inp.items()}
    shared = dict(host_consts())
    for k in ("ada_w", "w_in", "w_out", "mlp_w1", "mlp_w2", "hy_f1", "hy_f2", "hy_f3", "rw_w_up", "rw_a_up", "rw_g_up"):
        shared[k] = np.ascontiguousarray(A[k], dtype=f32)
    pm = _perm64()
    qk_idx = np.concatenate([h * 64 + pm for h in range(12)] + [768 + h * 64 + pm for h in range(4)])
    qk_idx = np.concatenate([qk_idx, np.arange(64)])
    shared["w_qkp"] = np.ascontiguousarray(A["w_in"][:, :, qk_idx], dtype=f32)
    shared["sink_bc"] = np.ascontiguousarray(np.broadcast_to(A["attn_sink"][:, None, :], (DEPTH, 128, 12)), dtype=f32)
    shared["rw_short_fm"] = np.ascontiguousarray(np.stack([_fm(A["rw_short_w"][l]).transpose(0, 2, 1) for l in range(DEPTH)]))
    shared["rw_w0_fm"] = np.ascontiguousarray(np.stack([_fm(A["rw_w0"][l]) for l in range(DEPTH)]))
    shared["rw_a0_fm"] = np.ascontiguousarray(np.stack([_fm(A["rw_a0"][l]) for l in range(DEPTH)]))
    shared["rw_vec_fm"] = np.ascontiguousarray(np.stack([_fm(np.stack([A["rw_k_k"][l], A["rw_k_a"][l], A["rw_r_k"][l].reshape(768),
                                                                        A["rw_gn_w"][l], A["rw_gn_b"][l]])) for l in range(DEPTH)]))
    shared["hy_short_fm"] = np.ascontiguousarray(np.stack([_fm(A["hy_short_w"][l]).transpose(0, 2, 1) for l in range(DEPTH)]))
    shared["hy_b1"] = np.ascontiguousarray(A["hy_b1"].reshape(DEPTH, 64, 1))
    shared["hy_b2"] = np.ascontiguousarray(A["hy_b2"].reshape(DEPTH, 64, 1))
    shared["hy_decay_bc"] = np.ascontiguousarray(np.broadcast_to(A["hy_decay"][:, None, :], (DEPTH, 128, 1024)), dtype=f32)
    shared["hy_skip_fm"] = np.ascontiguousarray(np.stack([_fm(A["hy_skip"][l, 0]) for l in range(DEPTH)]))
    shared["ada_b_fm"] = np.ascontiguousarray(np.stack([_fm(A["ada_b"][l]) for l in range(DEPTH)]))
    shared["norm_w_fm"] = np.ascontiguousarray(np.stack([_fm(A["norm_w"][l]) for l in range(DEPTH)]))
    in_maps = []
    for c in range(8):
        m = dict(shared)
        m["xT_s"] = np.ascontiguousarray(A["x_sample"][c].T)
        m["xT_p"] = np.ascontiguousarray(A["x_prompt"][2 * c:2 * c + 2].reshape(512, D).T)
        cond = np.stack([A["c"][c], A["c_ctx"]], axis=-1)
        m["kctxT"] = np.ascontiguousarray(A["cache_k"][c].transpose(0, 2, 3, 1))
        m["vctx"] = np.ascontiguousarray(A["cache_v"][c].reshape(DEPTH, 512, 256))
        m["st0T"] = np.ascontiguousarray(A["state_rwkv"][c].transpose(0, 1, 2, 4, 3))
        m["condT"] = np.ascontiguousarray(cond.reshape(KC, 128, 2).transpose(1, 0, 2))
        in_maps.append(m)
    names = set(kb.dram_in.keys())
    in_maps = [{k: v for k, v in m.items() if k in names} for m in in_maps]
    missing = names - set(in_maps[0].keys())
    for k in missing:
        ap = kb.dram_in[k]
        dt = ml_dtypes.bfloat16 if ap.dtype == BF16 else f32
        for m in in_maps:
            m[k] = np.zeros(ap.shape, dt)
    ncores = cfg.get("ncores", 8)
    if cfg.get("trace", 0):
        res = run_bass_kernel_spmd(kb.nc, in_maps[:ncores], core_ids=list(range(ncores)), trace=True)
        print("EXEC_NS", res.exec_time_ns, "n_ins", kb.n_ins, "cnt", kb.cnt)
    else:
        res = run_bass_kernel_spmd(kb.nc, in_maps[:ncores], core_ids=list(range(ncores)))
    R = list(res.results)
    while len(R) < 8:
        R.append(R[0])
    y_s = np.stack([R[c]["yT_s"].T for c in range(8)])
    y_p = np.concatenate([R[c]["yT_p"].T.reshape(2, 256, D) for c in range(8)])
    nk = np.zeros((16, DEPTH, 256, 4, 64), f32)
    nv = np.zeros((16, DEPTH, 256, 4, 64), f32)
    ns = np.zeros((16, DEPTH, 2, 12, 64, 64), f32)
    for c in range(8):
        kT = R[c]["newkT"]
        v_ = R[c]["newv"]
        sT = R[c]["newstT"]
        for s in range(2):
            nk[2 * c + s] = kT[:, :, :, s * 256:(s + 1) * 256].transpose(0, 3, 1, 2)
            nv[2 * c + s] = v_[:, s * 256:(s + 1) * 256, :].reshape(DEPTH, 256, 4, 64)
            ns[2 * c + s] = sT[:, s].transpose(0, 1, 2, 4, 3)
    return (y_p.astype(f32), y_s.astype(f32), nk, nv, ns)
```

```python
import math
from contextlib import ExitStack, contextmanager
import numpy as np
import ml_dtypes
import concourse.bass as bass
import concourse.mybir as mybir
from concourse.bass_utils import run_bass_kernel_spmd

F32 = mybir.dt.float32
BF16 = mybir.dt.bfloat16
AF = mybir.ActivationFunctionType
ALU = mybir.AluOpType

D = 2048
KC = 16
DEPTH = 2
TB = 512
CH = 128
IN_COLS = 5760
C_Q, C_K, C_V, C_HY, C_RKV, C_WDN, C_ADN, C_GDN = 0, 768, 1024, 1280, 2816, 5120, 5312, 5504
MAGIC = 12582912.0
SAME_SYNC = False


_KB = [None]


class Buf:
    __slots__ = ("name", "w", "r", "dsem", "dcnt")

    def __init__(self, name):
        self.name = name
        self.w = None
        self.r = {}
        self.dsem = None
        self.dcnt = 0
        kb = _KB[0]
        if kb is not None:
            if kb.freed:
                self.r = {"f%d" % k: t for k, t in kb.freed.items()}
            if kb.phase_all:
                kb.phase_all[-1].append(self)


class T:
    def __init__(self, t, name):
        self.t = t
        self.b = Buf(name)

    def __getitem__(self, key):
        return self.t[key]


class KB:
    def __init__(self):
        self.nc = bass.Bass("TRN2", target_bir_lowering=False)
        nc = self.nc
        self.E = {"pe": nc.tensor, "act": nc.scalar, "dve": nc.vector, "pool": nc.gpsimd, "sp": nc.sync}
        self.sem = {e: nc.alloc_semaphore("prog_" + e) for e in self.E}
        self.cnt = {e: 0 for e in self.E}
        self.waited = {e: {} for e in self.E}
        self.semkey = {}
        self.dma_latest = {}
        self.n_ins = 0
        self.dram_in = {}
        self.dram_out = {}
        self.psum = []
        self.ps_i = 0
        self.uid = 0
        self.free_dsems = []
        self.live = []
        self.freed = {}
        self.phase_all = []
        self.retired = []
        _KB[0] = self
        self.keep = []
        self.dpool = {"sp": [], "pool": []}
        self.phase_bufs = []

    def _key(self, sem):
        k = self.semkey.get(id(sem))
        if k is None:
            k = len(self.semkey) + 1
            self.semkey[id(sem)] = k
            self.keep.append(sem)
        return k

    def _wait(self, eng, deps):
        for (sem, val, owner) in deps:
            if owner == eng:
                if eng == "pe" or not SAME_SYNC:
                    continue
                if val > self.cnt[eng]:
                    continue
            elif owner in self.cnt and val > self.cnt[owner]:
                raise RuntimeError(f"wait on pending count: {eng} waits {owner} {val} > {self.cnt[owner]}")
            k = self._key(sem)
            if self.waited[eng].get(k, 0) >= val:
                continue
            self.E[eng].wait_ge(sem, val)
            self.waited[eng][k] = val

    def _deps(self, reads, writes):
        deps = []
        for b in reads:
            if b.w is not None:
                deps.append(b.w)
        for b in writes:
            if b.w is not None:
                deps.append(b.w)
            deps.extend(b.r.values())
        return deps

    def op(self, eng, fn, reads=(), writes=(), inc=True):
        self._wait(eng, self._deps(reads, writes))
        ins = fn(self.E[eng])
        self.n_ins += 1
        if inc:
            ins.then_inc(self.sem[eng], 1)
            self.cnt[eng] += 1
            tag = (self.sem[eng], self.cnt[eng], eng)
        else:
            tag = (self.sem[eng], self.cnt[eng] + 1, eng)
        for b in reads:
            b.r[eng] = tag
        for b in writes:
            b.w = tag
            b.r = {}
        return ins

    def dma(self, q, out, in_, reads=(), writes=()):
        self._wait(q, self._deps(reads, writes))
        owner = (list(writes) + list(reads))[0]
        if owner.dsem is None:
            if self.dpool[q]:
                owner.dsem = self.dpool[q].pop()
            else:
                owner.dsem = [self.nc.alloc_semaphore(f"d{self.uid}"), 0, q]
                self.uid += 1
            if self.phase_bufs:
                self.phase_bufs[-1].append(owner)
        assert owner.dsem[2] == q, "a Buf's DMA semaphore must stay on one queue type"
        ins = self.E[q].dma_start(out=out, in_=in_)
        ins.then_inc(owner.dsem[0], 16)
        owner.dsem[1] += 16
        self.n_ins += 1
        tag = (owner.dsem[0], owner.dsem[1], "dma")
        self.dma_latest[self._key(owner.dsem[0])] = tag
        for b in reads:
            b.r["dma%d" % id(owner)] = tag
        for b in writes:
            b.w = tag
            b.r = {}

    def barrier(self):
        tags = [(self.sem[f], self.cnt[f], f) for f in self.E if self.cnt[f] > 0]
        tags += list(self.dma_latest.values())
        for e in self.E:
            self._wait(e, [t for t in tags if t[2] != e])
        self.freed = {}
        for ds in self.retired:
            self.dpool[ds[2]].append(ds)
        self.retired = []

    def din(self, name, shape, dtype=F32):
        self.dram_in[name] = self.nc.dram_tensor(name, list(shape), dtype, kind="ExternalInput").ap()
        return self.dram_in[name]

    def dout(self, name, shape):
        self.dram_out[name] = self.nc.dram_tensor(name, list(shape), F32, kind="ExternalOutput").ap()
        return self.dram_out[name]

    def sb(self, name, shape, dtype=F32):
        self.uid += 1
        self.live.append((name, int(np.prod(shape[1:])) * (2 if dtype == BF16 else 4)))
        return T(self.nc.alloc_sbuf_tensor(f"{name}_{self.uid}", list(shape), dtype), name)

    @contextmanager
    def phase(self, barrier=False):
        st = ExitStack()
        self.phase_all.append([])
        kb = self

        class Ph:
            def sb(self_, name, shape, dtype=F32):
                kb.uid += 1
                try:
                    t = st.enter_context(kb.nc.sbuf_tensor(f"{name}_{kb.uid}", list(shape), dtype))
                except Exception:
                    print("LIVE:", [(n, b) for n, b in kb.live])
                    raise
                nb = int(np.prod(shape[1:])) * (2 if dtype == BF16 else 4)
                kb.live.append((name, nb))
                st.callback(lambda: kb.live.remove((name, nb)))
                return T(t, name)
        self.phase_bufs.append([])
        try:
            yield Ph()
        finally:
            allb = self.phase_all.pop()
            if barrier:
                self.barrier()
                for b in self.phase_bufs.pop():
                    self.dpool[b.dsem[2]].append(b.dsem)
                    b.dsem = None
            else:
                for b in allb:
                    for t in ([b.w] if b.w is not None else []) + list(b.r.values()):
                        if t[2] in self.cnt and t[1] > self.cnt[t[2]]:
                            t = (t[0], self.cnt[t[2]] + 1, t[2])
                        k = self._key(t[0])
                        if k not in self.freed or self.freed[k][1] < t[1]:
                            self.freed[k] = t
                for b in self.phase_bufs.pop():
                    self.retired.append(b.dsem)
                    b.dsem = None
            st.close()

    def init_psum(self):
        for i in range(8):
            self.psum.append(T(self.nc.alloc_psum_tensor(f"ps{i}", [128, 512], F32), f"ps{i}"))

    def ps(self):
        p = self.psum[self.ps_i % 8]
        self.ps_i += 1
        return p

    def mm(self, ps, out, lhsT, rhs, start, stop, reads, inc=None, tp=None):
        if inc is None:
            inc = stop
        kw = {}
        if tp is not None:
            kw["tile_position"] = tp
        return self.op("pe", lambda e: e.matmul(out, lhsT=lhsT, rhs=rhs, start=start, stop=stop, **kw),
                       reads=reads, writes=[ps.b], inc=inc)

    def tr(self, ps, out, in_, ident, reads):
        return self.op("pe", lambda e: e.transpose(out=out, in_=in_, identity=ident), reads=reads, writes=[ps.b])

    def act(self, out, in_, func, reads, writes, bias=None, scale=None, eng="act"):
        kw = {}
        if bias is not None:
            kw["bias"] = bias
        if scale is not None:
            kw["scale"] = scale
        return self.op("act", lambda e: e.activation(out=out, in_=in_, func=func, **kw), reads=reads, writes=writes)

    def tt(self, out, in0, in1, op, reads, writes, eng="dve"):
        return self.op(eng, lambda e: e.tensor_tensor(out=out, in0=in0, in1=in1, op=op), reads=reads, writes=writes)

    def ts(self, out, in0, s1, op0, reads, writes, s2=None, op1=None, eng="dve"):
        if op1 is None:
            return self.op(eng, lambda e: e.tensor_scalar(out=out, in0=in0, scalar1=s1, scalar2=None, op0=op0),
                           reads=reads, writes=writes)
        return self.op(eng, lambda e: e.tensor_scalar(out=out, in0=in0, scalar1=s1, scalar2=s2, op0=op0, op1=op1),
                       reads=reads, writes=writes)

    def stt(self, out, in0, scalar, in1, op0, op1, reads, writes):
        return self.op("dve", lambda e: e.scalar_tensor_tensor(out=out, in0=in0, scalar=scalar, in1=in1, op0=op0, op1=op1),
                       reads=reads, writes=writes)

    def copy(self, out, in_, reads, writes, eng="dve"):
        if eng == "act":
            return self.act(out, in_, AF.Identity, reads, writes)
        return self.op(eng, lambda e: e.tensor_copy(out=out, in_=in_), reads=reads, writes=writes)

    def memset(self, out, val, writes, eng="dve"):
        return self.op(eng, lambda e: e.memset(out, val), writes=writes)


class WRing:
    def __init__(self, kb, ph, name, nslots, shape):
        self.kb = kb
        self.slots = [ph.sb(f"{name}{i}", shape, BF16) for i in range(nslots)]
        self.i = 0

    def load(self, dram_view, view=None):
        s = self.slots[self.i % len(self.slots)]
        self.i += 1
        dst = s.t[:] if view is None else view(s)
        self.kb.dma("pool", dst, dram_view, writes=[s.b])
        return s


def stream(kb, ring, views, dstview=None):
    n = len(views)
    ns = len(ring.slots)
    loaded = []
    nxt = 0
    for i in range(n):
        while nxt < n and nxt < i + ns:
            loaded.append(ring.load(views[nxt], dstview))
            nxt += 1
        yield i, loaded[i]


def build(cfg):
    kb = KB()
    nc = kb.nc
    kb.init_psum()
    do_att, do_hy, do_rw = cfg.get("att", 1), cfg.get("hy", 1), cfg.get("rw", 1)

    xT = {"S": kb.din("xT_s", [D, 1024]), "P": kb.din("xT_p", [D, 512])}
    yT = {"S": kb.dout("yT_s", [D, 1024]), "P": kb.dout("yT_p", [D, 512])}
    condT = kb.din("condT", [128, KC, 2])
    ada_w = kb.din("ada_w", [DEPTH, D, 6 * D])
    ada_b = kb.din("ada_b_fm", [DEPTH, 128, 96])
    norm_w = kb.din("norm_w_fm", [DEPTH, 128, 4, KC])
    w_in = kb.din("w_in", [DEPTH, D, IN_COLS])
    w_qkp = kb.din("w_qkp", [DEPTH, D, 1088])
    w_out = kb.din("w_out", [DEPTH, D, D])
    mlp_w1 = kb.din("mlp_w1", [DEPTH, D, 4 * D])
    mlp_w2 = kb.din("mlp_w2", [DEPTH, 4 * D, D])
    consts = {}

    def cin(name, shape, dtype=F32):
        consts[name] = kb.din(name, shape, dtype)
        return consts[name]
    ident_d = cin("c_ident", [128, 128])
    bones_d = cin("c_bones", [128, 128])
    kctxT = kb.din("kctxT", [DEPTH, 4, 64, 512])
    vctx = kb.din("vctx", [DEPTH, 512, 256])
    sink_d = kb.din("sink_bc", [DEPTH, 128, 12])
    rope_c = cin("c_rope_cos", [64, 1024])
    rope_s = cin("c_rope_sin", [64, 1024])
    wmask_d = cin("c_wmask", [6, 128, 512], BF16)
    newk = kb.dout("newkT", [DEPTH, 4, 64, 512])
    newv = kb.dout("newv", [DEPTH, 512, 256])
    hy_short = kb.din("hy_short_fm", [DEPTH, 128, 12, 3])
    hy_f1 = kb.din("hy_f1", [DEPTH, 33, 64])
    hy_b1 = kb.din("hy_b1", [DEPTH, 64, 1])
    hy_f2 = kb.din("hy_f2", [DEPTH, 64, 64])
    hy_b2 = kb.din("hy_b2", [DEPTH, 64, 1])
    hy_f3 = kb.din("hy_f3", [DEPTH, 64, 1024])
    hy_decay = kb.din("hy_decay_bc", [DEPTH, 128, 1024])
    hy_skip = kb.din("hy_skip_fm", [DEPTH, 128, 4])
    hyc = {}
    for g, L in (("S", 1024), ("P", 256)):
        hyc[g] = dict(
            featT=cin(f"c_featT_{g}", [33, L]), t01=cin(f"c_t01_{g}", [128, L // 128]),
            C=cin(f"c_dftC_{g}", [L, L], BF16), S=cin(f"c_dftS_{g}", [L, L], BF16),
            Ch=cin(f"c_dftCh_{g}", [L, L], BF16), Sh=cin(f"c_dftSh_{g}", [L, L], BF16),
            Cb=cin(f"c_dftCb_{g}", [L, L], BF16), Sb=cin(f"c_dftSb_{g}", [L, L], BF16), Si=cin(f"c_dftSi_{g}", [L, L], BF16))
    rw_short = kb.din("rw_short_fm", [DEPTH, 128, 18, 3])
    rw_w0 = kb.din("rw_w0_fm", [DEPTH, 128, 2, 6])
    rw_a0 = kb.din("rw_a0_fm", [DEPTH, 128, 2, 6])
    rw_wup = kb.din("rw_w_up", [DEPTH, 2, 96, 768])
    rw_aup = kb.din("rw_a_up", [DEPTH, 2, 96, 768])
    rw_gup = kb.din("rw_g_up", [DEPTH, 256, 768])
    rw_vec = kb.din("rw_vec_fm", [DEPTH, 128, 5, 6])
    st0T = kb.din("st0T", [DEPTH, 2, 12, 64, 64])
    newst = kb.dout("newstT", [DEPTH, 2, 2, 12, 64, 64])
    scanmask_d = cin("c_scanmask", [128, 1024])
    rwmask_d = cin("c_rwmask", [2, 128, 512], BF16)
    rwmaskA_d = cin("c_rwmaskA", [2, 128, 128], BF16)

    ident = kb.sb("ident", [128, 128])
    bones = kb.sb("bones", [128, 128])
    ones_bf = kb.sb("ones_bf", [128, 128], BF16)
    mod = kb.sb("mod", [128, DEPTH, 2, 6, KC])
    nw = kb.sb("nw", [128, DEPTH, 4, KC])
    dmod = kb.sb("dmod", [128, DEPTH, 2, 6, KC])
    epsb = kb.sb("epsb", [128, 1])
    negpi = kb.sb("negpi", [128, 1])
    kb.dma("sp", ident[:], ident_d, writes=[ident.b])
    kb.dma("sp", bones[:], bones_d, writes=[bones.b])
    kb.dma("sp", nw[:], norm_w.rearrange("l p a c -> p l a c"), writes=[nw.b])
    esink_raw = kb.sb("esink_raw", [128, DEPTH, 12])
    esink_all = kb.sb("esink_all", [128, DEPTH, 12])
    kb.dma("sp", esink_raw[:], sink_d.rearrange("l p h -> p l h"), writes=[esink_raw.b])
    kb.act(esink_all[:], esink_raw[:], AF.Exp, [esink_raw.b], [esink_all.b])
    kb.memset(ones_bf[:], 1.0, [ones_bf.b])
    kb.memset(epsb[:], 1e-6, [epsb.b])
    kb.memset(negpi[:], -math.pi, [negpi.b])

    with kb.phase() as ph:
        cnd = ph.sb("cnd", [128, KC, 2])
        cndb = ph.sb("cndb", [128, KC, 2], BF16)
        adab = ph.sb("adab", [128, DEPTH, 96])
        kb.dma("sp", cnd[:], condT, writes=[cnd.b])
        kb.dma("sp", adab[:], ada_b.rearrange("l p j -> p l j"), writes=[adab.b])
        kb.act(cndb[:], cnd[:], AF.Silu, [cnd.b], [cndb.b])
        ring = WRing(kb, ph, "adaw", 3, [128, KC, 512])
        for l in range(DEPTH):
            wv = ada_w[l].rearrange("(kc p) n -> p kc n", p=128)
            views = [wv[:, :, i * 512:(i + 1) * 512] for i in range(24)]
            pst = kb.ps()
            for i, slot in stream(kb, ring, views):
                for j in range(4):
                    jj = i * 4 + j
                    for kc in range(KC):
                        kb.mm(pst, pst[:, 2 * jj:2 * jj + 2], slot[:, kc, j * 128:(j + 1) * 128], cndb[:, kc, :],
                              kc == 0, kc == KC - 1, [slot.b, cndb.b])
            for g in range(2):
                kb.tt(mod[:, l, g, :, :].rearrange("p a c -> p (a c)"),
                      pst[:, 0:192].rearrange("p (j g) -> p j g", g=2)[:, :, g], adab[:, l, :], ALU.add,
                      [pst.b, adab.b], [mod.b])
        for l in range(DEPTH):
            for g in range(2):
                m = lambda i: mod[:, l, g, i, :]
                dm = lambda i: dmod[:, l, g, i, :]
                kb.stt(dm(0), m(1), 1.0, nw[:, l, 0, :], ALU.add, ALU.mult, [mod.b, nw.b], [dmod.b])
                kb.copy(dm(1), m(0), [mod.b], [dmod.b])
                kb.tt(dm(2), m(2), nw[:, l, 1, :], ALU.mult, [mod.b, nw.b], [dmod.b])
                kb.stt(dm(3), m(4), 1.0, nw[:, l, 2, :], ALU.add, ALU.mult, [mod.b, nw.b], [dmod.b])
                kb.copy(dm(4), m(3), [mod.b], [dmod.b])
                kb.tt(dm(5), m(5), nw[:, l, 3, :], ALU.mult, [mod.b, nw.b], [dmod.b])

    def rstd_from(ph, src_chunks, T_, tag, out=None):
        rstd = out if out is not None else ph.sb("rstd" + tag, [128, T_])
        sq = ph.sb("sq" + tag, [128, 2, TB], BF16)
        sqb = [Buf("sq0"), Buf("sq1")]
        for tb in range(T_ // TB):
            pst = kb.ps()
            for c in range(KC):
                ap, bufs = src_chunks(c, tb)
                kb.act(sq[:, c % 2, :], ap, AF.Square, bufs, [sqb[c % 2]])
                kb.mm(pst, pst[:, :], ones_bf[:], sq[:, c % 2, :], c == 0, c == KC - 1, [sqb[c % 2], ones_bf.b], inc=True)
            sl = slice(tb * TB, (tb + 1) * TB)
            kb.act(rstd[:, sl], pst[:, :], AF.Ln, [pst.b, epsb.b], [rstd.b], bias=epsb[:], scale=1.0 / D)
            kb.act(rstd[:, sl], rstd[:, sl], AF.Exp, [rstd.b], [rstd.b], scale=-0.5)
        return rstd

    rstd_cache = {}

    def make_h(ph, x, T_, l, g, which, tag):
        h = ph.sb("h" + tag, [128, KC, T_], BF16)
        with kb.phase() as p2:
            if "r" in rstd_cache:
                rstd = rstd_cache["r"]
            else:
                rstd = rstd_from(p2, lambda c, tb: (x[:, c, tb * TB:(tb + 1) * TB], [x.b]), T_, tag)
            tmp = p2.sb("htmp", [128, 2, T_])
            tb_ = [Buf("t0"), Buf("t1")]
            for c in range(KC):
                kb.tt(tmp[:, c % 2, :], x[:, c, :], rstd[:], ALU.mult, [x.b, rstd.b], [tb_[c % 2]])
                kb.act(h[:, c, :], tmp[:, c % 2, :], AF.Identity, [tb_[c % 2], dmod.b], [h.b],
                       bias=dmod[:, l, g, which + 1, c:c + 1], scale=dmod[:, l, g, which, c:c + 1])
        return h


    def proj(ph, wd, ncols_list, xk, nkc, T_, evac, name, pk=128, nslots=3, tilew=None, loadw=None, group=1):
        wds = wd if isinstance(wd, (list, tuple)) else [wd]
        ns_ = len(wds)
        tilew = tilew or max(w for _, w in ncols_list)
        ring = WRing(kb, ph, name, nslots, [pk, nkc, ns_, group * tilew])
        wvs = [w_.rearrange("(kc p) n -> p kc n", p=pk) for w_ in wds]
        n = len(ncols_list)
        ngr = (n + group - 1) // group
        loaded = []
        nxt = 0
        for ci in range(n):
            gidx = ci // group
            while nxt < ngr and nxt < gidx + nslots:
                s_ = ring.slots[nxt % nslots]
                first = nxt * group
                c0_, _w = ncols_list[first]
                totw = sum(w__ for _, w__ in ncols_list[first:first + group])
                for j in range(ns_):
                    c0j = c0_[j] if isinstance(c0_, (list, tuple)) else c0_
                    kb.dma("pool", s_[:, :, j, 0:totw], wvs[j][:, :, c0j:c0j + totw], writes=[s_.b])
                loaded.append(s_)
                nxt += 1
            slot = loaded[gidx]
            w = ncols_list[ci][1]
            off = (ci % group) * w
            tbw = min(TB, T_)
            for tb in range(max(1, T_ // TB)):
                sl = slice(tb * tbw, (tb + 1) * tbw)
                psts = []
                for j in range(ns_):
                    pst = kb.ps()
                    for kc in range(nkc):
                        xap, xb = xk(kc, sl)
                        kb.mm(pst, pst[0:w, 0:tbw], slot[:, kc, j, off:off + w], xap, kc == 0, kc == nkc - 1, [slot.b] + xb)
                    psts.append(pst)
                evac(ci, tb, psts[0] if ns_ == 1 else psts, w, sl)

    def run_group(g):
        with kb.phase(barrier=True) as phg:
            run_group_(g, phg)

    def run_group_(g, phg):
        T_ = 1024 if g == "S" else 512
        L = 1024 if g == "S" else 256
        nseq = T_ // L
        gi = 0 if g == "S" else 1
        NTB = T_ // TB
        x = phg.sb("x" + g, [128, KC, T_])
        kb.dma("sp", x[:], xT[g].rearrange("(c p) t -> p c t", p=128), writes=[x.b])
        for l in range(DEPTH):
            with kb.phase(barrier=True) as phm:
                rwout = phm.sb("rwout", [128, 6, T_], BF16)
                rstd1 = phm.sb("rstd1", [128, T_])
                with kb.phase() as pr_:
                    rstd_from(pr_, lambda c, tb: (x[:, c, tb * TB:(tb + 1) * TB], [x.b]), T_, "c", out=rstd1)
                rstd_cache["r"] = rstd1
                if do_rw:
                    rwkv_mixer(phm, g, l, x, rwout, T_, L, nseq, gi)
                else:
                    kb.memset(rwout[:], 0.0, [rwout.b])
                mix = phm.sb("mix", [128, 10, T_], BF16)
                kb.memset(mix[:], 0.0, [mix.b])
                if do_hy:
                    hyena_mixer(phm, g, l, x, mix, T_, L, nseq, gi)
                if do_att:
                    att_mixer(phm, g, l, x, mix, T_, L, nseq, gi)
                rstd_cache.pop("r", None)
                with kb.phase() as ph:
                    resid_branch(ph, w_out[l], KC, lambda kc, sl: ((mix[:, kc, sl], [mix.b]) if kc < 10 else (rwout[:, kc - 10, sl], [rwout.b])),
                                 x, T_, l, gi, 2, "wo")
            for tb in range(NTB):
                sl = slice(tb * TB, (tb + 1) * TB)
                with kb.phase(barrier=True) as ph:
                    xs = T(x.t, "xs")
                    xs.b = x.b
                    h2 = ph.sb("h2", [128, KC, TB], BF16)
                    with kb.phase() as p2:
                        rstd = rstd_from(p2, lambda c, _tb: (x[:, c, sl], [x.b]), TB, "m")
                        tmp = p2.sb("htmp", [128, 2, TB])
                        tb_ = [Buf("t0"), Buf("t1")]
                        for c in range(KC):
                            kb.tt(tmp[:, c % 2, :], x[:, c, sl], rstd[:], ALU.mult, [x.b, rstd.b], [tb_[c % 2]])
                            kb.act(h2[:, c, :], tmp[:, c % 2, :], AF.Identity, [tb_[c % 2], dmod.b], [h2.b],
                                   bias=dmod[:, l, gi, 4, c:c + 1], scale=dmod[:, l, gi, 3, c:c + 1])
                    fbuf = ph.sb("fbuf", [128, KC, TB])
                    for half in range(2):
                        with kb.phase() as p3:
                            hid = p3.sb("hid", [128, 32, TB], BF16)
                            rtmp = p3.sb("rtmp", [128, 2, TB])
                            rb = [Buf("r0"), Buf("r1")]

                            def ev1(ci, _tb, pst, w, _sl):
                                kb.act(rtmp[:, ci % 2, :], pst[:, :], AF.Relu, [pst.b], [rb[ci % 2]])
                                kb.tt(hid[:, ci, :], rtmp[:, ci % 2, :], rtmp[:, ci % 2, :], ALU.mult, [rb[ci % 2]], [hid.b])
                            proj(p3, mlp_w1[l], [(half * 4096 + i * 128, 128) for i in range(32)],
                                 lambda kc, s_: (h2[:, kc, s_], [h2.b]), KC, TB, ev1, "w1", group=2)

                            def ev2(ci, _tb, pst, w, _sl):
                                if half == 0:
                                    kb.copy(fbuf[:, ci, :], pst[:, :], [pst.b], [fbuf.b], eng="act")
                                else:
                                    kb.tt(fbuf[:, ci, :], fbuf[:, ci, :], pst[:, :], ALU.add, [pst.b, fbuf.b], [fbuf.b])
                            ring2 = WRing(kb, p3, "w2", 3, [128, 16, 256])
                            w2v = mlp_w2[l][half * 4096:(half + 1) * 4096, :].rearrange("(kc p) n -> p kc n", p=128)
                            jobs = [(tp_, ks) for tp_ in range(8) for ks in range(2)]
                            ld = []
                            nx = 0
                            for ji, (tp_, ks) in enumerate(jobs):
                                while nx < len(jobs) and nx < ji + 3:
                                    tq, kq = jobs[nx]
                                    s2 = ring2.slots[nx % 3]
                                    kb.dma("pool", s2[:], w2v[:, kq * 16:(kq + 1) * 16, tq * 256:(tq + 1) * 256], writes=[s2.b])
                                    ld.append(s2)
                                    nx += 1
                                slot = ld[ji]
                                if ks == 0:
                                    pp2 = [kb.ps(), kb.ps()]
                                for t2 in range(2):
                                    for kc in range(16):
                                        kb.mm(pp2[t2], pp2[t2][:, :], slot[:, kc, t2 * 128:(t2 + 1) * 128], hid[:, ks * 16 + kc, :],
                                              ks == 0 and kc == 0, ks == 1 and kc == 15, [slot.b, hid.b], inc=(kc == 15))
                                if ks == 1:
                                    for t2 in range(2):
                                        ev2(tp_ * 2 + t2, 0, pp2[t2], 128, None)
                    norm_add(ph, fbuf, x, sl, TB, l, gi, 5, "f")
        kb.dma("sp", yT[g].rearrange("(c p) t -> p c t", p=128), x[:], reads=[x.b])

    def norm_add(ph, obuf, x, sl, tw, l, gi, which, tag):
        with kb.phase() as p2:
            rstd = rstd_from(p2, lambda c, _tb: (obuf[:, c, :], [obuf.b]), tw, tag)
            tmp = p2.sb("ntmp", [128, 2, tw])
            tb_ = [Buf("t0"), Buf("t1")]
            for c in range(KC):
                kb.tt(tmp[:, c % 2, :], obuf[:, c, :], rstd[:], ALU.mult, [obuf.b, rstd.b], [tb_[c % 2]])
                kb.stt(x[:, c, sl], tmp[:, c % 2, :], dmod[:, l, gi, which, c:c + 1], x[:, c, sl], ALU.mult, ALU.add,
                       [tb_[c % 2], dmod.b, x.b], [x.b])

    def resid_branch(ph, wd, nkc, xk, x, T_, l, gi, which, name):
        for tb in range(T_ // TB):
            sl = slice(tb * TB, (tb + 1) * TB)
            with kb.phase() as p1:
                obuf = p1.sb("obuf", [128, KC, TB])

                def ev(ci, _tb, pst, w, _sl):
                    kb.copy(obuf[:, ci, :], pst[:, :], [pst.b], [obuf.b], eng="act")
                proj(p1, wd, [(i * 128, 128) for i in range(KC)],
                     lambda kc, s_: xk(kc, slice(sl.start + s_.start, sl.start + s_.stop)), nkc, TB, ev, name, group=2)
                norm_add(p1, obuf, x, sl, TB, l, gi, which, "o")

    def rwkv_mixer(phm, g, l, x, rwout, T_, L, nseq, gi):
        NCH = T_ // 128
        ncs = L // 128
        NB = max(1, T_ // TB)
        BW = min(TB, T_)
        CPB = BW // 128
        C0 = math.exp(-0.5)
        c4 = lambda ap: ap.rearrange("p (c t) -> p c t", t=128)
        with kb.phase() as ph:
            wdn = ph.sb("wdn", [128, 2, T_], BF16)
            adn = ph.sb("adn", [128, 2, T_], BF16)
            gdn = ph.sb("gdn", [128, 2, T_], BF16)
            rsw = ph.sb("rsw", [128, 18, 3])
            w0 = ph.sb("w0", [128, 2, 6])
            a0 = ph.sb("a0", [128, 2, 6])
            vec = ph.sb("vec", [128, 5, 6])
            rwm = ph.sb("rwm", [128, 2, 512], BF16)
            rwmA = ph.sb("rwmA", [128, 2, 128], BF16)
            scm = ph.sb("scm", [128, BW])
            eps24 = ph.sb("eps24", [128, 1])
            epsgn = ph.sb("epsgn", [128, 1])
            identb = ph.sb("identb", [128, 128], BF16)
            kb.memset(wdn[:], 0.0, [wdn.b])
            kb.memset(adn[:], 0.0, [adn.b])
            kb.memset(eps24[:], 1e-24, [eps24.b])
            kb.memset(epsgn[:], 64e-5, [epsgn.b])
            kb.copy(identb[:], ident[:], [ident.b], [identb.b])
            kb.dma("sp", rsw[:], rw_short[l], writes=[rsw.b])
            kb.dma("sp", w0[:], rw_w0[l], writes=[w0.b])
            kb.dma("sp", a0[:], rw_a0[l], writes=[a0.b])
            kb.dma("sp", vec[:], rw_vec[l], writes=[vec.b])
            kb.dma("sp", rwm[:], rwmask_d.rearrange("d p q -> p d q"), writes=[rwm.b])
            kb.dma("sp", rwmA[:], rwmaskA_d.rearrange("d p q -> p d q"), writes=[rwmA.b])
            kb.dma("sp", scm[:], scanmask_d[:, 0:BW], writes=[scm.b])
            h_keep = make_h(ph, x, T_, l, gi, 0, "rk") if g == "P" else None
            with kb.phase() as p1:
                h = h_keep if h_keep is not None else make_h(p1, x, T_, l, gi, 0, "r")
                tiles = [(C_WDN, 96), (C_WDN + 96, 96), (C_ADN, 96), (C_ADN + 96, 96), (C_GDN, 128), (C_GDN + 128, 128)]

                def evl(ci, tb, pst, w, sl):
                    n_ = sl.stop - sl.start
                    if ci < 2:
                        kb.act(wdn[0:96, ci, sl], pst[0:96, 0:n_], AF.Tanh, [pst.b], [wdn.b])
                    elif ci < 4:
                        kb.act(adn[0:96, ci - 2, sl], pst[0:96, 0:n_], AF.Identity, [pst.b], [adn.b])
                    else:
                        kb.act(gdn[:, ci - 4, sl], pst[:, 0:n_], AF.Sigmoid, [pst.b], [gdn.b])
                proj(p1, w_in[l], tiles, lambda kc, sl: (h[:, kc, sl], [h.b]), KC, T_, evl, "wlo", nslots=2)
            wv_in = w_in[l].rearrange("(kc p) n -> p kc n", p=128)
            for j in range(6):
                with kb.phase(barrier=True) as pp:
                    KRz = [[pp.sb(f"KRz{d}{hh}", [128, NCH, 2, 128], BF16) for hh in range(2)] for d in range(2)]
                    BK = [pp.sb(f"BK{d}", [128, NCH, 2, 128], BF16) for d in range(2)]
                    BKT = [pp.sb(f"BKT{d}", [128, NCH, 2, 128], BF16) for d in range(2)]
                    etot = pp.sb("etot", [128, 2, NCH])
                    VT = pp.sb("VT", [128, NCH, 128], BF16)
                    yacc = pp.sb("yacc", [128, T_])
                    bonus = pp.sb("bonus", [128, T_], BF16)
                    ups = pp.sb("ups", [128, 3, 2, 128], BF16)
                    cv = [pp.sb(f"rcv{a}", [128, T_]) for a in range(3)]
                    kb.memset(ups[:], 0.0, [ups.b])
                    for d in range(2):
                        for hh in range(2):
                            kb.memset(KRz[d][hh][:], 0.0, [KRz[d][hh].b])
                        kb.dma("pool", ups[0:96, 0, d, :], rw_wup[l, d][:, j * 128:(j + 1) * 128], writes=[ups.b])
                        kb.dma("pool", ups[0:96, 1, d, :], rw_aup[l, d][:, j * 128:(j + 1) * 128], writes=[ups.b])
                        kb.dma("pool", ups[:, 2, d, :], rw_gup[l][d * 128:(d + 1) * 128, j * 128:(j + 1) * 128], writes=[ups.b])
                    kb.memset(yacc[:], 0.0, [yacc.b])
                    with kb.phase() as pj:
                        h = h_keep if h_keep is not None else make_h(pj, x, T_, l, gi, 0, "r")
                        wt = pj.sb("wrkv", [128, KC, 3, 128], BF16)
                        for a in range(3):
                            c0 = C_RKV + a * 768 + j * 128
                            kb.dma("pool", wt[:, :, a, :], wv_in[:, :, c0:c0 + 128], writes=[wt.b])
                        for a in range(3):
                            wi = a * 6 + j
                            pss = []
                            for tb in range(NB):
                                pst = kb.ps()
                                for kc in range(KC):
                                    kb.mm(pst, pst[:, 0:BW], wt[:, kc, a, :], h[:, kc, tb * BW:(tb + 1) * BW], kc == 0, kc == KC - 1, [wt.b, h.b])
                                pss.append(pst)
                            for tb in range(NB):
                                kb.ts(cv[a][:, tb * BW:(tb + 1) * BW], pss[tb][:, 0:BW], rsw[:, wi, 1:2], ALU.mult, [pss[tb].b, rsw.b], [cv[a].b])
                            for s_ in range(nseq):
                                for tb in range(NB):
                                    lo, hi = max(s_ * L, tb * BW), min((s_ + 1) * L, (tb + 1) * BW)
                                    if lo >= hi:
                                        continue
                                    o = tb * BW
                                    kb.stt(cv[a][:, lo + 1:hi], pss[tb][:, lo - o:hi - 1 - o], rsw[:, wi, 0:1], cv[a][:, lo + 1:hi], ALU.mult, ALU.add,
                                           [pss[tb].b, rsw.b, cv[a].b], [cv[a].b])
                                    kb.stt(cv[a][:, lo:hi - 1], pss[tb][:, lo + 1 - o:hi - o], rsw[:, wi, 2:3], cv[a][:, lo:hi - 1], ALU.mult, ALU.add,
                                           [pss[tb].b, rsw.b, cv[a].b], [cv[a].b])
                                    if lo > s_ * L:
                                        kb.stt(cv[a][:, lo:lo + 1], pss[tb - 1][:, BW - 1:BW], rsw[:, wi, 0:1], cv[a][:, lo:lo + 1], ALU.mult, ALU.add,
                                               [pss[tb - 1].b, rsw.b, cv[a].b], [cv[a].b])
                                    if hi < (s_ + 1) * L:
                                        kb.stt(cv[a][:, hi - 1:hi], pss[tb + 1][:, 0:1], rsw[:, wi, 2:3], cv[a][:, hi - 1:hi], ALU.mult, ALU.add,
                                               [pss[tb + 1].b, rsw.b, cv[a].b], [cv[a].b])
                    rc, kc_, vc = cv
                    with kb.phase() as pq:
                        tp_ = [pq.sb(f"tp{i}", [128, BW]) for i in range(8)]
                        kap, t1, cum, ta, b32, kd32, excl, E = tp_
                        ntot = pq.sb("ntot", [128, CPB])
                        for bi in range(NB):
                            bs = slice(bi * BW, (bi + 1) * BW)
                            cs = slice(bi * CPB, (bi + 1) * CPB)
                            kb.ts(kap[:], kc_[:, bs], vec[:, 0, j:j + 1], ALU.mult, [kc_.b, vec.b], [kap.b])
                            kb.tt(t1[:], kap[:], kap[:], ALU.mult, [kap.b], [t1.b])
                            pst = kb.ps()
                            kb.mm(pst, pst[:, 0:BW], bones[:], t1[:], True, True, [bones.b, t1.b])
                            kb.act(t1[:], pst[:, 0:BW], AF.Ln, [pst.b, eps24.b], [t1.b], bias=eps24[:], scale=1.0)
                            kb.act(t1[:], t1[:], AF.Exp, [t1.b], [t1.b], scale=-0.5)
                            kb.tt(kap[:], kap[:], t1[:], ALU.mult, [kap.b, t1.b], [kap.b])
                            kb.tt(t1[:], rc[:, bs], kc_[:, bs], ALU.mult, [rc.b, kc_.b], [t1.b])
                            kb.ts(t1[:], t1[:], vec[:, 2, j:j + 1], ALU.mult, [t1.b, vec.b], [t1.b])
                            pst = kb.ps()
                            kb.mm(pst, pst[:, 0:BW], bones[:], t1[:], True, True, [bones.b, t1.b])
                            kb.tt(bonus[:, bs], pst[:, 0:BW], vc[:, bs], ALU.mult, [pst.b, vc.b], [bonus.b])
                            for c in range(CPB):
                                cg = bi * CPB + c
                                pst = kb.ps()
                                kb.tr(pst, pst[:, 0:128], vc[:, cg * 128:(cg + 1) * 128], ident[:], [vc.b, ident.b])
                                kb.copy(VT[:, cg, :], pst[:, 0:128], [pst.b], [VT.b], eng="act")
                            for d in range(2):
                                pst = kb.ps()
                                kb.mm(pst, pst[:, 0:BW], ups[:, 0, d, :], wdn[:, d, bs], True, True, [ups.b, wdn.b])
                                kb.act(t1[:], pst[:, 0:BW], AF.Sigmoid, [pst.b, w0.b], [t1.b], bias=w0[:, d, j:j + 1], scale=1.0)
                                kb.ts(t1[:], t1[:], -C0, ALU.mult, [t1.b], [t1.b])
                                kb.op("dve", lambda e: e.tensor_tensor_scan(out=cum[:], data0=scm[:], data1=t1[:], initial=0.0,
                                                                             op0=ALU.mult, op1=ALU.add), [scm.b, t1.b], [cum.b])
                                kb.tt(excl[:], cum[:], t1[:], ALU.subtract, [cum.b, t1.b], [excl.b])
                                pst = kb.ps()
                                kb.mm(pst, pst[:, 0:BW], ups[:, 1, d, :], adn[:, d, bs], True, True, [ups.b, adn.b])
                                kb.act(ta[:], pst[:, 0:BW], AF.Sigmoid, [pst.b, a0.b], [ta.b], bias=a0[:, d, j:j + 1], scale=1.0)
                                kb.tt(b32[:], kap[:], ta[:], ALU.mult, [kap.b, ta.b], [b32.b])
                                kb.ts(ta[:], ta[:], -1.0, ALU.add, [ta.b, vec.b], [ta.b], s2=vec[:, 1, j:j + 1], op1=ALU.mult)
                                kb.stt(kd32[:], ta[:], 1.0, kc_[:, bs], ALU.add, ALU.mult, [ta.b, kc_.b], [kd32.b])
                                totv = c4(cum[:])[:, :, 127]
                                kb.act(etot[:, d, cs], totv, AF.Exp, [cum.b], [etot.b])
                                kb.ts(ntot[:], totv, -1.0, ALU.mult, [cum.b], [ntot.b])

                                def expo(src, sc, use_tot, d=d):
                                    if d == 1:
                                        kb.act(E[:], src[:], AF.Exp, [src.b], [E.b], scale=sc)
                                    else:
                                        for c in range(CPB):
                                            bias = ntot[:, c:c + 1] if use_tot < 0 else cum[:, c * 128 + 127:c * 128 + 128]
                                            kb.act(E[:, c * 128:(c + 1) * 128], src[:, c * 128:(c + 1) * 128], AF.Exp, [src.b, ntot.b, cum.b], [E.b],
                                                   bias=bias, scale=sc)
                                if d == 0:
                                    expo(excl, 1.0, -1)
                                else:
                                    expo(cum, -1.0, 0)
                                for hh in range(2):
                                    pb = 64 * hh
                                    kb.tt(KRz[d][hh][pb:pb + 64, cs, 0, :], c4(kap[pb:pb + 64, :]), c4(E[pb:pb + 64, :]), ALU.mult,
                                          [kap.b, E.b], [KRz[d][hh].b])
                                if d == 0:
                                    expo(cum, 1.0, -1)
                                else:
                                    expo(excl, -1.0, 0)
                                for hh in range(2):
                                    pb = 64 * hh
                                    kb.tt(KRz[d][hh][pb:pb + 64, cs, 1, :], c4(rc[pb:pb + 64, bs]), c4(E[pb:pb + 64, :]), ALU.mult,
                                          [rc.b, E.b], [KRz[d][hh].b])
                                if d == 0:
                                    expo(cum, -1.0, +1)
                                else:
                                    expo(excl, 1.0, 0)
                                kb.tt(b32[:], b32[:], E[:], ALU.mult, [b32.b, E.b], [b32.b])
                                kb.tt(kd32[:], kd32[:], E[:], ALU.mult, [kd32.b, E.b], [kd32.b])
                                kb.copy(BK[d][:, cs, 0, :], c4(b32[:]), [b32.b], [BK[d].b])
                                kb.copy(BK[d][:, cs, 1, :], c4(kd32[:]), [kd32.b], [BK[d].b])
                                for c in range(CPB):
                                    cg = bi * CPB + c
                                    pst = kb.ps()
                                    kb.tr(pst, pst[:, 0:128], b32[:, c * 128:(c + 1) * 128], ident[:], [b32.b, ident.b])
                                    kb.tr(pst, pst[:, 128:256], kd32[:, c * 128:(c + 1) * 128], ident[:], [kd32.b, ident.b])
                                    kb.copy(BKT[d][:, cg, :, :], pst[:, 0:256].rearrange("p (a k) -> p a k", a=2), [pst.b], [BKT[d].b], eng="act")
                    with kb.phase() as pc:
                        SS = pc.sb("SS", [128, nseq, 2, 64])
                        SH = pc.sb("SH", [128, nseq, 2, 64])
                        SHb = pc.sb("SHb", [128, nseq, 2, 64], BF16)
                        RHs = [pc.sb(f"RH{i}", [128, 4, 64], BF16) for i in range(2)]
                        USs = [pc.sb(f"US{i}", [128, 4, 64], BF16) for i in range(2)]
                        if g == "S":
                            for d in range(2):
                                for hh in range(2):
                                    kb.dma("sp", SS[64 * hh:64 * hh + 64, 0, d, :], st0T[l, d, 2 * j + hh], writes=[SS.b])
                        else:
                            kb.memset(SS[:], 0.0, [SS.b])
                        u3 = lambda ap, n_, w_: ap.rearrange("p (u t) -> p u t", t=w_)
                        HS = min(ncs, 4)
                        NBT = (HS * nseq * 4) // 4
                        AB3 = [pc.sb(f"AB3_{b}", [128, 4, 384], BF16) for b in range(NBT)]
                        MTs = [pc.sb(f"MTs{b}", [128, 4, 128], BF16) for b in range(NBT)]
                        XX = [[pc.sb(f"XX{b}{i}", [128, 4, 128], BF16) for i in range(2)] for b in range(NBT)]
                        XXT = [[pc.sb(f"XXT{b}{i}", [128, 4, 128], BF16) for i in range(2)] for b in range(NBT)]

                        def units_of(i):
                            us = []
                            for s_ in range(nseq):
                                for d in range(2):
                                    c = s_ * ncs + (i if d == 0 else ncs - 1 - i)
                                    for hh in range(2):
                                        us.append((hh, d, c, s_))
                            return us
                        for half in range(ncs // HS):
                            steps = list(range(half * HS, (half + 1) * HS))
                            batches = []
                            for i in steps:
                                us = units_of(i)
                                for b0 in range(0, len(us), 4):
                                    batches.append((i, us[b0:b0 + 4]))
                            for b, (i, ub) in enumerate(batches):
                                pA = kb.ps()
                                for u, (hh, d, c, s_) in enumerate(ub):
                                    pst = kb.ps()
                                    kr = KRz[d][hh][:, c, :, :].rearrange("p a t -> p (a t)")
                                    kb.mm(pst, pst[:, 0:256], BK[d][:, c, 0, :], kr, True, True, [BK[d].b, KRz[d][hh].b])
                                    kb.mm(pst, pst[:, 256:512], BK[d][:, c, 1, :], kr, True, True, [BK[d].b, KRz[d][hh].b])
                                    kb.tt(XXT[b][0][:, u, :], pst[:, 0:128], rwm[:, d, 0:128], ALU.mult, [pst.b, rwm.b], [XXT[b][0].b])
                                    kb.tt(AB3[b][:, u, :], pst[:, 128:512], rwm[:, d, 128:512], ALU.mult, [pst.b, rwm.b], [AB3[b].b])
                                    kb.mm(pA, pA[:, u * 128:(u + 1) * 128], KRz[d][hh][:, c, 0, :], BK[d][:, c, 0, :], True, True,
                                          [BK[d].b, KRz[d][hh].b])
                                    kb.tt(XX[b][0][:, u, :], pA[:, u * 128:(u + 1) * 128], rwmA[:, d, :], ALU.mult, [pA.b, rwmA.b], [XX[b][0].b])
                                for u in range(4):
                                    kb.tt(MTs[b][:, u, :], XXT[b][0][:, u, :], identb[:], ALU.add, [XXT[b][0].b, identb.b], [MTs[b].b])
                            cur = 0
                            for r_ in range(6):
                                p1s, p2s = [], []
                                for b in range(len(batches)):
                                    X, XT = XX[b][cur], XXT[b][cur]
                                    p1_ = kb.ps()
                                    for u in range(4):
                                        kb.mm(p1_, p1_[:, u * 128:(u + 1) * 128], XT[:, u, :], X[:, u, :], True, True, [XT.b, X.b])
                                    p1s.append(p1_)
                                    kb.copy(XX[b][1 - cur][:], u3(p1_[:, :], 4, 128), [p1_.b], [XX[b][1 - cur].b], eng=("act" if b % 2 else "dve"))
                                    if r_ < 5:
                                        p2_ = kb.ps()
                                        for u in range(4):
                                            kb.mm(p2_, p2_[:, u * 128:(u + 1) * 128], X[:, u, :], XT[:, u, :], True, True, [XT.b, X.b])
                                        kb.copy(XXT[b][1 - cur][:], u3(p2_[:, :], 4, 128), [p2_.b], [XXT[b][1 - cur].b], eng="act")
                                for b in range(len(batches)):
                                    X2 = XX[b][1 - cur]
                                    p3_ = kb.ps()
                                    for u in range(4):
                                        kb.mm(p3_, p3_[:, u * 128:(u + 1) * 128], X2[:, u, :], MTs[b][:, u, :], True, True, [X2.b, MTs[b].b])
                                    kb.tt(MTs[b][:], MTs[b][:], u3(p3_[:, :], 4, 128), ALU.add, [MTs[b].b, p3_.b], [MTs[b].b])
                                cur = 1 - cur
                            for i in steps:
                                for s_ in range(nseq):
                                    for d in range(2):
                                        c = s_ * ncs + (i if d == 0 else ncs - 1 - i)
                                        kb.ts(SH[:, s_, d, :], SS[:, s_, d, :], etot[:, d, c:c + 1], ALU.mult, [SS.b, etot.b], [SH.b])
                                kb.copy(SHb[:], SH[:], [SH.b], [SHb.b], eng="act")
                                bl = [(b, ub) for b, (ii, ub) in enumerate(batches) if ii == i]
                                pRs, pUs = {}, {}
                                for b, ub in bl:
                                    pR = kb.ps()
                                    for u, (hh, d, c, s_) in enumerate(ub):
                                        pb = 64 * hh
                                        kb.mm(pR, pR[:, u * 64:(u + 1) * 64], KRz[d][hh][:, c, 0, :], SHb[:, s_, d, :], True, False,
                                              [KRz[d][hh].b, SHb.b], inc=False)
                                        kb.mm(pR, pR[:, u * 64:(u + 1) * 64], AB3[b][:, u, 128:256], VT[:, c, pb:pb + 64], False, True,
                                              [AB3[b].b, VT.b], inc=True)
                                    pRs[b] = pR
                                RHb = {}
                                for k_, (b, ub) in enumerate(bl):
                                    RHt = RHs[k_]
                                    kb.act(RHt[:], u3(pRs[b][:, 0:256], 4, 64), AF.Identity, [pRs[b].b], [RHt.b], scale=-1.0)
                                    RHb[b] = RHt
                                for b, ub in bl:
                                    pU = kb.ps()
                                    for u in range(4):
                                        kb.mm(pU, pU[:, u * 64:(u + 1) * 64], MTs[b][:, u, :], RHb[b][:, u, :], True, True, [MTs[b].b, RHb[b].b])
                                    pUs[b] = pU
                                USb = {}
                                for k_, (b, ub) in enumerate(bl):
                                    USt = USs[k_]
                                    kb.copy(USt[:], u3(pUs[b][:, 0:256], 4, 64), [pUs[b].b], [USt.b], eng="act")
                                    USb[b] = USt
                                for b, ub in bl:
                                    pY = kb.ps()
                                    pS = kb.ps()
                                    for u, (hh, d, c, s_) in enumerate(ub):
                                        pb = 64 * hh
                                        q = u // 2
                                        yo = pY[pb:pb + 64, q * 128:(q + 1) * 128]
                                        kb.mm(pY, yo, SHb[:, s_, d, :], KRz[d][hh][:, c, 1, :], True, False, [SHb.b, KRz[d][hh].b], inc=False, tp=(0, pb))
                                        kb.mm(pY, yo, USb[b][:, u, :], AB3[b][:, u, 0:128], False, False, [USb[b].b, AB3[b].b], inc=False, tp=(0, pb))
                                        kb.mm(pY, yo, VT[:, c, pb:pb + 64], AB3[b][:, u, 256:384], False, True, [VT.b, AB3[b].b], inc=True, tp=(0, pb))
                                        so = pS[pb:pb + 64, q * 64:(q + 1) * 64]
                                        kb.mm(pS, so, BKT[d][:, c, 0, pb:pb + 64], USb[b][:, u, :], True, False, [BKT[d].b, USb[b].b], inc=False, tp=(0, pb))
                                        kb.mm(pS, so, BKT[d][:, c, 1, pb:pb + 64], VT[:, c, pb:pb + 64], False, True, [BKT[d].b, VT.b], inc=True, tp=(0, pb))
                                    for q in range(2):
                                        hh, d, c, s_ = ub[2 * q]
                                        kb.tt(yacc[:, c * 128:(c + 1) * 128], yacc[:, c * 128:(c + 1) * 128], pY[:, q * 128:(q + 1) * 128], ALU.add,
                                              [yacc.b, pY.b], [yacc.b])
                                        kb.tt(SS[:, s_, d, :], SH[:, s_, d, :], pS[:, q * 64:(q + 1) * 64], ALU.add, [SH.b, pS.b], [SS.b])
                        if g == "P":
                            for s_ in range(nseq):
                                for d in range(2):
                                    for hh in range(2):
                                        kb.dma("sp", newst[l, s_, d, 2 * j + hh], SS[64 * hh:64 * hh + 64, s_, d, :], reads=[SS.b])
                    with kb.phase() as pe:
                        dev = pe.sb("dev", [128, BW])
                        sq2 = pe.sb("sq2", [128, BW])
                        for bi in range(NB):
                            bs = slice(bi * BW, (bi + 1) * BW)
                            pst = kb.ps()
                            kb.mm(pst, pst[:, 0:BW], bones[:], yacc[:, bs], True, True, [bones.b, yacc.b])
                            kb.stt(dev[:], pst[:, 0:BW], -1.0 / 64, yacc[:, bs], ALU.mult, ALU.add, [pst.b, yacc.b], [dev.b])
                            kb.tt(sq2[:], dev[:], dev[:], ALU.mult, [dev.b], [sq2.b])
                            pst = kb.ps()
                            kb.mm(pst, pst[:, 0:BW], bones[:], sq2[:], True, True, [bones.b, sq2.b])
                            kb.act(sq2[:], pst[:, 0:BW], AF.Ln, [pst.b, epsgn.b], [sq2.b], bias=epsgn[:], scale=1.0 / 64)
                            kb.act(sq2[:], sq2[:], AF.Exp, [sq2.b], [sq2.b], scale=-0.5)
                            kb.tt(dev[:], dev[:], sq2[:], ALU.mult, [dev.b, sq2.b], [dev.b])
                            kb.ts(dev[:], dev[:], vec[:, 3, j:j + 1], ALU.mult, [dev.b, vec.b], [dev.b], s2=vec[:, 4, j:j + 1], op1=ALU.add)
                            kb.tt(dev[:], dev[:], bonus[:, bs], ALU.add, [dev.b, bonus.b], [dev.b])
                            pst = kb.ps()
                            for k2 in range(2):
                                kb.mm(pst, pst[:, 0:BW], ups[:, 2, k2, :], gdn[:, k2, bs], k2 == 0, k2 == 1, [ups.b, gdn.b])
                            kb.tt(rwout[:, j, bs], dev[:], pst[:, 0:BW], ALU.mult, [dev.b, pst.b], [rwout.b])

    def hyena_mixer(phm, g, l, x, mix, T_, L, nseq, gi):
        NT = L // 128
        hc = hyc[g]
        TW = min(L, 512)
        with kb.phase() as ph:
            x0c = ph.sb("x0c", [128, 4, T_], BF16)
            zf = ph.sb("zf", [128, 4, T_], BF16)
            ztm = ph.sb("ztm", [128, T_ // 128, 512], BF16)
            shw = ph.sb("shw", [128, 12, 3])
            skp = ph.sb("skp", [128, 4])
            kb.dma("sp", shw[:], hy_short[l], writes=[shw.b])
            kb.dma("sp", skp[:], hy_skip[l], writes=[skp.b])
            with kb.phase() as p1:
                h = make_h(p1, x, T_, l, gi, 0, "h")
                raw = [p1.sb(f"raw{i}", [128, T_]) for i in range(3)]
                cv = [p1.sb(f"cv{i}", [128, T_]) for i in range(3)]
                zt32 = p1.sb("zt32", [128, T_])
                tiles = []
                for jc in range(4):
                    for a in range(3):
                        tiles.append((C_HY + a * 512 + jc * 128, 128))
                NTBp = max(1, T_ // TB)

                def ev(ci, tb, pst, w, sl):
                    a, jc = ci % 3, ci // 3
                    kb.copy(raw[a][:, sl], pst[:, 0:sl.stop - sl.start], [pst.b], [raw[a].b])
                    if tb != NTBp - 1:
                        return
                    wi = a * 4 + jc
                    for s_ in range(nseq):
                        a0, b0 = s_ * L, (s_ + 1) * L
                        kb.ts(cv[a][:, a0:b0], raw[a][:, a0:b0], shw[:, wi, 1:2], ALU.mult, [raw[a].b, shw.b], [cv[a].b])
                        kb.stt(cv[a][:, a0 + 1:b0], raw[a][:, a0:b0 - 1], shw[:, wi, 0:1], cv[a][:, a0 + 1:b0], ALU.mult, ALU.add,
                               [raw[a].b, shw.b, cv[a].b], [cv[a].b])
                        kb.stt(cv[a][:, a0:b0 - 1], raw[a][:, a0 + 1:b0], shw[:, wi, 2:3], cv[a][:, a0:b0 - 1], ALU.mult, ALU.add,
                               [raw[a].b, shw.b, cv[a].b], [cv[a].b])
                    if a != 2:
                        return
                    kb.copy(x0c[:, jc, :], cv[0][:], [cv[0].b], [x0c.b])
                    kb.tt(zt32[:], cv[1][:], cv[2][:], ALU.mult, [cv[1].b, cv[2].b], [zt32.b])
                    kb.copy(zf[:, jc, :], zt32[:], [zt32.b], [zf.b])
                    for tt_ in range(T_ // 128):
                        pt_ = kb.ps()
                        kb.tr(pt_, pt_[:, 0:128], zt32[:, tt_ * 128:(tt_ + 1) * 128], ident[:], [zt32.b, ident.b])
                        kb.copy(ztm[:, tt_, jc * 128:(jc + 1) * 128], pt_[:, 0:128], [pt_.b], [ztm.b])
                proj(p1, w_in[l], tiles, lambda kc, sl: (h[:, kc, sl], [h.b]), KC, T_, ev, "why")
            HP = ph.sb("HP", [128, NT, 512], BF16)
            HQ = ph.sb("HQ", [128, NT, 512], BF16)
            YP = ph.sb("YP", [128, nseq * NT, 512], BF16)
            YQ = ph.sb("YQ", [128, nseq * NT, 512], BF16)
            with kb.phase() as pa0:
              filt = pa0.sb("filt", [128, NT, 1024], BF16)
              with kb.phase() as pa:
                ft = pa.sb("ft", [128, L])
                f1 = pa.sb("f1", [128, 64])
                f2 = pa.sb("f2", [128, 64])
                f3 = pa.sb("f3", [128, 1024])
                b12 = pa.sb("b12", [64, 2])
                h1 = pa.sb("h1", [128, L])
                h2 = pa.sb("h2f", [128, L])
                vb = pa.sb("vb", [64, TW])
                nn = pa.sb("nn", [64, TW])
                dec = pa.sb("dec", [128, 1024])
                ee = pa.sb("ee", [128, 1024])
                nt01 = pa.sb("nt01", [128, NT])
                for t_ in (ft, f1, f2, f3, h1, h2):
                    kb.memset(t_[:], 0.0, [t_.b])
                kb.dma("sp", ft[0:33, :], hc["featT"], writes=[ft.b])
                kb.dma("sp", f1[0:33, :], hy_f1[l], writes=[f1.b])
                kb.dma("sp", f2[0:64, :], hy_f2[l], writes=[f2.b])
                kb.dma("sp", f3[0:64, :], hy_f3[l], writes=[f3.b])
                kb.dma("sp", b12[:, 0:1], hy_b1[l], writes=[b12.b])
                kb.dma("sp", b12[:, 1:2], hy_b2[l], writes=[b12.b])
                kb.dma("sp", dec[:], hy_decay[l], writes=[dec.b])
                kb.dma("sp", nt01[:], hc["t01"], writes=[nt01.b])
                kb.stt(ee[:], dec[:], -1.0, dec[:], ALU.mult, ALU.max, [dec.b], [ee.b])
                kb.copy(dec[:], ee[:], [ee.b], [dec.b])

                def sin_layer(wt, src, dst, bcol):
                    for b_ in range(L // TW):
                        sl = slice(b_ * TW, (b_ + 1) * TW)
                        pst = kb.ps()
                        kb.mm(pst, pst[0:64, 0:TW], wt[:, 0:64], src[:, sl], True, True, [wt.b, src.b])
                        kb.ts(vb[:], pst[0:64, 0:TW], b12[:, bcol:bcol + 1], ALU.add, [pst.b, b12.b], [vb.b])
                        kb.ts(nn[:], vb[:], 1.0 / (2 * math.pi), ALU.mult, [vb.b], [nn.b], s2=MAGIC, op1=ALU.add)
                        kb.ts(nn[:], nn[:], -MAGIC, ALU.add, [nn.b], [nn.b])
                        kb.stt(vb[:], nn[:], -2 * math.pi, vb[:], ALU.mult, ALU.add, [nn.b, vb.b], [vb.b])
                        kb.act(dst[0:64, sl], vb[:], AF.Sin, [vb.b], [dst.b])
                sin_layer(f1, ft, h1, 0)
                sin_layer(f2, h1, h2, 1)
                for j in range(NT):
                    kb.act(ee[:], dec[:], AF.Exp, [dec.b, nt01.b], [ee.b], scale=nt01[:, j:j + 1])
                    for hf in range(2):
                        pst = kb.ps()
                        kb.mm(pst, pst[:, :], h2[:, j * 128:(j + 1) * 128], f3[:, hf * 512:(hf + 1) * 512], True, True, [h2.b, f3.b])
                        kb.stt(filt[:, j, hf * 512:(hf + 1) * 512], ee[:, hf * 512:(hf + 1) * 512], 0.05, pst[:, :], ALU.add, ALU.mult,
                               [ee.b, pst.b], [filt.b])
              with kb.phase() as pa:
                ring = WRing(kb, pa, "hm", 2, [128, 4, NT, 128])
                mats = [hc["Ch"], hc["Cb"], hc["Sh"], hc["Sb"]]
                mv = [m_.rearrange("(j p) f -> p j f", p=128) for m_ in mats]
                for i in range(NT):
                    slot = ring.slots[i % 2]
                    for a in range(4):
                        kb.dma("sp", slot[:, a, :, :], mv[a][:, :, i * 128:(i + 1) * 128], writes=[slot.b])
                    for q_, dstH in ((0, HP), (1, HQ)):
                        pst = kb.ps()
                        for j in range(NT):
                            kb.mm(pst, pst[:, :], slot[:, 2 * q_, j, :], filt[:, j, 0:512], j == 0, False, [slot.b, filt.b])
                            kb.mm(pst, pst[:, :], slot[:, 2 * q_ + 1, j, :], filt[:, j, 512:1024], False, j == NT - 1, [slot.b, filt.b])
                        kb.copy(dstH[:, i, :], pst[:, :], [pst.b], [dstH.b])
            with kb.phase() as pf:
                ring = WRing(kb, pf, "fm", 2, [128, 2, NT, 128])
                mv = [m_.rearrange("(j p) f -> p j f", p=128) for m_ in (hc["C"], hc["S"])]
                tm = pf.sb("tm", [128, 4, 512])
                for i in range(NT):
                    slot = ring.slots[i % 2]
                    for a in range(2):
                        kb.dma("sp", slot[:, a, :, :], mv[a][:, :, i * 128:(i + 1) * 128], writes=[slot.b])
                    for s_ in range(nseq):
                        pp = kb.ps()
                        pq = kb.ps()
                        for a, pst in ((0, pp), (1, pq)):
                            for j in range(NT):
                                kb.mm(pst, pst[:, :], slot[:, a, j, :], ztm[:, s_ * NT + j, :], j == 0, j == NT - 1, [slot.b, ztm.b])
                        ii = s_ * NT + i
                        rd = [pp.b, pq.b, HP.b, HQ.b]
                        kb.tt(tm[:, 0, :], pp[:, :], HP[:, i, :], ALU.mult, rd, [tm.b])
                        kb.tt(tm[:, 1, :], pq[:, :], HQ[:, i, :], ALU.mult, rd, [tm.b])
                        kb.tt(tm[:, 2, :], pp[:, :], HQ[:, i, :], ALU.mult, rd, [tm.b])
                        kb.tt(tm[:, 3, :], pq[:, :], HP[:, i, :], ALU.mult, rd, [tm.b])
                        kb.tt(YP[:, ii, :], tm[:, 0, :], tm[:, 1, :], ALU.subtract, [tm.b], [YP.b])
                        kb.tt(YQ[:, ii, :], tm[:, 2, :], tm[:, 3, :], ALU.add, [tm.b], [YQ.b])
                        if i == 0:
                            kb.copy(YP[0:1, ii, :], tm[0:1, 0, :], [tm.b], [YP.b])
                            kb.copy(YQ[0:1, ii, :], tm[0:1, 1, :], [tm.b], [YQ.b])
            with kb.phase() as pi_:
                Cm = pi_.sb("Cm", [128, NT, L], BF16)
                Sm = pi_.sb("Sm", [128, NT, L], BF16)
                kb.dma("sp", Cm[:], hc["C"].rearrange("(i p) t -> p i t", p=128), writes=[Cm.b])
                kb.dma("sp", Sm[:], hc["Si"].rearrange("(i p) t -> p i t", p=128), writes=[Sm.b])
                t32 = pi_.sb("t32", [128, 2, TW])
                tbf = [Buf("a"), Buf("b")]
                n_ = 0
                for s_ in range(nseq):
                    for jc in range(4):
                        for tbk in range(L // TW):
                            pst = kb.ps()
                            for i in range(NT):
                                ii = s_ * NT + i
                                kb.mm(pst, pst[:, 0:TW], YP[:, ii, jc * 128:(jc + 1) * 128], Cm[:, i, tbk * TW:(tbk + 1) * TW], i == 0, False,
                                      [YP.b, Cm.b])
                                kb.mm(pst, pst[:, 0:TW], YQ[:, ii, jc * 128:(jc + 1) * 128], Sm[:, i, tbk * TW:(tbk + 1) * TW], False, i == NT - 1,
                                      [YQ.b, Sm.b])
                            tsl = slice(s_ * L + tbk * TW, s_ * L + (tbk + 1) * TW)
                            kb.stt(t32[:, n_ % 2, :], zf[:, jc, tsl], skp[:, jc:jc + 1], pst[:, 0:TW], ALU.mult, ALU.add,
                                   [zf.b, skp.b, pst.b], [tbf[n_ % 2]])
                            kb.tt(mix[:, 6 + jc, tsl], t32[:, n_ % 2, :], x0c[:, jc, tsl], ALU.mult, [tbf[n_ % 2], x0c.b], [mix.b])
                            n_ += 1

    def att_mixer(phm, g, l, x, mix, T_, L, nseq, gi):
        with kb.phase() as ph:
            qT = ph.sb("qT", [128, 12, T_], BF16)
            kT = ph.sb("kT", [128, 4, T_], BF16)
            kb.memset(qT[:], 0.0, [qT.b])
            kb.memset(kT[:], 0.0, [kT.b])
            vtm = ph.sb("vtm", [128, T_ // 128, 256], BF16)
            esink = T(esink_all.t, "esink")
            esink.b = esink_all.b
            esink_l = l
            if cfg.get("a_stop", 9) <= 1:
                return
            with kb.phase() as p1:
                h = make_h(p1, x, T_, l, gi, 0, "a")
                if cfg.get("a_stop", 9) <= 2:
                    return
                xk = lambda kc, sl: (h[:, kc, sl], [h.b])
                tiles = [(C_Q + 64 * i, 64) for i in range(12)] + [(C_K + 64 * i, 64) for i in range(4)]
                if g == "P":
                    kst = p1.sb("kst", [64, 4, T_])

                    def evqk(ci, tb, pst, w, sl):
                        ce = "dve"
                        if ci < 12:
                            kb.copy(qT[0:64, ci, sl], pst[0:64, :], [pst.b], [qT.b], eng=ce)
                        else:
                            kb.copy(kT[0:64, ci - 12, sl], pst[0:64, :], [pst.b], [kT.b], eng=ce)
                            kb.copy(kst[:, ci - 12, sl], pst[0:64, :], [pst.b], [kst.b])
                    proj(p1, w_in[l], tiles, xk, KC, T_, evqk, "wqk")
                    if not cfg.get("no_newk", 0):
                        kb.dma("sp", newk[l].rearrange("h d t -> d h t"), kst[:], reads=[kst.b])
                else:
                    rc = p1.sb("rc", [64, T_])
                    rs = p1.sb("rs", [64, T_])
                    kb.dma("sp", rc[:], rope_c, writes=[rc.b])
                    kb.dma("sp", rs[:], rope_s, writes=[rs.b])
                    rt = p1.sb("rt", [64, 2, TB])
                    tiles2 = [((C_Q + 64 * i, 64 * i), 64) for i in range(12)] + [((C_K + 64 * i, 768 + 64 * i), 64) for i in range(4)]

                    def evqk(ci, tb, psts, w, sl):
                        dst = qT[0:64, ci, sl] if ci < 12 else kT[0:64, ci - 12, sl]
                        db = qT.b if ci < 12 else kT.b
                        kb.tt(rt[:, 0, :], psts[0][0:64, :], rc[:, sl], ALU.mult, [psts[0].b, rc.b], [rt.b])
                        kb.tt(rt[:, 1, :], psts[1][0:64, :], rs[:, sl], ALU.mult, [psts[1].b, rs.b], [rt.b])
                        kb.tt(dst, rt[:, 0, :], rt[:, 1, :], ALU.add, [rt.b], [db])
                    proj(p1, [w_in[l], w_qkp[l]], tiles2, xk, KC, T_, evqk, "wqk", nslots=2)
                if cfg.get("a_stop", 9) <= 3:
                    return
                wv_ = p1.sb("wv", [128, KC, 256], BF16)
                kb.dma("pool", wv_[:], w_in[l].rearrange("(kc p) n -> p kc n", p=128)[:, :, C_V:C_V + 256], writes=[wv_.b])
                if g == "P":
                    vst = p1.sb("vst", [128, T_ // 128, 256])
                for tt_ in range(T_ // 128):
                    pst = kb.ps()
                    for kc in range(KC):
                        kb.mm(pst, pst[:, 0:256], h[:, kc, tt_ * 128:(tt_ + 1) * 128], wv_[:, kc, :], kc == 0, kc == KC - 1, [h.b, wv_.b])
                    kb.copy(vtm[:, tt_, :], pst[:, 0:256], [pst.b], [vtm.b], eng="act")
                    if g == "P":
                        kb.copy(vst[:, tt_, :], pst[:, 0:256], [pst.b, vtm.b], [vst.b])
                if g == "P":
                    kb.dma("sp", newv[l].rearrange("(a p) c -> p a c", p=128), vst[:], reads=[vst.b])
            if cfg.get("att_noscore", 0):
                return
            with kb.phase() as p2:
                NPT = 4
                pT = [p2.sb(f"pT{i}", [128, 512], BF16) for i in range(NPT)]
                pti = [0]
                rden = p2.sb("rden", [128, 512])
                if g == "S":
                    kcx = p2.sb("kcx", [128, 4, 512], BF16)
                    kb.memset(kcx[:], 0.0, [kcx.b])
                    vcx = p2.sb("vcx", [128, 4, 256], BF16)
                    wm = p2.sb("wm", [128, 6, 512], BF16)
                    kb.dma("pool", kcx[0:64], kctxT[l].rearrange("h d t -> d h t"), writes=[kcx.b])
                    kb.dma("pool", vcx[:], vctx[l].rearrange("(a p) c -> p a c", p=128), writes=[vcx.b])
                    kb.dma("sp", wm[:], wmask_d.rearrange("r p q -> p r q"), writes=[wm.b])
                QW = 512 if g == "S" else 256
                for qg in range(T_ // QW):
                    q0 = qg * QW
                    for hq in range(12):
                        kvh = hq // 3
                        pb = 64 * (hq % 2)
                        kts = []
                        if g == "S":
                            for j in range(4):
                                kts.append((kcx[:, kvh, j * 128:(j + 1) * 128], [kcx.b], vcx[:, j, kvh * 64:(kvh + 1) * 64], [vcx.b], None))
                            for jt in range(max(0, 4 * qg - 1), min(8, 4 * qg + 5)):
                                kts.append((kT[:, kvh, jt * 128:(jt + 1) * 128], [kT.b], vtm[:, jt, kvh * 64:(kvh + 1) * 64], [vtm.b],
                                            wm[:, jt - 4 * qg + 1, :]))
                        else:
                            for j in range(2):
                                jt = qg * 2 + j
                                kts.append((kT[:, kvh, jt * 128:(jt + 1) * 128], [kT.b], vtm[:, jt, kvh * 64:(kvh + 1) * 64], [vtm.b], None))
                        po = kb.psum[2 * (hq % 2)]
                        pd = kb.psum[2 * (hq % 2) + 1]
                        nk_ = len(kts)
                        for i, (kap, kbufs, vap, vbufs, mask) in enumerate(kts):
                            pss = kb.psum[4 + pti[0] % 4]
                            kb.mm(pss, pss[:, 0:QW], kap, qT[:, hq, q0:q0 + QW], True, True, kbufs + [qT.b])
                            pt = pT[pti[0] % NPT]
                            pti[0] += 1
                            kb.act(pt[:, 0:QW], pss[:, 0:QW], AF.Exp, [pss.b], [pt.b], scale=0.125)
                            if mask is not None:
                                kb.tt(pt[:, 0:QW], pt[:, 0:QW], mask, ALU.mult, [pt.b, wm.b], [pt.b])
                            kb.mm(po, po[pb:pb + 64, 0:QW], vap, pt[:, 0:QW], i == 0, i == nk_ - 1, vbufs + [pt.b], tp=(0, pb))
                            kb.mm(pd, pd[pb:pb + 64, 0:QW], ones_bf[:, 0:64], pt[:, 0:QW], i == 0, i == nk_ - 1, [ones_bf.b, pt.b], tp=(0, pb))
                        kb.ts(rden[pb:pb + 64, 0:QW], pd[pb:pb + 64, 0:QW], esink_all[pb:pb + 64, l, hq:hq + 1], ALU.add, [pd.b, esink.b], [rden.b])
                        kb.op("dve", lambda e: e.reciprocal(out=rden[pb:pb + 64, 0:QW], in_=rden[pb:pb + 64, 0:QW]), [rden.b], [rden.b])
                        kb.tt(mix[pb:pb + 64, hq // 2, q0:q0 + QW], po[pb:pb + 64, 0:QW], rden[pb:pb + 64, 0:QW], ALU.mult,
                              [po.b, rden.b], [mix.b])

    for g in cfg.get("groups", ("S", "P")):
        run_group(g)
    kb.barrier()
    return kb


def _fm(v, p=128):
    v = np.asarray(v)
    n = v.shape[-1] // p
    return np.ascontiguousarray(np.moveaxis(v.reshape(v.shape[:-1] + (n, p)), -1, 0))


def host_consts():
    c = {}
    c["c_ident"] = np.eye(128, dtype=np.float32)
    b = np.zeros((128, 128), np.float32)
    b[:64, :64] = 1
    b[64:, 64:] = 1
    c["c_bones"] = b
    t = np.arange(1024)
    rows, cols = (t // 64).astype(np.float64), (t % 64).astype(np.float64)
    freqs = 10000.0 ** (-np.arange(0, 32, 2, dtype=np.float64) / 32)
    rc = np.zeros((64, 1024)); rs = np.zeros((64, 1024))
    for d in range(64):
        pos = rows if d < 32 else cols
        ang = pos * freqs[d % 16]
        rc[d] = np.cos(ang)
        rs[d] = np.sin(ang) * (-1.0 if (d % 32) < 16 else 1.0)
    c["c_rope_cos"] = rc.astype(np.float32)
    c["c_rope_sin"] = rs.astype(np.float32)
    wm = np.zeros((6, 128, 512), np.float32)
    kk_, qq_ = np.meshgrid(np.arange(128), np.arange(512), indexing="ij")
    for r in range(6):
        rel = r - 1
        wm[r] = (np.abs(qq_ - rel * 128 - kk_) <= 128)
    c["c_wmask"] = wm.astype(ml_dtypes.bfloat16)
    sm = np.ones((128, 1024), np.float32)
    sm[:, ::128] = 0.0
    c["c_scanmask"] = sm
    ss_, tt_ = np.meshgrid(np.arange(128), np.arange(128), indexing="ij")
    rwm = np.zeros((2, 128, 512), np.float32)
    rwa = np.zeros((2, 128, 128), np.float32)
    for d in range(2):
        strict = (ss_ < tt_) if d == 0 else (ss_ > tt_)
        incl = (ss_ <= tt_) if d == 0 else (ss_ >= tt_)
        rwm[d] = np.concatenate([-1.0 * strict, 1.0 * incl, 1.0 * strict, 1.0 * incl], axis=1)
        rwa[d] = -1.0 * strict.T
    c["c_rwmask"] = rwm.astype(ml_dtypes.bfloat16)
    c["c_rwmaskA"] = rwa.astype(ml_dtypes.bfloat16)
    c.update(hy_consts(1024, "S"))
    c.update(hy_consts(256, "P"))
    return c


def hy_consts(L, g):
    c = {}
    bf = ml_dtypes.bfloat16
    t = np.arange(L, dtype=np.float64)
    t01 = t / max(L - 1, 1)
    bands = np.linspace(1e-4, 15, 16)
    ang = (2.0 * math.pi / L) * t[:, None] * bands[None, :]
    feat = np.concatenate([t01[:, None], np.cos(ang), -np.sin(ang)], axis=-1)
    c[f"c_featT_{g}"] = np.ascontiguousarray(feat.T).astype(np.float32)
    c[f"c_t01_{g}"] = np.ascontiguousarray((-t01).reshape(L // 128, 128).T).astype(np.float32)
    f = np.arange(L, dtype=np.float64)
    A = math.pi * np.outer(t, f) / L
    sgn = np.where(np.arange(L) % 2 == 0, 1.0, -1.0)
    C = np.cos(A)
    Sf = np.sin(A)
    Sf[:, 0] = sgn
    sc = np.full(L, 1.0 / L)
    sc[0] = 0.5 / L
    Ch = C * sc[None, :]
    Sh = Sf * sc[None, :]
    Cb = Ch.copy()
    Cb[0, :] = 0
    Sb = -np.sin(A) * sc[None, :]
    Sb[:, 0] = sgn * sc[0]
    Sb[0, :] = 0
    c[f"c_dftC_{g}"] = C.astype(bf)
    c[f"c_dftS_{g}"] = Sf.astype(bf)
    c[f"c_dftSi_{g}"] = np.ascontiguousarray(Sf.T).astype(bf)
    c[f"c_dftCh_{g}"] = Ch.astype(bf)
    c[f"c_dftSh_{g}"] = Sh.astype(bf)
    c[f"c_dftCb_{g}"] = Cb.astype(bf)
    c[f"c_dftSb_{g}"] = Sb.astype(bf)
    return c


def _perm64():
    p = np.zeros(64, np.int64)
    for d in range(64):
        p[d] = d + 16 if (d % 32) < 16 else d - 16
    return p


_CACHE = {}


def kernel(**inp):
    cfg = inp.pop("_cfg", {})
    key = repr(sorted(cfg.items()))
    if key not in _CACHE:
        _CACHE[key] = build(cfg)
    kb = _CACHE[key]
    f32 = np.float32
    A = {k: np.asarray(v) for k, v in inp.items()}
    shared = dict(host_consts())
    for k in ("ada_w", "w_in", "w_out", "mlp_w1", "mlp_w2", "hy_f1", "hy_f2", "hy_f3", "rw_w_up", "rw_a_up", "rw_g_up"):
        shared[k] = np.ascontiguousarray(A[k], dtype=f32)
    pm = _perm64()
    qk_idx = np.concatenate([h * 64 + pm for h in range(12)] + [768 + h * 64 + pm for h in range(4)])
    qk_idx = np.concatenate([qk_idx, np.arange(64)])
    shared["w_qkp"] = np.ascontiguousarray(A["w_in"][:, :, qk_idx], dtype=f32)
    shared["sink_bc"] = np.ascontiguousarray(np.broadcast_to(A["attn_sink"][:, None, :], (DEPTH, 128, 12)), dtype=f32)
    shared["rw_short_fm"] = np.ascontiguousarray(np.stack([_fm(A["rw_short_w"][l]).transpose(0, 2, 1) for l in range(DEPTH)]))
    shared["rw_w0_fm"] = np.ascontiguousarray(np.stack([_fm(A["rw_w0"][l]) for l in range(DEPTH)]))
    shared["rw_a0_fm"] = np.ascontiguousarray(np.stack([_fm(A["rw_a0"][l]) for l in range(DEPTH)]))
    shared["rw_vec_fm"] = np.ascontiguousarray(np.stack([_fm(np.stack([A["rw_k_k"][l], A["rw_k_a"][l], A["rw_r_k"][l].reshape(768),
                                                                        A["rw_gn_w"][l], A["rw_gn_b"][l]])) for l in range(DEPTH)]))
    shared["hy_short_fm"] = np.ascontiguousarray(np.stack([_fm(A["hy_short_w"][l]).transpose(0, 2, 1) for l in range(DEPTH)]))
    shared["hy_b1"] = np.ascontiguousarray(A["hy_b1"].reshape(DEPTH, 64, 1))
    shared["hy_b2"] = np.ascontiguousarray(A["hy_b2"].reshape(DEPTH, 64, 1))
    shared["hy_decay_bc"] = np.ascontiguousarray(np.broadcast_to(A["hy_decay"][:, None, :], (DEPTH, 128, 1024)), dtype=f32)
    shared["hy_skip_fm"] = np.ascontiguousarray(np.stack([_fm(A["hy_skip"][l, 0]) for l in range(DEPTH)]))
    shared["ada_b_fm"] = np.ascontiguousarray(np.stack([_fm(A["ada_b"][l]) for l in range(DEPTH)]))
    shared["norm_w_fm"] = np.ascontiguousarray(np.stack([_fm(A["norm_w"][l]) for l in range(DEPTH)]))
    in_maps = []
    for c in range(8):
        m = dict(shared)
        m["xT_s"] = np.ascontiguousarray(A["x_sample"][c].T)
        m["xT_p"] = np.ascontiguousarray(A["x_prompt"][2 * c:2 * c + 2].reshape(512, D).T)
        cond = np.stack([A["c"][c], A["c_ctx"]], axis=-1)
        m["kctxT"] = np.ascontiguousarray(A["cache_k"][c].transpose(0, 2, 3, 1))
        m["vctx"] = np.ascontiguousarray(A["cache_v"][c].reshape(DEPTH, 512, 256))
        m["st0T"] = np.ascontiguousarray(A["state_rwkv"][c].transpose(0, 1, 2, 4, 3))
        m["condT"] = np.ascontiguousarray(cond.reshape(KC, 128, 2).transpose(1, 0, 2))
        in_maps.append(m)
    names = set(kb.dram_in.keys())
    in_maps = [{k: v for k, v in m.items() if k in names} for m in in_maps]
    missing = names - set(in_maps[0].keys())
    for k in missing:
        ap = kb.dram_in[k]
        dt = ml_dtypes.bfloat16 if ap.dtype == BF16 else f32
        for m in in_maps:
            m[k] = np.zeros(ap.shape, dt)
    ncores = cfg.get("ncores", 8)
    if cfg.get("trace", 0):
        res = run_bass_kernel_spmd(kb.nc, in_maps[:ncores], core_ids=list(range(ncores)), trace=True)
        print("EXEC_NS", res.exec_time_ns, "n_ins", kb.n_ins, "cnt", kb.cnt)
    else:
        res = run_bass_kernel_spmd(kb.nc, in_maps[:ncores], core_ids=list(range(ncores)))
    R = list(res.results)
    while len(R) < 8:
        R.append(R[0])
    y_s = np.stack([R[c]["yT_s"].T for c in range(8)])
    y_p = np.concatenate([R[c]["yT_p"].T.reshape(2, 256, D) for c in range(8)])
    nk = np.zeros((16, DEPTH, 256, 4, 64), f32)
    nv = np.zeros((16, DEPTH, 256, 4, 64), f32)
    ns = np.zeros((16, DEPTH, 2, 12, 64, 64), f32)
    for c in range(8):
        kT = R[c]["newkT"]
        v_ = R[c]["newv"]
        sT = R[c]["newstT"]
        for s in range(2):
            nk[2 * c + s] = kT[:, :, :, s * 256:(s + 1) * 256].transpose(0, 3, 1, 2)
            nv[2 * c + s] = v_[:, s * 256:(s + 1) * 256, :].reshape(DEPTH, 256, 4, 64)
            ns[2 * c + s] = sT[:, s].transpose(0, 1, 2, 4, 3)
    return (y_p.astype(f32), y_s.astype(f32), nk, nv, ns)
```
